# Optimizing a Trainium2 kernel written in Bass

```python
import jax, jax.numpy as jnp
from jax import lax
import numpy as np

D_MODEL = 1024
BATCH = 16
SEQ = 4096
DEPTH = 1
DEC_BATCH = 32
DEC_SEQ = 16
PAST_LEN = 4096

CHUNK = 64
POOL_WIDTH = D_MODEL // 1
POOL_GROUPS = 4
POOL_GROUP_DIM = POOL_WIDTH // POOL_GROUPS
POOL_WINDOWS = (2, 4, 8, 16)
POOL_HIST = max(POOL_WINDOWS) - 1
SSD_WIDTH = D_MODEL
SSD_HEAD_DIM = 64
SSD_HEADS = SSD_WIDTH // SSD_HEAD_DIM
SSD_GROUPS = 2
SSD_HEADS_PER_GROUP = SSD_HEADS // SSD_GROUPS
SSD_STATE = 128
SSD_CONV = 4
SSD_CHUNK = CHUNK
CONV_DIM = SSD_WIDTH + 2 * SSD_GROUPS * SSD_STATE
MIX_WIDTH = POOL_WIDTH + SSD_WIDTH
D_IN_PROJ = POOL_WIDTH + SSD_WIDTH + CONV_DIM + SSD_HEADS
D_FF = 4 * D_MODEL
EPS = 1e-6

kernel_name = "hymba_pool_ssd_streaming_step"


def rms_norm(x, w):
    xf = x.astype(jnp.float32)
    y = xf * lax.rsqrt(jnp.mean(xf * xf, axis=-1, keepdims=True) + EPS)
    return y.astype(x.dtype) * w


def pool_mixer(u, hist, pos0, pool_w, pool_b, pool_scale):
    b, L, _ = u.shape
    full = jnp.concatenate([hist, u], axis=1)
    cs = jnp.cumsum(full.astype(jnp.float32), axis=1)
    cs = jnp.concatenate([jnp.zeros((b, 1, POOL_WIDTH), jnp.float32), cs], axis=1)
    end = cs[:, POOL_HIST + 1:]
    pos = (jnp.arange(L) + pos0).astype(jnp.float32)
    means = []
    for g, w in enumerate(POOL_WINDOWS):
        lo, hi = g * POOL_GROUP_DIM, (g + 1) * POOL_GROUP_DIM
        start = cs[:, POOL_HIST + 1 - w:POOL_HIST + 1 - w + L, lo:hi]
        cnt = jnp.minimum(jnp.float32(w), pos + 1.0)[None, :, None]
        means.append((end[:, :, lo:hi] - start) / cnt)
    pooled = jnp.stack(means, axis=2) - u.reshape(b, L, POOL_GROUPS, POOL_GROUP_DIM).astype(jnp.float32)
    out = jnp.einsum('blgc,gcd->blgd', pooled.astype(u.dtype), pool_w) + pool_b
    return out.reshape(b, L, POOL_WIDTH) * pool_scale, full[:, -POOL_HIST:]


def causal_conv(xbc, hist, conv_w, conv_b):
    L = xbc.shape[1]
    full = jnp.concatenate([hist, xbc], axis=1)
    out = conv_b
    for k in range(SSD_CONV):
        out = out + conv_w[k] * full[:, k:k + L]
    return jax.nn.silu(out), full[:, -(SSD_CONV - 1):]


def ssd_scan(xh, dt, A, Bm, Cm, h0):
    b, L, H, P = xh.shape
    G, Hg, N = SSD_GROUPS, SSD_HEADS_PER_GROUP, SSD_STATE
    Q = min(SSD_CHUNK, L)
    nc = L // Q
    x = xh.astype(jnp.float32).reshape(b, nc, Q, G, Hg, P)
    dtc = dt.reshape(b, nc, Q, G, Hg)
    Bc = Bm.astype(jnp.float32).reshape(b, nc, Q, G, N)
    Cc = Cm.astype(jnp.float32).reshape(b, nc, Q, G, N)
    a = dtc * A.reshape(G, Hg)
    acs = jnp.moveaxis(jnp.cumsum(a, axis=2), 2, -1)
    xdt = x * dtc[..., None]
    mask = jnp.tril(jnp.ones((Q, Q), dtype=bool))
    seg = acs[..., :, None] - acs[..., None, :]
    Lmat = jnp.where(mask, jnp.exp(jnp.where(mask, seg, 0.0)), 0.0)
    CB = jnp.einsum('bcign,bcjgn->bcgij', Cc, Bc)
    M = CB[:, :, :, None] * Lmat
    y_diag = jnp.einsum('bcghij,bcjghp->bcighp', M, xdt)
    decay_states = jnp.exp(acs[..., -1:] - acs)
    states = jnp.einsum('bcjgn,bcghj,bcjghp->bcghpn', Bc, decay_states, xdt)
    chunk_decay = jnp.exp(acs[..., -1])

    def step(h, inp):
        dec, st = inp
        return h * dec[..., None, None] + st, h

    h_init = h0.astype(jnp.float32).reshape(b, G, Hg, P, N)
    h_final, h_enter = lax.scan(step, h_init, (jnp.moveaxis(chunk_decay, 1, 0), jnp.moveaxis(states, 1, 0)))
    h_enter = jnp.moveaxis(h_enter, 0, 1)
    y_off = jnp.einsum('bcign,bcghpn,bcghi->bcighp', Cc, h_enter, jnp.exp(acs))
    y = (y_diag + y_off).reshape(b, L, H, P)
    return y, h_final.reshape(b, H, P, N)


def ssd_mixer(z, xbc, dt_raw, conv_hist, h0, conv_w, conv_b, dt_bias, a_log, d_skip, ssd_norm_w):
    b, L, _ = z.shape
    xbc, new_conv = causal_conv(xbc, conv_hist, conv_w, conv_b)
    xs, Bm, Cm = jnp.split(xbc, [SSD_WIDTH, SSD_WIDTH + SSD_GROUPS * SSD_STATE], axis=-1)
    xh = xs.reshape(b, L, SSD_HEADS, SSD_HEAD_DIM)
    Bm = Bm.reshape(b, L, SSD_GROUPS, SSD_STATE)
    Cm = Cm.reshape(b, L, SSD_GROUPS, SSD_STATE)
    dt = jax.nn.softplus(dt_raw.astype(jnp.float32) + dt_bias.astype(jnp.float32))
    A = -jnp.exp(a_log.astype(jnp.float32))
    y, h_new = ssd_scan(xh, dt, A, Bm, Cm, h0)
    y = y + d_skip.astype(jnp.float32)[:, None] * xh.astype(jnp.float32)
    y = y.reshape(b, L, SSD_WIDTH) * jax.nn.silu(z.astype(jnp.float32))
    yg = y.reshape(b, L, SSD_GROUPS, SSD_WIDTH // SSD_GROUPS)
    yg = yg * lax.rsqrt(jnp.mean(yg * yg, axis=-1, keepdims=True) + EPS)
    y = yg.reshape(b, L, SSD_WIDTH).astype(z.dtype) * ssd_norm_w
    return y, new_conv, h_new.astype(h0.dtype)


def trunk_layer(x, c, pool_hist, conv_hist, h0, pos0, norm_mix_w, norm_ffn_w, w_ada, b_ada, w_in,
                pool_w, pool_b, pool_scale, conv_w, conv_b, dt_bias, a_log, d_skip, ssd_norm_w,
                w_out, w_ff1, b_ff1, w_ff2, b_ff2):
    mod = (jax.nn.silu(c) @ w_ada + b_ada)[:, None, :]
    sh1, sc1, g1, sh2, sc2, g2 = jnp.split(mod, 6, axis=-1)
    h = rms_norm(x, norm_mix_w) * (1.0 + sc1) + sh1
    proj = h @ w_in
    u, z, xbc, dt_raw = jnp.split(proj, [POOL_WIDTH, POOL_WIDTH + SSD_WIDTH,
                                         POOL_WIDTH + SSD_WIDTH + CONV_DIM], axis=-1)
    ya, new_pool = pool_mixer(u, pool_hist, pos0, pool_w, pool_b, pool_scale)
    yb, new_conv, new_h = ssd_mixer(z, xbc, dt_raw, conv_hist, h0, conv_w, conv_b,
                                    dt_bias, a_log, d_skip, ssd_norm_w)
    x = x + g1 * (jnp.concatenate([ya, yb], axis=-1) @ w_out)
    h = rms_norm(x, norm_ffn_w) * (1.0 + sc2) + sh2
    f = jnp.square(jax.nn.relu(h @ w_ff1 + b_ff1)) @ w_ff2 + b_ff2
    x = x + g2 * f
    return x, new_pool, new_conv, new_h


def setup_inputs(seed: int = 0) -> dict:
    key = jax.random.key(seed)
    ks = jax.random.split(key, 32)
    f32 = jnp.float32
    nrm = lambda k, shape, s=1.0: jax.random.normal(k, shape, f32) * s
    dt0 = jnp.exp(jax.random.uniform(ks[17], (DEPTH, SSD_HEADS), f32) * (np.log(0.1) - np.log(0.001)) + np.log(0.001))
    dt_bias = dt0 + jnp.log(-jnp.expm1(-dt0))
    return {
        "x_prompt": nrm(ks[0], (BATCH, SEQ, D_MODEL)),
        "x_sample": nrm(ks[1], (DEC_BATCH, DEC_SEQ, D_MODEL)),
        "cache_pool": nrm(ks[2], (DEPTH, DEC_BATCH, POOL_HIST, POOL_WIDTH)),
        "cache_conv": nrm(ks[3], (DEPTH, DEC_BATCH, SSD_CONV - 1, CONV_DIM)),
        "state_ssm": nrm(ks[4], (DEPTH, DEC_BATCH, SSD_HEADS, SSD_HEAD_DIM, SSD_STATE), 0.5),
        "c_prompt": nrm(ks[5], (BATCH, D_MODEL)),
        "c_sample": nrm(ks[6], (DEC_BATCH, D_MODEL)),
        "norm_mix_w": 1.0 + nrm(ks[7], (DEPTH, D_MODEL), 0.05),
        "norm_ffn_w": 1.0 + nrm(ks[8], (DEPTH, D_MODEL), 0.05),
        "w_ada": nrm(ks[9], (DEPTH, D_MODEL, 6 * D_MODEL), 0.5 * D_MODEL ** -0.5),
        "b_ada": nrm(ks[10], (DEPTH, 6 * D_MODEL), 0.02),
        "w_in": nrm(ks[11], (DEPTH, D_MODEL, D_IN_PROJ), D_MODEL ** -0.5),
        "pool_w": nrm(ks[12], (DEPTH, POOL_GROUPS, POOL_GROUP_DIM, POOL_GROUP_DIM), POOL_GROUP_DIM ** -0.5),
        "pool_b": nrm(ks[13], (DEPTH, POOL_GROUPS, POOL_GROUP_DIM), 0.02),
        "pool_scale": 1.0 + nrm(ks[14], (DEPTH, POOL_WIDTH), 0.1),
        "conv_w": nrm(ks[15], (DEPTH, SSD_CONV, CONV_DIM), 0.5),
        "conv_b": nrm(ks[16], (DEPTH, CONV_DIM), 0.02),
        "dt_bias": dt_bias,
        "a_log": jnp.log(jax.random.uniform(ks[18], (DEPTH, SSD_HEADS), f32, 1.0, 16.0)),
        "d_skip": 1.0 + nrm(ks[19], (DEPTH, SSD_HEADS), 0.1),
        "ssd_norm_w": 1.0 + nrm(ks[20], (DEPTH, SSD_WIDTH), 0.05),
        "w_out": nrm(ks[21], (DEPTH, MIX_WIDTH, D_MODEL), MIX_WIDTH ** -0.5),
        "w_ff1": nrm(ks[22], (DEPTH, D_MODEL, D_FF), D_MODEL ** -0.5),
        "b_ff1": nrm(ks[23], (DEPTH, D_FF), 0.02),
        "w_ff2": nrm(ks[24], (DEPTH, D_FF, D_MODEL), D_FF ** -0.5),
        "b_ff2": nrm(ks[25], (DEPTH, D_MODEL), 0.02),
        "final_norm_w": 1.0 + nrm(ks[26], (D_MODEL,), 0.05),
    }


def reference(x_prompt, x_sample, cache_pool, cache_conv, state_ssm, c_prompt, c_sample,
              norm_mix_w, norm_ffn_w, w_ada, b_ada, w_in, pool_w, pool_b, pool_scale,
              conv_w, conv_b, dt_bias, a_log, d_skip, ssd_norm_w, w_out,
              w_ff1, b_ff1, w_ff2, b_ff2, final_norm_w):
    dt = x_prompt.dtype
    xp, xs = x_prompt, x_sample
    pool_p, conv_p, ssm_p = [], [], []
    pool_s, conv_s, ssm_s = [], [], []
    for l in range(DEPTH):
        w = (norm_mix_w[l], norm_ffn_w[l], w_ada[l], b_ada[l], w_in[l], pool_w[l], pool_b[l],
             pool_scale[l], conv_w[l], conv_b[l], dt_bias[l], a_log[l], d_skip[l], ssd_norm_w[l],
             w_out[l], w_ff1[l], b_ff1[l], w_ff2[l], b_ff2[l])
        zp = jnp.zeros((BATCH, POOL_HIST, POOL_WIDTH), dt)
        zc = jnp.zeros((BATCH, SSD_CONV - 1, CONV_DIM), dt)
        zh = jnp.zeros((BATCH, SSD_HEADS, SSD_HEAD_DIM, SSD_STATE), dt)
        xp, npool, nconv, nh = trunk_layer(xp, c_prompt, zp, zc, zh, 0, *w)
        pool_p.append(npool); conv_p.append(nconv); ssm_p.append(nh)
        xs, npool, nconv, nh = trunk_layer(xs, c_sample, cache_pool[l], cache_conv[l], state_ssm[l], PAST_LEN, *w)
        pool_s.append(npool); conv_s.append(nconv); ssm_s.append(nh)
    y_prompt = rms_norm(xp, final_norm_w)
    y_sample = rms_norm(xs, final_norm_w)
    return (y_prompt, y_sample, jnp.stack(pool_p, 0), jnp.stack(conv_p, 0), jnp.stack(ssm_p, 0),
            jnp.stack(pool_s, 0), jnp.stack(conv_s, 0), jnp.stack(ssm_s, 0))
```

```python
import numpy as np
from contextlib import ExitStack
import concourse.bass as bass
import concourse.mybir as mybir
from concourse.bass_utils import run_bass_kernel_spmd

F32 = mybir.dt.float32
BF16 = mybir.dt.bfloat16
AF = mybir.ActivationFunctionType
ALU = mybir.AluOpType

NCORES = 8
D = 1024
SEQ = 4096
EPS = 1e-6
SEM_LIM = 12000


class Op:
    __slots__ = ("eng", "fn", "deps", "needed", "cnt", "dma", "dsem", "dval", "name")

    def __init__(self, eng, fn, name=""):
        self.eng = eng
        self.fn = fn
        self.deps = []
        self.needed = False
        self.cnt = 0
        self.dma = False
        self.dsem = None
        self.dval = 0
        self.name = name


class Prog:
    ENGS = ("pe", "act", "dve", "pool", "sp")

    def __init__(self):
        self.streams = {e: [] for e in self.ENGS}
        self.lastw = {}
        self.readers = {}
        self.over = {}
        self.dma_cnt = {}
        self.out_dmas = []
        self.all_dma_sems = {}

    def alias(self, a_list, b_list):
        for a in a_list:
            for b in b_list:
                self.over.setdefault(a, set()).add(b)
                self.over.setdefault(b, set()).add(a)

    def _expand(self, p):
        o = self.over.get(p)
        if o:
            return [p] + list(o)
        return [p]

    def add(self, eng, fn, reads=(), writes=(), dma_sem=None, name="", is_out=False):
        o = Op(eng, fn, name)
        deps = {}

        def dep(d, kind):
            if d is None or d is o:
                return
            k = id(d)
            if k not in deps:
                deps[k] = (d, kind)
            elif kind == "raw":
                deps[k] = (d, kind)

        for p0 in reads:
            for p in self._expand(p0):
                dep(self.lastw.get(p), "raw")
            if p0.startswith("ps"):
                r = self.readers.get(p0)
                if r:
                    for d in r[0].values():
                        dep(d, "war")
        for p0 in writes:
            for p in self._expand(p0):
                dep(self.lastw.get(p), "waw")
                r = self.readers.get(p)
                if r:
                    for d in r[0].values():
                        dep(d, "war")
                    for d in r[1]:
                        dep(d, "war")
        o.deps = list(deps.values())
        for p in reads:
            r = self.readers.setdefault(p, [{}, []])
            if dma_sem is not None:
                r[1].append(o)
            else:
                r[0][eng] = o
        for p in writes:
            self.lastw[p] = o
            self.readers[p] = [{}, []]
        if dma_sem is not None:
            o.dma = True
            o.dsem = dma_sem
            c = self.dma_cnt.get(id(dma_sem), 0) + 1
            self.dma_cnt[id(dma_sem)] = c
            o.dval = 16 * c
            self.all_dma_sems[id(dma_sem)] = dma_sem
            if is_out:
                self.out_dmas.append(o)
        self.streams[eng].append(o)
        return o

    def finalize(self, nc, stack):
        for e in self.ENGS:
            for o in self.streams[e]:
                for d, kind in o.deps:
                    if d.dma:
                        continue
                    if d.eng == o.eng and o.eng == "pe":
                        continue
                    d.needed = True
        sems = {}
        for e in self.ENGS:
            c = 0
            for o in self.streams[e]:
                if o.needed and not o.dma:
                    c += 1
                    o.cnt = c
            nep = (c + SEM_LIM - 1) // SEM_LIM + 1
            sems[e] = [stack.enter_context(nc.semaphore(f"s_{e}_{i}")) for i in range(nep)]
        self.sems = sems
        return sems

    def emit_stream(self, e, eh):
        sems = self.sems
        done = {k: 0 for k in self.ENGS}
        dwait = {}
        for o in self.streams[e]:
            need = {}
            for d, kind in o.deps:
                if d.dma:
                    k = id(d.dsem)
                    if dwait.get(k, 0) < d.dval:
                        dwait[k] = d.dval
                        eh.wait_ge(d.dsem, d.dval)
                    continue
                if d.eng == o.eng and o.eng == "pe":
                    continue
                if d.cnt > need.get(d.eng, 0):
                    need[d.eng] = d.cnt
            for de, c in need.items():
                if c > done[de]:
                    done[de] = c
                    ep = (c - 1) // SEM_LIM
                    loc = (c - 1) % SEM_LIM + 1
                    eh.wait_ge(sems[de][ep], loc)
            ins = o.fn(eh)
            if o.dma:
                ins.then_inc(o.dsem, 16)
            elif o.needed:
                ep = (o.cnt - 1) // SEM_LIM
                ins.then_inc(sems[e][ep], 1)
        if e == "sp":
            for k, sm in self.all_dma_sems.items():
                eh.wait_ge(sm, self.dma_cnt[k] * 16)


class Cfg:
    n_pseq = 2
    n_ptiles = 8
    n_sseq = 4
    do_sample = True
    R = 4
    stop = 99
    stop_tile = 0
    dump = ()


V_NMW, V_NFW, V_PB, V_PS, V_CW, V_CB, V_B1, V_B2, V_BADA, V_SNW = 0, 8, 16, 24, 32, 80, 92, 124, 132, 180
NV = 188
R_FNW, R_SNW, R_DSK, R_DTB, R_ALOG = 0, 1024, 2048, 2064, 2080
NR = 2096


def build_program(cfg):
    nc = bass.Bass("TRN2", target_bir_lowering=False)
    P = Prog()
    stack = ExitStack()

    def din(name, shape, dt=F32):
        return nc.dram_tensor(name, list(shape), dt, kind="ExternalInput").ap()

    def dout(name, shape, dt=F32):
        return nc.dram_tensor(name, list(shape), dt, kind="ExternalOutput").ap()

    def dscr(name, shape, dt=BF16):
        return nc.dram_tensor(name, list(shape), dt, kind="Internal").ap()

    NPS, NSS = cfg.n_pseq, cfg.n_sseq
    NSEQ = NPS + NSS
    xp = din("xp", [2, SEQ, D])
    xs = din("xs", [64, D])
    cpool = din("cpool", [128, 8, 4, 15])
    cconv = din("cconv", [128, 12, 4, 3])
    sstate = din("sstate", [4, 128, 1024])
    cT_d = din("cT", [128, 8, 6])
    wada_d = din("w_ada_r", [12, 128, 8 * 512])
    win_d = din("w_in_r", [7, 128, 8 * 512])
    wdt_d = din("w_dt_r", [128, 8 * 16])
    poolw_d = din("pool_w_r", [128, 4 * 2 * 256])
    wout_d = din("w_out_r", [8, 128, 16 * 128])
    wff1_d = din("w_ff1_r", [8, 128, 8 * 512])
    wff2_d = din("w_ff2_r", [8, 128, 32 * 128])
    vecs_d = din("vecs", [128, NV])
    rows_d = din("rows", [1, NR])

    y_p = dout("y_p", [2, SEQ, D])
    y_s = dout("y_s", [64, D])
    npool_p = dout("npool_p", [2, 15, 1024])
    nconv_p = dout("nconv_p", [2, 3, 1536])
    nssm_p = dout("nssm_p", [2, 1024, 128])
    npool_s = dout("npool_s", [4, 15, 1024])
    nconv_s = dout("nconv_s", [4, 3, 1536])
    nssm_s = dout("nssm_s", [4, 1024, 128])

    win_b = dscr("win_b", [7, 128, 4096])
    wdt_b = dscr("wdt_b", [128, 128])
    poolw_b = dscr("poolw_b", [128, 2048])
    wout_b = dscr("wout_b", [8, 128, 2048])
    wff1_b = dscr("wff1_b", [8, 128, 4096])
    wff2_b = dscr("wff2_b", [8, 128, 4096])

    def sem(name):
        return stack.enter_context(nc.semaphore(name))

    def sb(name, shape, dt=F32):
        return nc.alloc_sbuf_tensor("sb_" + name, list(shape), dt)

    ident_f = sb("ident_f", [128, 128])
    ident_b = sb("ident_b", [128, 128], BF16)
    mask_le = sb("mask_le", [128, 128])
    mask_gt = sb("mask_gt", [128, 128])
    ones_f = sb("ones_f", [128, 128])
    epsb = sb("epsb", [128, 2])
    vecs = sb("vecs", [128, NV])
    fnwb = sb("fnwb", [128, 1024])
    dskb = sb("dskb", [128, 16])
    dtbb = sb("dtbb", [128, 16])
    Ab = sb("Ab", [128, 16])
    diag = sb("diag", [128, 48, 128], BF16)
    Did = sb("Did", [128, 16, 128], BF16)
    corr = sb("corr", [128, 4, 16])
    cT = sb("cT", [128, 8, 6])
    mod = sb("mod", [128, 48, 6])
    wm1 = sb("wm1", [128, 8, 6])
    wm2 = sb("wm2", [128, 8, 6])
    g2b2 = sb("g2b2", [128, 8, 6])
    hT = sb("hT", [128, 1024])
    hTb = sb("hTb", [128, 1024], BF16)
    xres = sb("xres", [128, 7, 1024])
    xn = sb("xn", [128, 2, 1024], BF16)
    ystage = xn[:, :, :].bitcast(F32) if False else None
    hTt = sb("hTt", [128, 8, 512], BF16)
    ss = sb("ss", [128, 16])
    catT = sb("catT", [128, 16, 512], BF16)
    G1 = sb("G1", [128, 18720], BF16)
    aT = G1[:, 0:16384].rearrange("p (k t) -> p k t", k=32)
    U_p = G1[:, 0:8 * 527].rearrange("p (k s t) -> p k s t", k=8, s=1)
    U_s = G1[:, 0:8 * 4 * 31].rearrange("p (k s t) -> p k s t", k=8, s=4)
    o1 = 8 * 527
    PSCR = G1[:, o1:o1 + 4 * 2 * 527].bitcast(F32)
    o2 = o1 + 4 * 2 * 527
    pooled = G1[:, o2:o2 + 4 * 2 * 512].rearrange("p (r k t) -> p r k t", r=4, k=2)
    o3 = o2 + 4096
    XBC_p = G1[:, o3:o3 + 12 * 515].rearrange("p (k s t) -> p k s t", k=12, s=1)
    XBC_s = G1[:, o3:o3 + 12 * 4 * 19].rearrange("p (k s t) -> p k s t", k=12, s=4)
    assert o3 + 12 * 515 <= 18720
    G2 = sb("G2", [128, 10240], BF16)
    XC = G2[:, 0:6144].rearrange("p (k t) -> p k t", k=12)
    zs = G2[:, 6144:10240].rearrange("p (b d) -> p b d", b=4)
    fT = G2[:, 0:8192].bitcast(F32).rearrange("p (k t) -> p k t", k=8)
    dtr = sb("dtr", [128, 4, 16])
    dtt = sb("dtt", [128, 4, 16])
    dte = sb("dte", [128, 4, 16])
    lnt = sb("lnt", [128, 16])
    uhist = sb("uhist", [128, 8, 1, 15], BF16)
    xhist = sb("xhist", [128, 12, 1, 3], BF16)
    a_t = sb("a_t", [128, 4, 16])
    eall = sb("eall", [128, 2, 48])
    xdt = sb("xdt", [128, 1, 1024], BF16)
    xdtd = sb("xdtd", [128, 1, 1024], BF16)
    xD = sb("xD", [128, 1, 1024], BF16)
    Btm = sb("Btm", [128, 2, 256], BF16)
    CBm = sb("CBm", [128, 2, 2, 128], BF16)
    ostg = sb("ostg", [128, 1024])
    segb = sb("segb", [128, 2048], BF16)
    mask_gt_b = sb("mask_gt_b", [128, 128], BF16)
    Mt = sb("Mt", [128, 1, 2048], BF16)
    ytm = sb("ytm", [128, 1, 1024])
    ynb = sb("ynb", [128, 1, 1024], BF16)
    ssq = sb("ssq", [128, 8])
    ulast = sb("ulast", [128, 8, 4, 15])
    xlast = sb("xlast", [128, 12, 4, 3])
    ostage = ostg[:, :]
    cstage = Mt[:, 0, 0:1440].bitcast(F32).rearrange("p (k s t) -> p k s t", k=12, s=4)
    R = cfg.R
    ring = [sb(f"ring{i}", [128, 4096], BF16) for i in range(R)]
    ring_sem = [sem(f"rs{i}") for i in range(R)]
    wada_st = [G1[:, 0:8192].bitcast(F32), G1[:, 8192:16384].bitcast(F32)]
    wada_sem = [sem("was0"), sem("was1")]

    aTn = [f"aT{m}" for m in range(32)]
    G1P = ["U", "PSCR0", "PSCR1", "pooled0", "pooled1", "pooled2", "pooled3", "XBC"]
    P.alias(aTn, G1P)
    P.alias(["fT"], ["XC", "zs0", "zs1", "zs2", "zs3"])
    P.alias(["ynbw"], ["ynb0", "ynb1"])
    P.alias(["Mtg0", "Mtg1"], ["cstage"])
    P.alias(["wst0", "wst1"], aTn + G1P)

    PS = [nc.alloc_psum_tensor(f"ps{i}", [128, 1024], F32) for i in range(4)]
    ps_state = {"i": 0}

    ps_held = set()

    def ps1():
        i = ps_state["i"]
        while i in ps_held:
            i = (i + 1) % 8
        ps_state["i"] = (i + 1) % 8
        t = PS[i // 2]
        h = i % 2
        return t[:, h * 512:(h + 1) * 512], [f"ps{i}"]

    def ps2():
        i = ps_state["i"]
        if i % 2:
            i = (i + 1) % 8
        while i in ps_held or (i + 1) in ps_held:
            i = (i + 2) % 8
        ps_state["i"] = (i + 2) % 8
        return PS[i // 2][:, :], [f"ps{i}", f"ps{i + 1}"]

    def act(out, in_, func, reads, writes, bias=None, scale=None, accum=None, name=""):
        kw = {}
        if bias is not None:
            kw["bias"] = bias
        if scale is not None:
            kw["scale"] = scale
        if accum is not None:
            kw["accum_out"] = accum
        return P.add("act", lambda e: e.activation(out=out, in_=in_, func=func, **kw), reads, writes, name=name)

    def tt(eng, out, in0, in1, op, reads, writes, name=""):
        return P.add(eng, lambda e: e.tensor_tensor(out=out, in0=in0, in1=in1, op=op), reads, writes, name=name)

    def ts(eng, out, in0, s1, s2, op0, op1, reads, writes, name=""):
        if op1 is None:
            return P.add(eng, lambda e: e.tensor_scalar(out=out, in0=in0, scalar1=s1, scalar2=None, op0=op0),
                         reads, writes, name=name)
        return P.add(eng, lambda e: e.tensor_scalar(out=out, in0=in0, scalar1=s1, scalar2=s2, op0=op0, op1=op1),
                     reads, writes, name=name)

    def stt(eng, out, in0, scalar, in1, op0, op1, reads, writes, name=""):
        return P.add(eng, lambda e: e.scalar_tensor_tensor(out=out, in0=in0, scalar=scalar, in1=in1, op0=op0, op1=op1),
                     reads, writes, name=name)

    def cp(eng, out, in_, reads, writes, name=""):
        if eng == "act":
            return act(out, in_, AF.Identity, reads, writes, name=name)
        return P.add(eng, lambda e: e.tensor_copy(out=out, in_=in_), reads, writes, name=name)

    def memset(eng, ap, val, writes):
        return P.add(eng, lambda e: e.memset(ap, val), (), writes)

    def dma(out, in_, reads, writes, s, eng="sp", is_out=False, name=""):
        return P.add(eng, lambda e: e.dma_start(out=out, in_=in_), reads, writes, dma_sem=s, is_out=is_out, name=name)

    def mmgroup(out, pairs, reads, writes, name=""):
        n = len(pairs)

        def fn(e):
            ins = None
            for i, (l, r) in enumerate(pairs):
                ins = e.matmul(out, lhsT=l, rhs=r, start=(i == 0), stop=(i == n - 1))
            return ins
        return P.add("pe", fn, reads, writes, name=name)

    def transposes(items, reads, writes, name=""):
        def fn(e):
            ins = None
            for (o, i, idn) in items:
                ins = e.transpose(out=o, in_=i, identity=idn)
            return ins
        return P.add("pe", fn, reads, writes, name=name)

    cs = {k: sem("cs_" + k) for k in ("win", "wdt", "poolw", "wout", "wff1", "wff2")}
    for s_ in range(7):
        dma(win_b[s_], win_d[s_], (), ["d_win"], cs["win"], eng="pool")
    dma(wdt_b[:, :], wdt_d[:, :], (), ["d_wdt"], cs["wdt"], eng="pool")
    dma(poolw_b[:, :], poolw_d[:, :], (), ["d_poolw"], cs["poolw"], eng="pool")
    def cast_wout():
        for s_ in range(8):
            dma(wout_b[s_], wout_d[s_], (), ["d_wout"], cs["wout"], eng="pool")

    def cast_ff1(s_):
        dma(wff1_b[s_], wff1_d[s_], (), ["d_wff1"], cs["wff1"], eng="pool")

    def cast_ff2(s_):
        dma(wff2_b[s_], wff2_d[s_], (), ["d_wff2"], cs["wff2"], eng="pool")

    misc = [sem(f"misc{i}") for i in range(6)]
    dma(vecs[:, :], vecs_d[:, :], (), ["vecs"], misc[0])
    dma(cT[:, :, :], cT_d[:, :, :], (), ["cT"], misc[1])
    dma(fnwb[:, :], rows_d[0:1, R_FNW:R_FNW + 1024].partition_broadcast(128), (), ["fnwb"], misc[2])
    dma(dskb[:, :], rows_d[0:1, R_DSK:R_DSK + 16].partition_broadcast(128), (), ["dskb"], misc[4])
    dma(dtbb[:, :], rows_d[0:1, R_DTB:R_DTB + 16].partition_broadcast(128), (), ["dtbb"], misc[5])
    alog_sem = sem("alog")
    dma(Ab[:, :], rows_d[0:1, R_ALOG:R_ALOG + 16].partition_broadcast(128), (), ["Ab"], alog_sem)

    memset("pool", ident_f[:, :], 0.0, ["ident_f"])
    P.add("pool", lambda e: e.affine_select(out=ident_f[:, :], in_=ident_f[:, :], compare_op=ALU.not_equal, fill=1.0,
                                            base=0, pattern=[[-1, 128]], channel_multiplier=1),
          ["ident_f"], ["ident_f"])
    cp("pool", ident_b[:, :], ident_f[:, :], ["ident_f"], ["ident_b"])
    memset("pool", ones_f[:, :], 1.0, ["ones_f"])
    memset("pool", epsb[:, :], EPS, ["epsb"])
    memset("pool", epsb[:, 1:2], 1.0, ["epsb"])
    P.add("pool", lambda e: e.affine_select(out=mask_le[:, :], in_=ones_f[:, :], compare_op=ALU.is_ge, fill=0.0,
                                            base=0, pattern=[[1, 128]], channel_multiplier=-1),
          ["ones_f"], ["mask_le"])
    P.add("pool", lambda e: e.affine_select(out=mask_gt[:, :], in_=ones_f[:, :], compare_op=ALU.is_gt, fill=0.0,
                                            base=0, pattern=[[-1, 128]], channel_multiplier=1),
          ["ones_f"], ["mask_gt"])
    cp("pool", mask_gt_b[:, :], mask_gt[:, :], ["mask_gt"], ["mask_gt_b"])
    for j in range(48):
        ts("pool", diag[:, j, :], ident_f[:, :], vecs[:, V_CW + j:V_CW + j + 1], None, ALU.mult, None,
           ["ident_f", "vecs"], ["diag"])
    for h_ in range(16):
        ts("pool", Did[:, h_, :], ident_f[:, :], dskb[:, h_:h_ + 1], None, ALU.mult, None,
           ["ident_f", "dskb"], ["Did"])
    memset("pool", corr[:, :, :], 1.0, ["corr"])
    for g, w in enumerate((2, 4, 8, 16)):
        for t in range(w - 1):
            memset("pool", corr[:, g, t:t + 1], float(w) / float(t + 1), ["corr"])
    act(Ab[:, :], Ab[:, :], AF.Exp, ["Ab"], ["Ab"])
    ts("dve", Ab[:, :], Ab[:, :], -1.0, None, ALU.mult, None, ["Ab"], ["Ab"])
    act(cT[:, :, :], cT[:, :, :], AF.Silu, ["cT"], ["cT"])

    xsem = [sem(f"xs{b}") for b in range(7)]
    for b_ in range(4):
        if cfg.n_pseq > 0 and cfg.n_ptiles > 0:
            dma(xres[0:128, b_, :], xp[0, b_ * 128:(b_ + 1) * 128, :], (), [f"x{b_}"], xsem[b_])
        else:
            dma(xres[0:16, b_, :], xs[b_ * 16:(b_ + 1) * 16, :], (), [f"x{b_}"], xsem[b_])
    modrow = ostg
    for sl in range(12):
        st = wada_st[sl % 2]
        stn = f"wst{sl % 2}"
        dma(st, wada_d[sl], (), [stn], wada_sem[sl % 2])
        stv = st.rearrange("p (k n) -> p k n", k=8)
        pt, pn = ps1()
        mmgroup(pt[0:6, :], [(cT[:, kc, :], stv[:, kc, :]) for kc in range(8)], [stn, "cT"], pn)
        cp("act", modrow[0:6, 0:512], pt[0:6, :], pn, ["ostage"])
        ptT, pnT = ps1()
        transposes([(ptT[:, j * 8:j * 8 + 6], modrow[0:6, j * 128:(j + 1) * 128], ident_f[0:6, 0:6]) for j in range(4)],
                   ["ostage", "ident_f"], pnT)
        tt("dve", mod[:, sl * 4:(sl + 1) * 4, :],
           ptT[:, 0:32].rearrange("p (j s) -> p j s", j=4)[:, :, 0:6],
           vecs[:, V_BADA + sl * 4:V_BADA + sl * 4 + 4].unsqueeze(2).to_broadcast([128, 4, 6]),
           ALU.add, pnT + ["vecs"], ["mod"])
    ts("dve", wm1[:, :, :], mod[:, 8:16, :], 1.0, None, ALU.add, None, ["mod"], ["wm1"])
    tt("dve", wm1[:, :, :], wm1[:, :, :], vecs[:, V_NMW:V_NMW + 8].unsqueeze(2).to_broadcast([128, 8, 6]), ALU.mult,
       ["wm1", "vecs"], ["wm1"])
    ts("dve", wm2[:, :, :], mod[:, 32:40, :], 1.0, None, ALU.add, None, ["mod"], ["wm2"])
    tt("dve", wm2[:, :, :], wm2[:, :, :], vecs[:, V_NFW:V_NFW + 8].unsqueeze(2).to_broadcast([128, 8, 6]), ALU.mult,
       ["wm2", "vecs"], ["wm2"])
    tt("dve", g2b2[:, :, :], mod[:, 40:48, :], vecs[:, V_B2:V_B2 + 8].unsqueeze(2).to_broadcast([128, 8, 6]), ALU.mult,
       ["mod", "vecs"], ["g2b2"])

    def modv(part, kc, si):
        return mod[:, part * 8 + kc, si:si + 1]

    tiles = []
    for sq in range(NPS):
        for ti in range(cfg.n_ptiles):
            tiles.append(dict(kind="p", seq=sq, ti=ti, S=1, L=512, Q=128, NB=4,
                              first=(ti == 0), last=(ti == cfg.n_ptiles - 1)))
    if cfg.do_sample:
        tiles.append(dict(kind="s", S=4, L=16, Q=16, NB=4, first=True, last=True))

    slabs = []
    for _t in tiles:
        for s_ in range(4):
            slabs.append(("win", win_b[s_], 4096, "d_win", cs["win"], 7))
        slabs.append(("wdt", wdt_b[:, :], 128, "d_wdt", cs["wdt"], 1))
        for s_ in range(4, 7):
            slabs.append(("win", win_b[s_], 4096, "d_win", cs["win"], 7))
        slabs.append(("poolw", poolw_b[:, :], 2048, "d_poolw", cs["poolw"], 1))
        for s_ in range(8):
            slabs.append(("wout", wout_b[s_], 2048, "d_wout", cs["wout"], 8))
        for s_ in range(8):
            slabs.append(("wff1", wff1_b[s_], 4096, "d_wff1", cs["wff1"], 8))
        for s_ in range(8):
            slabs.append(("wff2", wff2_b[s_], 4096, "d_wff2", cs["wff2"], 8))
    slab_state = {"issued": 0, "cur": 0}

    def slab_issue():
        j = slab_state["issued"]
        if j >= len(slabs):
            return
        kind, src, n, piece, _csem, _ncast = slabs[j]
        slot = j % R
        dma(ring[slot][:, 0:n], src, [piece], [f"ring{slot}"], ring_sem[slot], name=f"slab{j}")
        slab_state["issued"] = j + 1

    def slab_get(kind, ahead=0):
        j = slab_state["cur"] + ahead
        assert slabs[j][0] == kind, (slabs[j][0], kind)
        slot = j % R
        return ring[slot], f"ring{slot}"

    def slab_done():
        slab_state["cur"] += 1
        slab_issue()

    for _ in range(R):
        slab_issue()

    ysem = [sem("ys0"), sem("ys1")]
    osem = sem("osem")
    csem_ = sem("csem")
    stsem = sem("stsem")
    x3sem = sem("x3sem")
    y_slot = {"i": 0}
    yst = xn[:, :, :].rearrange("p b d -> p (b d)").bitcast(F32).rearrange("p (r d) -> p r d", r=1)

    def rstd_from_ss(col, Q, n):
        act(lnt[0:Q, col:col + 1], ss[0:Q, col:col + 1], AF.Ln, [f"ss{col}", "epsb"], [f"lnt{col}"], bias=epsb[0:Q, 0:1], scale=1.0 / n)
        act(ss[0:Q, col:col + 1], lnt[0:Q, col:col + 1], AF.Exp, [f"lnt{col}"], [f"ss{col}"], scale=-0.5)

    def emit_xload(t, b):
        Q = t["Q"]
        if t["kind"] == "p":
            src = xp[t["seq"], t["ti"] * 512 + b * 128:t["ti"] * 512 + (b + 1) * 128, :]
        else:
            src = xs[b * 16:(b + 1) * 16, :]
        sl = t["xs"][b]
        dma(xres[0:Q, sl, :], src, (), [f"x{sl}"], xsem[sl])

    hTn = ["hTt0", "hTt1", "hTt2", "hTt3"]
    nbufs = [(xdt, "xdt"), (xD, "xD"), (xdtd, "xdtd"), (ynb, "ynbw")]

    def norm_stats(t, b, wm, shpart, alt=None):
        S, L, Q, NB = t["S"], t["L"], t["Q"], t["NB"]
        xs_ = t["xs"]
        nb_, nbn = nbufs[b]
        if alt is None:
            xsrc, xpc = xres[0:Q, xs_[b], :], f"x{xs_[b]}"
        else:
            xsrc, xpc = alt
        act(nb_[0:Q, 0, :], xsrc, AF.Square, [xpc], [nbn, f"ss{b}"], accum=ss[0:Q, b:b + 1])
        rstd_from_ss(b, Q, 1024.0)
        ts("dve", nb_[0:Q, 0, :], xsrc, ss[0:Q, b:b + 1], None, ALU.mult, None, [xpc, f"ss{b}"], [nbn])

    def norm_trev(t, b, wm, shpart):
        S, L, Q, NB = t["S"], t["L"], t["Q"], t["NB"]
        nb_, nbn = nbufs[b]
        pt2, pn2 = ps2()
        ptb = pt2.bitcast(BF16)
        for half in range(2):
            transposes([(ptb[:, half * 1024 + k4 * 128:half * 1024 + k4 * 128 + Q],
                         nb_[0:Q, 0, (half * 4 + k4) * 128:(half * 4 + k4 + 1) * 128], ident_b[0:Q, 0:Q])
                        for k4 in range(4)], [nbn, "ident_b"], [pn2[half]])
        si = t["seq"] if t["kind"] == "p" else 2 + b
        for kc in range(8):
            half, k4 = kc // 4, kc % 4
            src = ptb[:, half * 1024 + k4 * 128:half * 1024 + k4 * 128 + Q]
            dstp = hTt[:, kc, b * Q:(b + 1) * Q]
            if half == 0:
                act(dstp, src, AF.Identity, [pn2[half], wm[1], "mod"], [f"hTt{b}"],
                    bias=modv(shpart, kc, si), scale=wm[0][:, kc, si:si + 1])
            else:
                ts("dve", dstp, src, wm[0][:, kc, si:si + 1], modv(shpart, kc, si), ALU.mult, ALU.add,
                   [pn2[half], wm[1], "mod"], [f"hTt{b}"])

    def norm_all(t, wm, shpart):
        NB = t["NB"]
        norm_stats(t, 0, wm, shpart)
        for b in range(NB):
            if b + 1 < NB:
                norm_stats(t, b + 1, wm, shpart)
            norm_trev(t, b, wm, shpart)

    def emit_tile(t):
        S, L, Q, NB = t["S"], t["L"], t["Q"], t["NB"]
        T = S * L
        isp = t["kind"] == "p"
        xs_ = t["xs"]
        U = U_p if isp else U_s
        XBC = XBC_p if isp else XBC_s
        if isp and t["first"]:
            memset("pool", U[:, :, :, 0:15], 0.0, ["U"])
            memset("pool", XBC[:, :, :, 0:3], 0.0, ["XBC"])
        elif isp:
            cp("pool", U[:, :, :, 0:15], uhist[:, :, :, :], ["uhist"], ["U"])
            cp("pool", XBC[:, :, :, 0:3], xhist[:, :, :, :], ["xhist"], ["XBC"])
        if not isp:
            dma(cstage[:, 0:8, :, :], cpool[:, :, :, :], (), ["cstage"], csem_)
            cp("pool", U[:, :, :, 0:15], cstage[:, 0:8, :, :], ["cstage"], ["U"])
            dma(cstage[:, :, :, 0:3], cconv[:, :, :, :], (), ["cstage"], csem_)
            cp("pool", XBC[:, :, :, 0:3], cstage[:, :, :, 0:3], ["cstage"], ["XBC"])
        if t["idx"] == 0:
            norm_all(t, (wm1, "wm1"), 0)
        if t['stop'] <= 1:
            return

        def pool_elem(g):
            w = (2, 4, 8, 16)[g]
            W = 15 + L
            ug = U[:, 2 * g:2 * g + 2, :, :]
            sc = [PSCR[:, i * (2 * S * W):(i + 1) * (2 * S * W)].rearrange("p (k s t) -> p k s t", k=2, s=S)
                  for i in range(2)]
            cur, curn = ug, "U"
            sh = 1
            nb = 0
            while sh < w:
                dst, dn = sc[nb % 2], f"PSCR{nb % 2}"
                tt("dve", dst[:, :, :, sh:W], cur[:, :, :, sh:W], cur[:, :, :, 0:W - sh], ALU.add, [curn], [dn])
                cur, curn = dst, dn
                sh *= 2
                nb += 1
            if isp and t["first"]:
                for k2 in range(2):
                    tt("dve", cur[:, k2, 0, 15:31], cur[:, k2, 0, 15:31], corr[:, g, :], ALU.mult,
                       [curn, "corr"], [curn])
            stt("dve", pooled[:, g, :, 0:T].rearrange("p k (s l) -> p k s l", s=S), cur[:, :, :, 15:W], 1.0 / w,
                ug[:, :, :, 15:W], ALU.mult, ALU.subtract, [curn, "U"], [f"pooled{g}"])
        for sl in range(2):
            slot, sn = slab_get("win")
            wv = slot[:, :].rearrange("p (k n) -> p k n", k=8)
            for j in range(4):
                m = sl * 4 + j
                pt, pn = ps1()
                mmgroup(pt[:, 0:T], [(wv[:, kc, j * 128:(j + 1) * 128], hTt[:, kc, 0:T]) for kc in range(8)],
                        [sn] + hTn, pn)
                src = pt[:, 0:T].rearrange("p (s l) -> p s l", s=S)
                import os
                if not os.environ.get("DBG_NOEVAC"):
                    cp("act", U[:, m, :, 15:15 + L], src, pn, ["U"])
                if t["last"] and not os.environ.get("DBG_NOULAST"):
                    if os.environ.get("DBG_ULV") == "1":
                        cp("dve", ulast[:, m, 0, :], pt[:, T - 15:T], pn, ["ulast"])
                    elif os.environ.get("DBG_ULV") == "2":
                        cp("dve", ulast[:, m, 0:S, 0:8], src[:, :, L - 8:L], pn, ["ulast"])
                    elif os.environ.get("DBG_ULV") == "3":
                        cp("dve", ulast[:, m, 0:S, :], src[:, :, 0:15], pn, ["ulast"])
                    else:
                        cp(os.environ.get("DBG_ULENG", "dve"), ulast[:, m, 0:S, :], src[:, :, L - 15:L], pn, ["ulast"])
            slab_done()
            pool_elem(2 * sl)
            pool_elem(2 * sl + 1)
            if t["idx"] == 0 and sl == 1:
                cast_wout()

        if isp and not t["last"]:
            cp("pool", uhist[:, :, :, :], U[:, :, :, L:L + 15], ["U"], ["uhist"])
        if t['stop'] <= 1.2:
            return
        for sl in range(2):
            slot, sn = slab_get("win")
            wv = slot[:, :].rearrange("p (k n) -> p k n", k=8)
            for b in range(NB):
                pt, pn = ps1()
                mmgroup(pt[0:Q, :], [(hTt[:, kc, b * Q:(b + 1) * Q], wv[:, kc, :]) for kc in range(8)],
                        [sn] + hTn, pn)
                act(zs[0:Q, b, sl * 512:(sl + 1) * 512], pt[0:Q, :], AF.Silu, pn, [f"zs{b}"])
            slab_done()
        slot, sn = slab_get("wdt")
        wv = slot[:, 0:128].rearrange("p (k n) -> p k n", k=8)
        pt, pn = ps1()
        for b in range(NB):
            mmgroup(pt[0:Q, b * 16:(b + 1) * 16], [(hTt[:, kc, b * Q:(b + 1) * Q], wv[:, kc, :]) for kc in range(8)],
                    [sn] + hTn, pn)
        slab_done()
        ptv = pt[0:Q, 0:64].rearrange("p (b h) -> p b h", b=4)
        tt("dve", dtr[0:Q, :, :], ptv, dtbb[0:Q, :].unsqueeze(1).to_broadcast([Q, 4, 16]), ALU.add,
           pn + ["dtbb"], ["dtr"])
        act(dte[0:Q, :, :], dtr[0:Q, :, :], AF.Exp, ["dtr"], ["dte"])
        act(dtt[0:Q, :, :], dte[0:Q, :, :], AF.Ln, ["dte", "epsb"], ["dtt"], bias=epsb[0:Q, 1:2])
        tt("dve", a_t[0:Q, :, :], dtt[0:Q, :, :], Ab[0:Q, :].unsqueeze(1).to_broadcast([Q, 4, 16]), ALU.mult,
           ["dtt", "Ab"], ["a_t"])
        if t['stop'] <= 1.4:
            return
        for sl in range(3):
            slot, sn = slab_get("win")
            wv = slot[:, :].rearrange("p (k n) -> p k n", k=8)
            for j in range(4):
                m = sl * 4 + j
                pt, pn = ps1()
                mmgroup(pt[:, 0:T], [(wv[:, kc, j * 128:(j + 1) * 128], hTt[:, kc, 0:T]) for kc in range(8)],
                        [sn] + hTn, pn)
                src = pt[:, 0:T].rearrange("p (s l) -> p s l", s=S)
                cp("act", XBC[:, m, :, 3:3 + L], src, pn, ["XBC"])
                if t["last"]:
                    cp("dve", xlast[:, m, 0:S, :], src[:, :, L - 3:L], pn, ["xlast"])
            slab_done()
        if t['stop'] <= 1.6:
            return
        if t['stop'] <= 2:
            return
        slot, sn = slab_get("poolw")
        pwv = slot[:, 0:2048].rearrange("p (g k d) -> p g k d", g=4, k=2)
        for g in range(4):
            for m2 in range(2):
                pt, pn = ps1()
                mmgroup(pt[:, 0:T], [(pwv[:, g, k2, m2 * 128:(m2 + 1) * 128], pooled[:, g, k2, 0:T]) for k2 in range(2)],
                        [sn, f"pooled{g}"], pn)
                m = 2 * g + m2
                ts("dve", catT[:, m, 0:T], pt[:, 0:T], vecs[:, V_PB + m:V_PB + m + 1], vecs[:, V_PS + m:V_PS + m + 1],
                   ALU.add, ALU.mult, pn + ["vecs"], [f"cat{m}"])
        slab_done()
        if t['stop'] <= 3:
            return
        for m in range(12):
            pt, pn = ps1()
            mmgroup(pt[:, 0:T].rearrange("p (s l) -> p s l", s=S) if S > 1 else pt[:, 0:T],
                    [(diag[:, k * 12 + m, :], (XBC[:, m, :, k:k + L] if S > 1 else XBC[:, m, 0, k:k + L]))
                     for k in range(4)],
                    ["diag", "XBC"], pn)
            act(XC[:, m, 0:T], pt[:, 0:T], AF.Silu, pn + ["vecs"], ["XC"], bias=vecs[:, V_CB + m:V_CB + m + 1])
        if isp and not t["last"]:
            cp("pool", xhist[:, :, :, :], XBC[:, :, :, L:L + 3], ["XBC"], ["xhist"])
        if t['stop'] <= 4:
            return
        NEARLY = 3
        wo_early = []

        def wout_first_half():
            for m in range(NEARLY):
                slot, sn = slab_get("wout", ahead=m)
                wv = slot[:, 0:2048].rearrange("p (k n) -> p k n", k=16)
                pt, pn = ps1()
                pairs = [(wv[:, kc, :], catT[:, kc, 0:T]) for kc in range(8)]
                def fn1(e, pairs=pairs, pt=pt, T=T):
                    ins = None
                    for i, (l, r) in enumerate(pairs):
                        ins = e.matmul(pt[:, 0:T], lhsT=l, rhs=r, start=(i == 0), stop=False)
                    return ins
                P.add("pe", fn1, [sn] + [f"cat{k}" for k in range(8)], pn)
                wo_early.append((slot, sn, pt, pn))
                ps_held.add(int(pn[0][2:]))

        def ssd_s1(b):
            r2 = b % 2
            c0, c1 = b * Q, (b + 1) * Q
            pt, pn = ps1()
            mmgroup(pt[0:Q, 0:16], [(mask_le[0:Q, 0:Q], a_t[0:Q, b, :])], ["mask_le", "a_t"], pn)
            mmgroup(pt[0:Q, 16:32], [(mask_gt[0:Q, 0:Q], a_t[0:Q, b, :])], ["mask_gt", "a_t"], pn)
            mmgroup(pt[:, 32:48], [(ones_f[0:Q, :], a_t[0:Q, b, :])], ["ones_f", "a_t"], pn)
            act(eall[0:Q, r2, 0:32], pt[0:Q, 0:32], AF.Exp, pn, [f"eall{r2}"])
            act(eall[:, r2, 32:48], pt[:, 32:48], AF.Exp, pn, [f"eall{r2}"])
            edec = eall[0:Q, r2, 16:32]
            pt, pn = ps1()
            ptb = pt.bitcast(BF16)
            transposes([(ptb[0:Q, kc * 128:(kc + 1) * 128], XC[:, kc, c0:c1], ident_b[:, :]) for kc in range(8)],
                       ["XC", "ident_b"], pn)
            pxv = ptb[0:Q, 0:1024].rearrange("p (h d) -> p h d", h=16)
            tt("dve", xdt[0:Q, 0, :].rearrange("p (h d) -> p h d", h=16), pxv,
               dtt[0:Q, b, :].unsqueeze(2).to_broadcast([Q, 16, 64]), ALU.mult, pn + ["dtt"], ["xdt"])
            cp("act", xD[0:Q, 0, :], ptb[0:Q, 0:1024], pn, ["xD"])
            tt("pool", xdtd[0:Q, 0, :].rearrange("p (h d) -> p h d", h=16),
               xdt[0:Q, 0, :].rearrange("p (h d) -> p h d", h=16),
               edec.unsqueeze(2).to_broadcast([Q, 16, 64]), ALU.mult, ["xdt", f"eall{r2}"], ["xdtd"])
            pt, pn = ps1()
            ptb = pt.bitcast(BF16)
            transposes([(ptb[0:Q, g * 128:(g + 1) * 128], XC[:, 8 + g, c0:c1], ident_b[:, :]) for g in range(2)],
                       ["XC", "ident_b"], pn)
            cp("act", Btm[0:Q, r2, :], ptb[0:Q, 0:256], pn, [f"Btm{r2}"])
            pt, pn = ps1()
            for g in range(2):
                mmgroup(pt[0:Q, g * 128:g * 128 + Q], [(XC[:, 8 + g, c0:c1], XC[:, 10 + g, c0:c1])], ["XC"], pn)
            tt("dve", CBm[0:Q, r2, :, 0:Q], pt[0:Q, 0:256].rearrange("p (g i) -> p g i", g=2)[:, :, 0:Q],
               mask_le[0:Q, 0:Q].unsqueeze(1).to_broadcast([Q, 2, Q]), ALU.mult, pn + ["mask_le"], [f"CBm{r2}"])


        def ssd_seg(b):
            for g in range(2):
                segv = segb[0:Q, g * 1024:g * 1024 + 8 * Q].rearrange("p (h i) -> p h i", h=8)
                tt("pool", segv, mask_le[0:Q, 0:Q].unsqueeze(1).to_broadcast([Q, 8, Q]),
                   a_t[0:Q, b, g * 8:(g + 1) * 8].unsqueeze(2).to_broadcast([Q, 8, Q]), ALU.mult,
                   ["mask_le", "a_t"], [f"segb{g}"])

        def ssd_s2(b):
            r2 = b % 2
            hpm = min(max(1, 512 // Q), 8)
            Mtv = Mt[0:Q, 0, 0:16 * Q].rearrange("p (h i) -> p h i", h=16)
            for g in range(2):
                for h0 in range(0, 8, hpm):
                    pt, pn = ps1()
                    mmgroup(pt[0:Q, 0:hpm * Q],
                            [(mask_gt_b[0:Q, 0:Q], segb[0:Q, g * 1024 + h0 * Q:g * 1024 + (h0 + hpm) * Q])],
                            ["mask_gt_b", f"segb{g}"], pn)
                    act(Mtv[:, g * 8 + h0:g * 8 + h0 + hpm, :],
                        pt[0:Q, 0:hpm * Q].rearrange("p (h i) -> p h i", h=hpm), AF.Exp, pn, [f"Mtg{g}"])
                tt("dve", Mtv[:, g * 8:(g + 1) * 8, :], Mtv[:, g * 8:(g + 1) * 8, :],
                   CBm[0:Q, r2, g, 0:Q].unsqueeze(1).to_broadcast([Q, 8, Q]), ALU.mult,
                   [f"Mtg{g}", f"CBm{r2}"], [f"Mtg{g}"])

        def ssd_s3(b):
            r2 = b % 2
            c0, c1 = b * Q, (b + 1) * Q
            first_blk = isp and t["first"] and b == 0
            eacs = eall[0:Q, r2, 0:16]
            cd = eall[:, r2, 32:48]
            Mtv = Mt[0:Q, 0, 0:16 * Q].rearrange("p (h i) -> p h i", h=16)
            if not isp:
                dma(hT[:, :], sstate[b], (), ["hT"], stsem)
                cp("act", hTb[:, :], hT[:, :], ["hT"], ["hTb"])
            for g in range(2):
                py, pyn = ps1()
                def fn_yd(e, py=py, Mtv=Mtv, Q=Q, g=g):
                    ins = None
                    for hh in range(8):
                        h = g * 8 + hh
                        e.matmul(py[0:Q, hh * 64:(hh + 1) * 64], lhsT=Mtv[:, h, :], rhs=xdt[0:Q, 0, h * 64:(h + 1) * 64],
                                 start=True, stop=False)
                        ins = e.matmul(py[0:Q, hh * 64:(hh + 1) * 64], lhsT=Did[0:Q, h, 0:Q],
                                       rhs=xD[0:Q, 0, h * 64:(h + 1) * 64], start=False, stop=True)
                    return ins
                P.add("pe", fn_yd, [f"Mtg{g}", "xdt", "xD", "Did"], pyn)
                ysl_ = ytm[0:Q, 0, g * 512:(g + 1) * 512]
                zsl_ = zs[0:Q, b, g * 512:(g + 1) * 512]
                if not first_blk:
                    po, pon = ps1()
                    mmgroup(po[0:Q, :], [(XC[:, 10 + g, c0:c1], hTb[:, g * 512:(g + 1) * 512])], ["XC", "hTb"], pon)
                    tt("dve", ysl_.rearrange("p (h d) -> p h d", h=8), po[0:Q, :].rearrange("p (h d) -> p h d", h=8),
                       eacs[:, g * 8:(g + 1) * 8].unsqueeze(2).to_broadcast([Q, 8, 64]), ALU.mult,
                       pon + [f"eall{r2}"], [f"ytm{g}"])
                    tt("dve", ysl_, ysl_, py[0:Q, :], ALU.add, pyn + [f"ytm{g}"], [f"ytm{g}"])
                    tt("dve", ysl_, ysl_, zsl_, ALU.mult, [f"ytm{g}", f"zs{b}"], [f"ytm{g}"])
                else:
                    tt("dve", ysl_, py[0:Q, :], zsl_, ALU.mult, pyn + [f"zs{b}"], [f"ytm{g}"])
            pst, pstn = ps2()
            def fn_st(e, pst=pst, r2=r2, Q=Q):
                ins = None
                for g in range(2):
                    ins = e.matmul(pst[:, g * 512:(g + 1) * 512], lhsT=Btm[0:Q, r2, g * 128:(g + 1) * 128],
                                   rhs=xdtd[0:Q, 0, g * 512:(g + 1) * 512], start=True, stop=True)
                return ins
            P.add("pe", fn_st, [f"Btm{r2}", "xdtd"], pstn)
            if first_blk:
                cp("act", hT[:, :], pst[:, :], pstn, ["hT"])
            else:
                tt("pool", hT[:, :].rearrange("p (h d) -> p h d", h=16), hT[:, :].rearrange("p (h d) -> p h d", h=16),
                   cd.unsqueeze(2).to_broadcast([128, 16, 64]), ALU.mult, ["hT", f"eall{r2}"], ["hT"])
                tt("dve", hT[:, :], hT[:, :], pst[:, :], ALU.add, pstn + ["hT"], ["hT"])
            last_blk_of_seq = (not isp) or (t["last"] and b == NB - 1)
            if not last_blk_of_seq:
                cp("act", hTb[:, :], hT[:, :], ["hT"], ["hTb"])
            else:
                sq = t["seq"] if isp else b
                dst = (nssm_p if isp else nssm_s)[sq].rearrange("(c p) n -> p c n", p=128)
                pt2, pn2 = ps2()
                transposes([(pt2[:, c * 128:(c + 1) * 128], hT[:, c * 128:(c + 1) * 128], ident_f[:, :]) for c in range(8)],
                           ["hT", "ident_f"], pn2)
                cp("act", ostage[:, :], pt2[:, :], pn2, ["ostage"])
                dma(dst, ostage[:, :].rearrange("p (c n) -> p c n", c=8), ["ostage"], [], osem, is_out=True)

        def ssd_s4(b):
            r2 = b % 2
            for g in range(2):
                col = r2 * 2 + g
                ysl_ = ytm[0:Q, 0, g * 512:(g + 1) * 512]
                nsl_ = ynb[0:Q, 0, g * 512:(g + 1) * 512]
                act(nsl_, ysl_, AF.Square, [f"ytm{g}"], [f"ynb{g}", f"ssq{col}"], accum=ssq[0:Q, col:col + 1])
                act(lnt[0:Q, 12 + col:13 + col], ssq[0:Q, col:col + 1], AF.Ln, [f"ssq{col}", "epsb"], [f"lntq{col}"],
                    bias=epsb[0:Q, 0:1], scale=1.0 / 512.0)
                act(ssq[0:Q, col:col + 1], lnt[0:Q, 12 + col:13 + col], AF.Exp, [f"lntq{col}"], [f"ssq{col}"], scale=-0.5)
                act(nsl_, ysl_, AF.Identity, [f"ytm{g}", f"ssq{col}"], [f"ynb{g}"], scale=ssq[0:Q, col:col + 1])

        def ssd_s4b(b):
            c0, c1 = b * Q, (b + 1) * Q
            for g in range(2):
                pt, pn = ps1()
                ptb = pt.bitcast(BF16)
                transposes([(ptb[:, k4 * Q:(k4 + 1) * Q], ynb[0:Q, 0, (g * 4 + k4) * 128:(g * 4 + k4 + 1) * 128],
                             ident_b[0:Q, 0:Q]) for k4 in range(4)], [f"ynb{g}", "ident_b"], pn)
                for k4 in range(4):
                    kc = g * 4 + k4
                    act(catT[:, 8 + kc, c0:c1], ptb[:, k4 * Q:(k4 + 1) * Q], AF.Identity, pn + ["vecs"], [f"cat{8 + kc}"],
                        scale=vecs[:, V_SNW + kc:V_SNW + kc + 1])

        hoist = True
        ssd_seg(0)
        ssd_s1(0)
        ssd_s2(0)
        if NB > 1 and hoist:
            ssd_seg(1)
        for b in range(NB):
            if t["idx"] == 0:
                cast_ff1(2 * b)
                cast_ff1(2 * b + 1)
            ssd_s3(b)
            if b + 1 < NB:
                ssd_s1(b + 1)
            ssd_s4(b)
            if b + 1 < NB:
                if not hoist:
                    ssd_seg(b + 1)
                ssd_s2(b + 1)
                if b + 2 < NB and hoist:
                    ssd_seg(b + 2)
            if b == NB - 1:
                wout_first_half()
            ssd_s4b(b)
        if t['stop'] <= 5:
            return
        if t["last"]:
            for sg in range(S):
                sq = t["seq"] if isp else sg
                pt2, pn2 = ps2()
                transposes([(pt2[0:15, kc * 128:(kc + 1) * 128], ulast[:, kc, sg, :], ident_f[:, :]) for kc in range(8)],
                           ["ulast", "ident_f"], pn2)
                cp("act", ostage[0:15, :], pt2[0:15, :], pn2, ["ostage"])
                dma((npool_p if isp else npool_s)[sq], ostage[0:15, :], ["ostage"], [], osem, is_out=True)
                for half in range(2):
                    pt2, pn2 = ps2()
                    transposes([(pt2[0:3, kc * 128:(kc + 1) * 128], xlast[:, half * 6 + kc, sg, :], ident_f[:, :])
                                for kc in range(6)], ["xlast", "ident_f"], pn2)
                    cp("act", ostage[0:3, 0:768], pt2[0:3, 0:768], pn2, ["ostage"])
                    dma((nconv_p if isp else nconv_s)[sq][:, half * 768:(half + 1) * 768], ostage[0:3, 0:768],
                        ["ostage"], [], osem, is_out=True)
        if t['stop'] <= 55:
            return
        catn = [f"cat{m}" for m in range(16)]
        for m in range(8):
            if t["idx"] == 0:
                cast_ff2(m)
            if m < NEARLY:
                slot, sn, pt, pn = wo_early[m]
                wv = slot[:, 0:2048].rearrange("p (k n) -> p k n", k=16)
                pairs = [(wv[:, kc, :], catT[:, kc, 0:T]) for kc in range(8, 16)]
                def fn2(e, pairs=pairs, pt=pt, T=T):
                    ins = None
                    for i, (l, r) in enumerate(pairs):
                        ins = e.matmul(pt[:, 0:T], lhsT=l, rhs=r, start=False, stop=(i == len(pairs) - 1))
                    return ins
                P.add("pe", fn2, [sn] + catn[8:], pn)
                ps_held.discard(int(pn[0][2:]))
            else:
                slot, sn = slab_get("wout")
                wv = slot[:, 0:2048].rearrange("p (k n) -> p k n", k=16)
                pt, pn = ps1()
                mmgroup(pt[:, 0:T], [(wv[:, kc, :], catT[:, kc, 0:T]) for kc in range(16)], [sn] + catn, pn)
            slab_done()
            if isp:
                si = t["seq"]
                act(fT[:, m, 0:T], pt[:, 0:T], AF.Identity, pn + ["mod"], ["fT"], scale=modv(2, m, si))
            else:
                for sg in range(S):
                    act(fT[:, m, sg * L:(sg + 1) * L], pt[:, sg * L:(sg + 1) * L], AF.Identity, pn + ["mod"], ["fT"],
                        scale=modv(2, m, 2 + sg))
        for b in range(NB):
            pt2, pn2 = ps2()
            transposes([(pt2[0:Q, m * 128:(m + 1) * 128], fT[:, m, b * Q:(b + 1) * Q], ident_f[:, :]) for m in range(8)],
                       ["fT", "ident_f"], pn2)
            tt("dve", xres[0:Q, xs_[b], :], xres[0:Q, xs_[b], :], pt2[0:Q, :], ALU.add, pn2 + [f"x{xs_[b]}"], [f"x{xs_[b]}"])
        if t['stop'] <= 6:
            return
        norm_all(t, (wm2, "wm2"), 3)
        if t['stop'] <= 7:
            return
        for sl in range(8):
            slot, sn = slab_get("wff1")
            wv = slot[:, :].rearrange("p (k n) -> p k n", k=8)
            for j in range(4):
                m = sl * 4 + j
                pt, pn = ps1()
                mmgroup(pt[:, 0:T], [(wv[:, kc, j * 128:(j + 1) * 128], hTt[:, kc, 0:T]) for kc in range(8)],
                        [sn] + hTn, pn)
                act(aT[:, m, 0:T], pt[:, 0:T], AF.Relu, pn + ["vecs"], [f"aT{m}"], bias=vecs[:, V_B1 + m:V_B1 + m + 1])
                tt("pool" if m % 2 == 0 else "dve", aT[:, m, 0:T], aT[:, m, 0:T], aT[:, m, 0:T], ALU.mult,
                   [f"aT{m}"], [f"aT{m}"])
            slab_done()
        if t['stop'] <= 8:
            return
        nxt = t["next"]

        def st_n(b):
            if nxt is not None:
                norm_stats(nxt, b, (wm1, "wm1"), 0)

        def tr_n(b):
            if nxt is not None:
                norm_trev(nxt, b, (wm1, "wm1"), 0)

        if nxt is not None:
            for b_ in range(3):
                emit_xload(nxt, b_)
            Qn = nxt["Q"]
            if nxt["kind"] == "p":
                src3 = xp[nxt["seq"], nxt["ti"] * 512 + 384:nxt["ti"] * 512 + 512, :]
            else:
                src3 = xs[48:64, :]
            dma(ostg[0:Qn, :], src3, (), ["ostage"], x3sem)
            st_n(0)
            st_n(1)
        for m in range(8):
            if m == 2:
                tr_n(0)
                st_n(2)
            if m == 4:
                tr_n(1)
                if nxt is not None:
                    norm_stats(nxt, 3, (wm1, "wm1"), 0, alt=(ostg[0:nxt["Q"], :], "ostage"))
            if m == 6:
                tr_n(2)
            if m == 7:
                tr_n(3)
            slot, sn = slab_get("wff2")
            wv = slot[:, :].rearrange("p (k n) -> p k n", k=32)
            pt, pn = ps1()
            mmgroup(pt[:, 0:T], [(wv[:, kc, :], aT[:, kc, 0:T]) for kc in range(32)], [sn] + aTn, pn)
            slab_done()
            if isp:
                si = t["seq"]
                ts("dve", fT[:, m, 0:T], pt[:, 0:T], modv(5, m, si), g2b2[:, m, si:si + 1], ALU.mult, ALU.add,
                   pn + ["mod", "g2b2"], ["fT"])
            else:
                for sg in range(S):
                    si = 2 + sg
                    ts("dve", fT[:, m, sg * L:(sg + 1) * L], pt[:, sg * L:(sg + 1) * L], modv(5, m, si),
                       g2b2[:, m, si:si + 1], ALU.mult, ALU.add, pn + ["mod", "g2b2"], ["fT"])

        def tail_blk(b):
            pt2, pn2 = ps2()
            transposes([(pt2[0:Q, m * 128:(m + 1) * 128], fT[:, m, b * Q:(b + 1) * Q], ident_f[:, :]) for m in range(8)],
                       ["fT", "ident_f"], pn2)
            tt("dve", xres[0:Q, xs_[b], :], xres[0:Q, xs_[b], :], pt2[0:Q, :], ALU.add, pn2 + [f"x{xs_[b]}"], [f"x{xs_[b]}"])
            act(Mt[0:Q, 0, 0:1024], xres[0:Q, xs_[b], :], AF.Square, [f"x{xs_[b]}"],
                ["Mtg0", "Mtg1", f"ss{8 + b}"], accum=ss[0:Q, 8 + b:9 + b])
            rstd_from_ss(8 + b, Q, 1024.0)
            ysl = 0
            stt("dve", yst[0:Q, ysl, :], xres[0:Q, xs_[b], :], ss[0:Q, 8 + b:9 + b], fnwb[0:Q, :], ALU.mult, ALU.mult,
                [f"x{xs_[b]}", f"ss{8 + b}", "fnwb"], ["yst"])
            if isp:
                dst = y_p[t["seq"], t["ti"] * 512 + b * 128:t["ti"] * 512 + (b + 1) * 128, :]
            else:
                dst = y_s[b * 16:(b + 1) * 16, :]
            dma(dst, yst[0:Q, ysl, :], ["yst"], [], ysem[ysl], is_out=True)
            if nxt is not None and b == 0:
                emit_xload(nxt, 3)

        tail_blk(0)
        tail_blk(1)
        tail_blk(2)
        tail_blk(3)

    for ti_, t in enumerate(tiles):
        t["idx"] = ti_
        t["xs"] = [(4 * ti_ + b) % 7 for b in range(4)]
        t["next"] = tiles[ti_ + 1] if ti_ + 1 < len(tiles) else None
    if cfg.stop > 0:
        for ti_, t in enumerate(tiles):
            t["stop"] = cfg.stop if ti_ >= cfg.stop_tile else 99
            emit_tile(t)
            if t["stop"] < 99:
                break
    if cfg.stop >= 99:
        assert slab_state["cur"] == len(slabs)
    dumpable = dict(mod=(mod, ["mod"]), wm1=(wm1, ["wm1"]), hTt=(hTt, ["hTt"]), xn=(xn, ["xn0", "xn1", "xn2", "xn3"]),
                    ss=(ss, ["ss"]), G1=(G1, ["U", "XBC", "PSCR0", "PSCR1", "pooled0", "pooled1"]), G2=(G2, ["XC", "zs0", "zs1", "zs2", "zs3"]),
                    catT=(catT, [f"cat{m}" for m in range(16)]), dtt=(dtt, ["dtt"]), a_t=(a_t, ["a_t"]),
                    xres=(xres, ["x0", "x1", "x2", "x3"]), hT=(hT, ["hT"]), ytm=(ytm, ["ytm"]), Mt=(Mt, ["Mt0g0", "Mt0g1", "Mt1g0", "Mt1g1"]),
                    eall=(eall, ["eall0", "eall1"]), diag=(diag, ["diag"]), mask_le=(mask_le, ["mask_le"]),
                    mask_gt=(mask_gt, ["mask_gt"]), cT=(cT, ["cT"]), Ab=(Ab, ["Ab"]), xdt=(xdt, ["xdt0", "xdt1"]),
                    CBm=(CBm, ["CBm0", "CBm1"]), ynb=(ynb, ["ynb0", "ynb1"]), xD=(xD, ["xD"]), Btm=(Btm, ["Btm0", "Btm1"]))
    for nm in cfg.dump:
        tns, pcs = dumpable[nm]
        shp = list(tns.shape)
        dd = nc.dram_tensor("dbg_" + nm, shp, tns.dtype, kind="ExternalOutput").ap()
        full = tuple(slice(None) for _ in shp)
        dma(dd[full], tns[full], pcs, [], sem("dbgs_" + nm), is_out=True)

    P.finalize(nc, stack)
    with nc.Block() as block:
        @block.tensor
        def _(e):
            P.emit_stream("pe", e)

        @block.scalar
        def _(e):
            P.emit_stream("act", e)

        @block.vector
        def _(e):
            P.emit_stream("dve", e)

        @block.gpsimd
        def _(e):
            P.emit_stream("pool", e)

        @block.sync
        def _(e):
            P.emit_stream("sp", e)
    stack.close()
    return nc


def _fm(v, nk):
    return np.ascontiguousarray(np.asarray(v, np.float32).reshape(nk, 128).T)


def prep_shared(inp):
    f = lambda a: np.ascontiguousarray(np.asarray(a, np.float32))
    w_ada = f(inp["w_ada"][0])
    w_in = f(inp["w_in"][0])
    sh = {}
    sh["w_ada_r"] = np.ascontiguousarray(w_ada.reshape(8, 128, 12, 512).transpose(2, 1, 0, 3)).reshape(12, 128, 4096)
    sh["w_in_r"] = np.ascontiguousarray(w_in[:, :3584].reshape(8, 128, 7, 512).transpose(2, 1, 0, 3)).reshape(7, 128, 4096)
    sh["w_dt_r"] = np.ascontiguousarray(w_in[:, 3584:].reshape(8, 128, 16).transpose(1, 0, 2)).reshape(128, 128)
    pw = f(inp["pool_w"][0])
    sh["pool_w_r"] = np.ascontiguousarray(pw.reshape(4, 2, 128, 256).transpose(2, 0, 1, 3)).reshape(128, 2048)
    wo = f(inp["w_out"][0])
    sh["w_out_r"] = np.ascontiguousarray(wo.reshape(16, 128, 8, 128).transpose(2, 1, 0, 3)).reshape(8, 128, 2048)
    w1 = f(inp["w_ff1"][0])
    sh["w_ff1_r"] = np.ascontiguousarray(w1.reshape(8, 128, 8, 512).transpose(2, 1, 0, 3)).reshape(8, 128, 4096)
    w2 = f(inp["w_ff2"][0])
    sh["w_ff2_r"] = np.ascontiguousarray(w2.reshape(32, 128, 8, 128).transpose(2, 1, 0, 3)).reshape(8, 128, 4096)
    vecs = np.zeros((128, NV), np.float32)
    vecs[:, V_NMW:V_NMW + 8] = _fm(inp["norm_mix_w"][0], 8)
    vecs[:, V_NFW:V_NFW + 8] = _fm(inp["norm_ffn_w"][0], 8)
    vecs[:, V_PB:V_PB + 8] = _fm(np.asarray(inp["pool_b"][0]).reshape(-1), 8)
    vecs[:, V_PS:V_PS + 8] = _fm(inp["pool_scale"][0], 8)
    cw = np.asarray(inp["conv_w"][0], np.float32)
    for k in range(4):
        vecs[:, V_CW + k * 12:V_CW + (k + 1) * 12] = _fm(cw[k], 12)
    vecs[:, V_CB:V_CB + 12] = _fm(inp["conv_b"][0], 12)
    vecs[:, V_B1:V_B1 + 32] = _fm(inp["b_ff1"][0], 32)
    vecs[:, V_B2:V_B2 + 8] = _fm(inp["b_ff2"][0], 8)
    vecs[:, V_BADA:V_BADA + 48] = _fm(inp["b_ada"][0], 48)
    vecs[:, V_SNW:V_SNW + 8] = _fm(inp["ssd_norm_w"][0], 8)
    sh["vecs"] = vecs
    rows = np.zeros((1, NR), np.float32)
    rows[0, R_FNW:R_FNW + 1024] = np.asarray(inp["final_norm_w"], np.float32)
    rows[0, R_SNW:R_SNW + 1024] = np.asarray(inp["ssd_norm_w"][0], np.float32)
    rows[0, R_DSK:R_DSK + 16] = np.asarray(inp["d_skip"][0], np.float32)
    rows[0, R_DTB:R_DTB + 16] = np.asarray(inp["dt_bias"][0], np.float32)
    rows[0, R_ALOG:R_ALOG + 16] = np.asarray(inp["a_log"][0], np.float32)
    sh["rows"] = rows
    return sh


def prep_core(inp, c):
    f = lambda a: np.ascontiguousarray(np.asarray(a, np.float32))
    m = {}
    m["xp"] = f(inp["x_prompt"][2 * c:2 * c + 2])
    m["xs"] = f(inp["x_sample"][4 * c:4 * c + 4]).reshape(64, 1024)
    cp_ = f(inp["cache_pool"][0, 4 * c:4 * c + 4])
    m["cpool"] = np.ascontiguousarray(cp_.reshape(4, 15, 8, 128).transpose(3, 2, 0, 1))
    cc = f(inp["cache_conv"][0, 4 * c:4 * c + 4])
    m["cconv"] = np.ascontiguousarray(cc.reshape(4, 3, 12, 128).transpose(3, 2, 0, 1))
    st = f(inp["state_ssm"][0, 4 * c:4 * c + 4])
    m["sstate"] = np.ascontiguousarray(st.reshape(4, 1024, 128).transpose(0, 2, 1))
    cvec = np.concatenate([f(inp["c_prompt"][2 * c:2 * c + 2]), f(inp["c_sample"][4 * c:4 * c + 4])], 0)
    m["cT"] = np.ascontiguousarray(cvec.reshape(6, 8, 128).transpose(2, 1, 0))
    return m


_NC_CACHE = {}


def kernel(**inputs):
    cfg = Cfg()
    if "nc" not in _NC_CACHE:
        _NC_CACHE["nc"] = build_program(cfg)
    nc = _NC_CACHE["nc"]
    sh = prep_shared(inputs)
    in_maps = []
    for c in range(NCORES):
        m = dict(sh)
        m.update(prep_core(inputs, c))
        in_maps.append(m)
    res = run_bass_kernel_spmd(nc, in_maps, core_ids=list(range(NCORES)))
    rs = res.results
    cat = lambda k: np.concatenate([np.asarray(r[k], np.float32) for r in rs], 0)
    y_prompt = cat("y_p")
    y_sample = cat("y_s").reshape(32, 16, 1024)
    npool_p = cat("npool_p")[None]
    nconv_p = cat("nconv_p")[None]
    nssm_p = cat("nssm_p").reshape(16, 16, 64, 128)[None]
    npool_s = cat("npool_s")[None]
    nconv_s = cat("nconv_s")[None]
    nssm_s = cat("nssm_s").reshape(32, 16, 64, 128)[None]
    return (y_prompt, y_sample, npool_p, nconv_p, nssm_p, npool_s, nconv_s, nssm_s)
```

```python
import numpy as np
from contextlib import ExitStack
import concourse.bass as bass
import concourse.mybir as mybir
from concourse.bass_utils import run_bass_kernel_spmd

F32 = mybir.dt.float32
BF16 = mybir.dt.bfloat16
AF = mybir.ActivationFunctionType
ALU = mybir.AluOpType

NCORES = 8
D = 1024
SEQ = 4096
EPS = 1e-6
SEM_LIM = 12000


class Op:
    __slots__ = ("eng", "fn", "deps", "needed", "cnt", "dma", "dsem", "dval", "name")

    def __init__(self, eng, fn, name=""):
        self.eng = eng
        self.fn = fn
        self.deps = []
        self.needed = False
        self.cnt = 0
        self.dma = False
        self.dsem = None
        self.dval = 0
        self.name = name


class Prog:
    ENGS = ("pe", "act", "dve", "pool", "sp")

    def __init__(self):
        self.streams = {e: [] for e in self.ENGS}
        self.lastw = {}
        self.readers = {}
        self.over = {}
        self.dma_cnt = {}
        self.out_dmas = []
        self.all_dma_sems = {}

    def alias(self, a_list, b_list):
        for a in a_list:
            for b in b_list:
                self.over.setdefault(a, set()).add(b)
                self.over.setdefault(b, set()).add(a)

    def _expand(self, p):
        o = self.over.get(p)
        if o:
            return [p] + list(o)
        return [p]

    def add(self, eng, fn, reads=(), writes=(), dma_sem=None, name="", is_out=False):
        o = Op(eng, fn, name)
        deps = {}

        def dep(d, kind):
            if d is None or d is o:
                return
            k = id(d)
            if k not in deps:
                deps[k] = (d, kind)
            elif kind == "raw":
                deps[k] = (d, kind)

        for p0 in reads:
            for p in self._expand(p0):
                dep(self.lastw.get(p), "raw")
            if p0.startswith("ps"):
                r = self.readers.get(p0)
                if r:
                    for d in r[0].values():
                        dep(d, "war")
        for p0 in writes:
            for p in self._expand(p0):
                dep(self.lastw.get(p), "waw")
                r = self.readers.get(p)
                if r:
                    for d in r[0].values():
                        dep(d, "war")
                    for d in r[1]:
                        dep(d, "war")
        o.deps = list(deps.values())
        for p in reads:
            r = self.readers.setdefault(p, [{}, []])
            if dma_sem is not None:
                r[1].append(o)
            else:
                r[0][eng] = o
        for p in writes:
            self.lastw[p] = o
            self.readers[p] = [{}, []]
        if dma_sem is not None:
            o.dma = True
            o.dsem = dma_sem
            c = self.dma_cnt.get(id(dma_sem), 0) + 1
            self.dma_cnt[id(dma_sem)] = c
            o.dval = 16 * c
            self.all_dma_sems[id(dma_sem)] = dma_sem
            if is_out:
                self.out_dmas.append(o)
        self.streams[eng].append(o)
        return o

    def finalize(self, nc, stack):
        for e in self.ENGS:
            for o in self.streams[e]:
                for d, kind in o.deps:
                    if d.dma:
                        continue
                    if d.eng == o.eng and o.eng == "pe":
                        continue
                    d.needed = True
        sems = {}
        for e in self.ENGS:
            c = 0
            for o in self.streams[e]:
                if o.needed and not o.dma:
                    c += 1
                    o.cnt = c
            nep = (c + SEM_LIM - 1) // SEM_LIM + 1
            sems[e] = [stack.enter_context(nc.semaphore(f"s_{e}_{i}")) for i in range(nep)]
        self.sems = sems
        return sems

    def emit_stream(self, e, eh):
        sems = self.sems
        done = {k: 0 for k in self.ENGS}
        dwait = {}
        for o in self.streams[e]:
            need = {}
            for d, kind in o.deps:
                if d.dma:
                    k = id(d.dsem)
                    if dwait.get(k, 0) < d.dval:
                        dwait[k] = d.dval
                        eh.wait_ge(d.dsem, d.dval)
                    continue
                if d.eng == o.eng and o.eng == "pe":
                    continue
                if d.cnt > need.get(d.eng, 0):
                    need[d.eng] = d.cnt
            for de, c in need.items():
                if c > done[de]:
                    done[de] = c
                    ep = (c - 1) // SEM_LIM
                    loc = (c - 1) % SEM_LIM + 1
                    eh.wait_ge(sems[de][ep], loc)
            ins = o.fn(eh)
            if o.dma:
                ins.then_inc(o.dsem, 16)
            elif o.needed:
                ep = (o.cnt - 1) // SEM_LIM
                ins.then_inc(sems[e][ep], 1)
        if e == "sp":
            for k, sm in self.all_dma_sems.items():
                eh.wait_ge(sm, self.dma_cnt[k] * 16)


class Cfg:
    n_pseq = 2
    n_ptiles = 8
    n_sseq = 4
    do_sample = True
    R = 4
    stop = 99
    stop_tile = 0
    dump = ()


V_NMW, V_NFW, V_PB, V_PS, V_CW, V_CB, V_B1, V_B2, V_BADA, V_SNW = 0, 8, 16, 24, 32, 80, 92, 124, 132, 180
NV = 188
R_FNW, R_SNW, R_DSK, R_DTB, R_ALOG = 0, 1024, 2048, 2064, 2080
NR = 2096


def build_program(cfg):
    nc = bass.Bass("TRN2", target_bir_lowering=False)
    P = Prog()
    stack = ExitStack()

    def din(name, shape, dt=F32):
        return nc.dram_tensor(name, list(shape), dt, kind="ExternalInput").ap()

    def dout(name, shape, dt=F32):
        return nc.dram_tensor(name, list(shape), dt, kind="ExternalOutput").ap()

    def dscr(name, shape, dt=BF16):
        return nc.dram_tensor(name, list(shape), dt, kind="Internal").ap()

    NPS, NSS = cfg.n_pseq, cfg.n_sseq
    NSEQ = NPS + NSS
    xp = din("xp", [2, SEQ, D])
    xs = din("xs", [64, D])
    cpool = din("cpool", [128, 8, 4, 15])
    cconv = din("cconv", [128, 12, 4, 3])
    sstate = din("sstate", [4, 128, 1024])
    cT_d = din("cT", [128, 8, 6])
    wada_d = din("w_ada_r", [12, 128, 8 * 512])
    win_d = din("w_in_r", [7, 128, 8 * 512])
    wdt_d = din("w_dt_r", [128, 8 * 16])
    poolw_d = din("pool_w_r", [128, 4 * 2 * 256])
    wout_d = din("w_out_r", [8, 128, 16 * 128])
    wff1_d = din("w_ff1_r", [8, 128, 8 * 512])
    wff2_d = din("w_ff2_r", [8, 128, 32 * 128])
    vecs_d = din("vecs", [128, NV])
    rows_d = din("rows", [1, NR])

    y_p = dout("y_p", [2, SEQ, D])
    y_s = dout("y_s", [64, D])
    npool_p = dout("npool_p", [2, 15, 1024])
    nconv_p = dout("nconv_p", [2, 3, 1536])
    nssm_p = dout("nssm_p", [2, 1024, 128])
    npool_s = dout("npool_s", [4, 15, 1024])
    nconv_s = dout("nconv_s", [4, 3, 1536])
    nssm_s = dout("nssm_s", [4, 1024, 128])

    win_b = dscr("win_b", [7, 128, 4096])
    wdt_b = dscr("wdt_b", [128, 128])
    poolw_b = dscr("poolw_b", [128, 2048])
    wout_b = dscr("wout_b", [8, 128, 2048])
    wff1_b = dscr("wff1_b", [8, 128, 4096])
    wff2_b = dscr("wff2_b", [8, 128, 4096])

    def sem(name):
        return stack.enter_context(nc.semaphore(name))

    def sb(name, shape, dt=F32):
        return nc.alloc_sbuf_tensor("sb_" + name, list(shape), dt)

    ident_f = sb("ident_f", [128, 128])
    ident_b = sb("ident_b", [128, 128], BF16)
    mask_le = sb("mask_le", [128, 128])
    mask_gt = sb("mask_gt", [128, 128])
    ones_f = sb("ones_f", [128, 128])
    epsb = sb("epsb", [128, 2])
    vecs = sb("vecs", [128, NV])
    fnwb = sb("fnwb", [128, 1024])
    dskb = sb("dskb", [128, 16])
    dtbb = sb("dtbb", [128, 16])
    Ab = sb("Ab", [128, 16])
    diag = sb("diag", [128, 48, 128], BF16)
    Did = sb("Did", [128, 16, 128], BF16)
    corr = sb("corr", [128, 4, 16])
    cT = sb("cT", [128, 8, 6])
    mod = sb("mod", [128, 48, 6])
    wm1 = sb("wm1", [128, 8, 6])
    wm2 = sb("wm2", [128, 8, 6])
    g2b2 = sb("g2b2", [128, 8, 6])
    hT = sb("hT", [128, 1024])
    hTb = sb("hTb", [128, 1024], BF16)
    xres = sb("xres", [128, 5, 1024])
    xn = sb("xn", [128, 2, 1024], BF16)
    ystage = xn[:, :, :].bitcast(F32) if False else None
    hTt = sb("hTt", [128, 8, 512], BF16)
    ss = sb("ss", [128, 16])
    catT = sb("catT", [128, 16, 512], BF16)
    G1 = sb("G1", [128, 18720], BF16)
    aT = G1[:, 0:16384].rearrange("p (k t) -> p k t", k=32)
    U_p = G1[:, 0:8 * 527].rearrange("p (k s t) -> p k s t", k=8, s=1)
    U_s = G1[:, 0:8 * 4 * 31].rearrange("p (k s t) -> p k s t", k=8, s=4)
    o1 = 8 * 527
    PSCR = G1[:, o1:o1 + 4 * 2 * 527].bitcast(F32)
    o2 = o1 + 4 * 2 * 527
    pooled = G1[:, o2:o2 + 4 * 2 * 512].rearrange("p (r k t) -> p r k t", r=4, k=2)
    o3 = o2 + 4096
    XBC_p = G1[:, o3:o3 + 12 * 515].rearrange("p (k s t) -> p k s t", k=12, s=1)
    XBC_s = G1[:, o3:o3 + 12 * 4 * 19].rearrange("p (k s t) -> p k s t", k=12, s=4)
    assert o3 + 12 * 515 <= 18720
    G2 = sb("G2", [128, 10240], BF16)
    XC = G2[:, 0:6144].rearrange("p (k t) -> p k t", k=12)
    zs = G2[:, 6144:10240].rearrange("p (b d) -> p b d", b=4)
    fT = G2[:, 0:8192].bitcast(F32).rearrange("p (k t) -> p k t", k=8)
    dtr = sb("dtr", [128, 4, 16])
    dtt = sb("dtt", [128, 4, 16])
    lnt = sb("lnt", [128, 16])
    uhist = sb("uhist", [128, 8, 1, 15], BF16)
    xhist = sb("xhist", [128, 12, 1, 3], BF16)
    a_t = sb("a_t", [128, 4, 16])
    eall = sb("eall", [128, 2, 48])
    xdt = sb("xdt", [128, 2, 1024], BF16)
    xdtd = sb("xdtd", [128, 2, 1024], BF16)
    xD = sb("xD", [128, 2, 1024], BF16)
    Btm = sb("Btm", [128, 2, 256], BF16)
    CBm = sb("CBm", [128, 1, 2, 128], BF16)
    ostg = sb("ostg", [128, 1024])
    segb = sb("segb", [128, 2048], BF16)
    mask_gt_b = sb("mask_gt_b", [128, 128], BF16)
    Mt = sb("Mt", [128, 2, 2048], BF16)
    ytm = sb("ytm", [128, 1, 1024])
    ynb = sb("ynb", [128, 1, 1024], BF16)
    ssq = sb("ssq", [128, 8])
    ulast = sb("ulast", [128, 8, 4, 15])
    xlast = sb("xlast", [128, 12, 4, 3])
    ostage = ostg[:, :]
    cstage = Mt[:, 0, 0:1440].bitcast(F32).rearrange("p (k s t) -> p k s t", k=12, s=4)
    R = cfg.R
    ring = [sb(f"ring{i}", [128, 4096], BF16) for i in range(R)]
    ring_sem = [sem(f"rs{i}") for i in range(R)]
    wada_st = [G1[:, 0:8192].bitcast(F32), G1[:, 8192:16384].bitcast(F32)]
    wada_sem = [sem("was0"), sem("was1")]

    aTn = [f"aT{m}" for m in range(32)]
    G1P = ["U", "PSCR0", "PSCR1", "pooled0", "pooled1", "pooled2", "pooled3", "XBC"]
    P.alias(aTn, G1P)
    P.alias(["fT"], ["XC", "zs0", "zs1", "zs2", "zs3"])
    P.alias(["ynbw"], ["ynb0", "ynb1"])
    P.alias(["Mt0g0", "Mt0g1"], ["cstage"])
    P.alias(["wst0", "wst1"], aTn + G1P)

    PS = [nc.alloc_psum_tensor(f"ps{i}", [128, 1024], F32) for i in range(4)]
    ps_state = {"i": 0}

    ps_held = set()

    def ps1():
        i = ps_state["i"]
        while i in ps_held:
            i = (i + 1) % 8
        ps_state["i"] = (i + 1) % 8
        t = PS[i // 2]
        h = i % 2
        return t[:, h * 512:(h + 1) * 512], [f"ps{i}"]

    def ps2():
        i = ps_state["i"]
        if i % 2:
            i = (i + 1) % 8
        while i in ps_held or (i + 1) in ps_held:
            i = (i + 2) % 8
        ps_state["i"] = (i + 2) % 8
        return PS[i // 2][:, :], [f"ps{i}", f"ps{i + 1}"]

    def act(out, in_, func, reads, writes, bias=None, scale=None, accum=None, name=""):
        kw = {}
        if bias is not None:
            kw["bias"] = bias
        if scale is not None:
            kw["scale"] = scale
        if accum is not None:
            kw["accum_out"] = accum
        return P.add("act", lambda e: e.activation(out=out, in_=in_, func=func, **kw), reads, writes, name=name)

    def tt(eng, out, in0, in1, op, reads, writes, name=""):
        return P.add(eng, lambda e: e.tensor_tensor(out=out, in0=in0, in1=in1, op=op), reads, writes, name=name)

    def ts(eng, out, in0, s1, s2, op0, op1, reads, writes, name=""):
        if op1 is None:
            return P.add(eng, lambda e: e.tensor_scalar(out=out, in0=in0, scalar1=s1, scalar2=None, op0=op0),
                         reads, writes, name=name)
        return P.add(eng, lambda e: e.tensor_scalar(out=out, in0=in0, scalar1=s1, scalar2=s2, op0=op0, op1=op1),
                     reads, writes, name=name)

    def stt(eng, out, in0, scalar, in1, op0, op1, reads, writes, name=""):
        return P.add(eng, lambda e: e.scalar_tensor_tensor(out=out, in0=in0, scalar=scalar, in1=in1, op0=op0, op1=op1),
                     reads, writes, name=name)

    def cp(eng, out, in_, reads, writes, name=""):
        if eng == "act":
            return act(out, in_, AF.Identity, reads, writes, name=name)
        return P.add(eng, lambda e: e.tensor_copy(out=out, in_=in_), reads, writes, name=name)

    def memset(eng, ap, val, writes):
        return P.add(eng, lambda e: e.memset(ap, val), (), writes)

    def dma(out, in_, reads, writes, s, eng="sp", is_out=False, name=""):
        return P.add(eng, lambda e: e.dma_start(out=out, in_=in_), reads, writes, dma_sem=s, is_out=is_out, name=name)

    def mmgroup(out, pairs, reads, writes, name=""):
        n = len(pairs)

        def fn(e):
            ins = None
            for i, (l, r) in enumerate(pairs):
                ins = e.matmul(out, lhsT=l, rhs=r, start=(i == 0), stop=(i == n - 1))
            return ins
        return P.add("pe", fn, reads, writes, name=name)

    def transposes(items, reads, writes, name=""):
        def fn(e):
            ins = None
            for (o, i, idn) in items:
                ins = e.transpose(out=o, in_=i, identity=idn)
            return ins
        return P.add("pe", fn, reads, writes, name=name)

    cs = {k: sem("cs_" + k) for k in ("win", "wdt", "poolw", "wout", "wff1", "wff2")}
    for s_ in range(7):
        dma(win_b[s_], win_d[s_], (), ["d_win"], cs["win"], eng="pool")
    dma(wdt_b[:, :], wdt_d[:, :], (), ["d_wdt"], cs["wdt"], eng="pool")
    dma(poolw_b[:, :], poolw_d[:, :], (), ["d_poolw"], cs["poolw"], eng="pool")
    def cast_wout():
        for s_ in range(8):
            dma(wout_b[s_], wout_d[s_], (), ["d_wout"], cs["wout"], eng="pool")

    def cast_ff1(s_):
        dma(wff1_b[s_], wff1_d[s_], (), ["d_wff1"], cs["wff1"], eng="pool")

    def cast_ff2(s_):
        dma(wff2_b[s_], wff2_d[s_], (), ["d_wff2"], cs["wff2"], eng="pool")

    misc = [sem(f"misc{i}") for i in range(6)]
    dma(vecs[:, :], vecs_d[:, :], (), ["vecs"], misc[0])
    dma(cT[:, :, :], cT_d[:, :, :], (), ["cT"], misc[1])
    dma(fnwb[:, :], rows_d[0:1, R_FNW:R_FNW + 1024].partition_broadcast(128), (), ["fnwb"], misc[2])
    dma(dskb[:, :], rows_d[0:1, R_DSK:R_DSK + 16].partition_broadcast(128), (), ["dskb"], misc[4])
    dma(dtbb[:, :], rows_d[0:1, R_DTB:R_DTB + 16].partition_broadcast(128), (), ["dtbb"], misc[5])
    alog_sem = sem("alog")
    dma(Ab[:, :], rows_d[0:1, R_ALOG:R_ALOG + 16].partition_broadcast(128), (), ["Ab"], alog_sem)

    memset("pool", ident_f[:, :], 0.0, ["ident_f"])
    P.add("pool", lambda e: e.affine_select(out=ident_f[:, :], in_=ident_f[:, :], compare_op=ALU.not_equal, fill=1.0,
                                            base=0, pattern=[[-1, 128]], channel_multiplier=1),
          ["ident_f"], ["ident_f"])
    cp("pool", ident_b[:, :], ident_f[:, :], ["ident_f"], ["ident_b"])
    memset("pool", ones_f[:, :], 1.0, ["ones_f"])
    memset("pool", epsb[:, :], EPS, ["epsb"])
    memset("pool", epsb[:, 1:2], 1.0, ["epsb"])
    P.add("pool", lambda e: e.affine_select(out=mask_le[:, :], in_=ones_f[:, :], compare_op=ALU.is_ge, fill=0.0,
                                            base=0, pattern=[[1, 128]], channel_multiplier=-1),
          ["ones_f"], ["mask_le"])
    P.add("pool", lambda e: e.affine_select(out=mask_gt[:, :], in_=ones_f[:, :], compare_op=ALU.is_gt, fill=0.0,
                                            base=0, pattern=[[-1, 128]], channel_multiplier=1),
          ["ones_f"], ["mask_gt"])
    cp("pool", mask_gt_b[:, :], mask_gt[:, :], ["mask_gt"], ["mask_gt_b"])
    for j in range(48):
        ts("pool", diag[:, j, :], ident_f[:, :], vecs[:, V_CW + j:V_CW + j + 1], None, ALU.mult, None,
           ["ident_f", "vecs"], ["diag"])
    for h_ in range(16):
        ts("pool", Did[:, h_, :], ident_f[:, :], dskb[:, h_:h_ + 1], None, ALU.mult, None,
           ["ident_f", "dskb"], ["Did"])
    memset("pool", corr[:, :, :], 1.0, ["corr"])
    for g, w in enumerate((2, 4, 8, 16)):
        for t in range(w - 1):
            memset("pool", corr[:, g, t:t + 1], float(w) / float(t + 1), ["corr"])
    act(Ab[:, :], Ab[:, :], AF.Exp, ["Ab"], ["Ab"])
    ts("dve", Ab[:, :], Ab[:, :], -1.0, None, ALU.mult, None, ["Ab"], ["Ab"])
    act(cT[:, :, :], cT[:, :, :], AF.Silu, ["cT"], ["cT"])

    xsem = [sem(f"xs{b}") for b in range(5)]
    for b_ in range(4):
        if cfg.n_pseq > 0 and cfg.n_ptiles > 0:
            dma(xres[0:128, b_, :], xp[0, b_ * 128:(b_ + 1) * 128, :], (), [f"x{b_}"], xsem[b_])
        else:
            dma(xres[0:16, b_, :], xs[b_ * 16:(b_ + 1) * 16, :], (), [f"x{b_}"], xsem[b_])
    modrow = ostg
    for sl in range(12):
        st = wada_st[sl % 2]
        stn = f"wst{sl % 2}"
        dma(st, wada_d[sl], (), [stn], wada_sem[sl % 2])
        stv = st.rearrange("p (k n) -> p k n", k=8)
        pt, pn = ps1()
        mmgroup(pt[0:6, :], [(cT[:, kc, :], stv[:, kc, :]) for kc in range(8)], [stn, "cT"], pn)
        cp("act", modrow[0:6, 0:512], pt[0:6, :], pn, ["ostage"])
        ptT, pnT = ps1()
        transposes([(ptT[:, j * 8:j * 8 + 6], modrow[0:6, j * 128:(j + 1) * 128], ident_f[0:6, 0:6]) for j in range(4)],
                   ["ostage", "ident_f"], pnT)
        tt("dve", mod[:, sl * 4:(sl + 1) * 4, :],
           ptT[:, 0:32].rearrange("p (j s) -> p j s", j=4)[:, :, 0:6],
           vecs[:, V_BADA + sl * 4:V_BADA + sl * 4 + 4].unsqueeze(2).to_broadcast([128, 4, 6]),
           ALU.add, pnT + ["vecs"], ["mod"])
    ts("dve", wm1[:, :, :], mod[:, 8:16, :], 1.0, None, ALU.add, None, ["mod"], ["wm1"])
    tt("dve", wm1[:, :, :], wm1[:, :, :], vecs[:, V_NMW:V_NMW + 8].unsqueeze(2).to_broadcast([128, 8, 6]), ALU.mult,
       ["wm1", "vecs"], ["wm1"])
    ts("dve", wm2[:, :, :], mod[:, 32:40, :], 1.0, None, ALU.add, None, ["mod"], ["wm2"])
    tt("dve", wm2[:, :, :], wm2[:, :, :], vecs[:, V_NFW:V_NFW + 8].unsqueeze(2).to_broadcast([128, 8, 6]), ALU.mult,
       ["wm2", "vecs"], ["wm2"])
    tt("dve", g2b2[:, :, :], mod[:, 40:48, :], vecs[:, V_B2:V_B2 + 8].unsqueeze(2).to_broadcast([128, 8, 6]), ALU.mult,
       ["mod", "vecs"], ["g2b2"])

    def modv(part, kc, si):
        return mod[:, part * 8 + kc, si:si + 1]

    tiles = []
    for sq in range(NPS):
        for ti in range(cfg.n_ptiles):
            tiles.append(dict(kind="p", seq=sq, ti=ti, S=1, L=512, Q=128, NB=4,
                              first=(ti == 0), last=(ti == cfg.n_ptiles - 1)))
    if cfg.do_sample:
        tiles.append(dict(kind="s", S=4, L=16, Q=16, NB=4, first=True, last=True))

    slabs = []
    for _t in tiles:
        for s_ in range(4):
            slabs.append(("win", win_b[s_], 4096, "d_win", cs["win"], 7))
        slabs.append(("wdt", wdt_b[:, :], 128, "d_wdt", cs["wdt"], 1))
        for s_ in range(4, 7):
            slabs.append(("win", win_b[s_], 4096, "d_win", cs["win"], 7))
        slabs.append(("poolw", poolw_b[:, :], 2048, "d_poolw", cs["poolw"], 1))
        for s_ in range(8):
            slabs.append(("wout", wout_b[s_], 2048, "d_wout", cs["wout"], 8))
        for s_ in range(8):
            slabs.append(("wff1", wff1_b[s_], 4096, "d_wff1", cs["wff1"], 8))
        for s_ in range(8):
            slabs.append(("wff2", wff2_b[s_], 4096, "d_wff2", cs["wff2"], 8))
    slab_state = {"issued": 0, "cur": 0}

    def slab_issue():
        j = slab_state["issued"]
        if j >= len(slabs):
            return
        kind, src, n, piece, _csem, _ncast = slabs[j]
        slot = j % R
        dma(ring[slot][:, 0:n], src, [piece], [f"ring{slot}"], ring_sem[slot], name=f"slab{j}")
        slab_state["issued"] = j + 1

    def slab_get(kind, ahead=0):
        j = slab_state["cur"] + ahead
        assert slabs[j][0] == kind, (slabs[j][0], kind)
        slot = j % R
        return ring[slot], f"ring{slot}"

    def slab_done():
        slab_state["cur"] += 1
        slab_issue()

    for _ in range(R):
        slab_issue()

    ysem = [sem("ys0"), sem("ys1")]
    osem = sem("osem")
    csem_ = sem("csem")
    stsem = sem("stsem")
    xstg_sem = {1: sem("xg1"), 2: sem("xg2"), 3: sem("xg3")}
    y_slot = {"i": 0}
    yst = xn[:, :, :].rearrange("p b d -> p (b d)").bitcast(F32).rearrange("p (r d) -> p r d", r=1)

    def rstd_from_ss(col, Q, n):
        act(lnt[0:Q, col:col + 1], ss[0:Q, col:col + 1], AF.Ln, [f"ss{col}", "epsb"], [f"lnt{col}"], bias=epsb[0:Q, 0:1], scale=1.0 / n)
        act(ss[0:Q, col:col + 1], lnt[0:Q, col:col + 1], AF.Exp, [f"lnt{col}"], [f"ss{col}"], scale=-0.5)

    def emit_xload(t, b):
        Q = t["Q"]
        if t["kind"] == "p":
            src = xp[t["seq"], t["ti"] * 512 + b * 128:t["ti"] * 512 + (b + 1) * 128, :]
        else:
            src = xs[b * 16:(b + 1) * 16, :]
        sl = t["xs"][b]
        dma(xres[0:Q, sl, :], src, (), [f"x{sl}"], xsem[sl])

    hTn = ["hTt0", "hTt1", "hTt2", "hTt3"]
    nbufs = [(xdt, "xdt0"), (xD, "xD0"), (xdtd, "xdtd0"), (ynb, "ynbw")]

    def norm_stats(t, b, wm, shpart, alt=None):
        S, L, Q, NB = t["S"], t["L"], t["Q"], t["NB"]
        xs_ = t["xs"]
        nb_, nbn = nbufs[b]
        if alt is None:
            xsrc, xpc = xres[0:Q, xs_[b], :], [f"x{xs_[b]}"]
        else:
            xsrc, xpc = alt
        act(nb_[0:Q, 0, :], xsrc, AF.Square, xpc, [nbn, f"ss{b}"], accum=ss[0:Q, b:b + 1])
        rstd_from_ss(b, Q, 1024.0)
        ts("dve", nb_[0:Q, 0, :], xsrc, ss[0:Q, b:b + 1], None, ALU.mult, None, xpc + [f"ss{b}"], [nbn])

    def norm_trev(t, b, wm, shpart):
        S, L, Q, NB = t["S"], t["L"], t["Q"], t["NB"]
        nb_, nbn = nbufs[b]
        pt2, pn2 = ps2()
        ptb = pt2.bitcast(BF16)
        for half in range(2):
            transposes([(ptb[:, half * 1024 + k4 * 128:half * 1024 + k4 * 128 + Q],
                         nb_[0:Q, 0, (half * 4 + k4) * 128:(half * 4 + k4 + 1) * 128], ident_b[0:Q, 0:Q])
                        for k4 in range(4)], [nbn, "ident_b"], [pn2[half]])
        si = t["seq"] if t["kind"] == "p" else 2 + b
        for kc in range(8):
            half, k4 = kc // 4, kc % 4
            src = ptb[:, half * 1024 + k4 * 128:half * 1024 + k4 * 128 + Q]
            dstp = hTt[:, kc, b * Q:(b + 1) * Q]
            if half == 0:
                act(dstp, src, AF.Identity, [pn2[half], wm[1], "mod"], [f"hTt{b}"],
                    bias=modv(shpart, kc, si), scale=wm[0][:, kc, si:si + 1])
            else:
                ts("dve", dstp, src, wm[0][:, kc, si:si + 1], modv(shpart, kc, si), ALU.mult, ALU.add,
                   [pn2[half], wm[1], "mod"], [f"hTt{b}"])

    def norm_all(t, wm, shpart):
        NB = t["NB"]
        norm_stats(t, 0, wm, shpart)
        for b in range(NB):
            if b + 1 < NB:
                norm_stats(t, b + 1, wm, shpart)
            norm_trev(t, b, wm, shpart)

    def emit_tile(t):
        S, L, Q, NB = t["S"], t["L"], t["Q"], t["NB"]
        T = S * L
        isp = t["kind"] == "p"
        xs_ = t["xs"]
        U = U_p if isp else U_s
        XBC = XBC_p if isp else XBC_s
        if isp and t["first"]:
            memset("pool", U[:, :, :, 0:15], 0.0, ["U"])
            memset("pool", XBC[:, :, :, 0:3], 0.0, ["XBC"])
        elif isp:
            cp("pool", U[:, :, :, 0:15], uhist[:, :, :, :], ["uhist"], ["U"])
            cp("pool", XBC[:, :, :, 0:3], xhist[:, :, :, :], ["xhist"], ["XBC"])
        if not isp:
            dma(cstage[:, 0:8, :, :], cpool[:, :, :, :], (), ["cstage"], csem_)
            cp("pool", U[:, :, :, 0:15], cstage[:, 0:8, :, :], ["cstage"], ["U"])
            dma(cstage[:, :, :, 0:3], cconv[:, :, :, :], (), ["cstage"], csem_)
            cp("pool", XBC[:, :, :, 0:3], cstage[:, :, :, 0:3], ["cstage"], ["XBC"])
        if t["idx"] == 0:
            norm_all(t, (wm1, "wm1"), 0)
        if t['stop'] <= 1:
            return

        def pool_elem(g):
            w = (2, 4, 8, 16)[g]
            W = 15 + L
            ug = U[:, 2 * g:2 * g + 2, :, :]
            sc = [PSCR[:, i * (2 * S * W):(i + 1) * (2 * S * W)].rearrange("p (k s t) -> p k s t", k=2, s=S)
                  for i in range(2)]
            cur, curn = ug, "U"
            sh = 1
            nb = 0
            while sh < w:
                dst, dn = sc[nb % 2], f"PSCR{nb % 2}"
                tt("dve", dst[:, :, :, sh:W], cur[:, :, :, sh:W], cur[:, :, :, 0:W - sh], ALU.add, [curn], [dn])
                cur, curn = dst, dn
                sh *= 2
                nb += 1
            if isp and t["first"]:
                for k2 in range(2):
                    tt("dve", cur[:, k2, 0, 15:31], cur[:, k2, 0, 15:31], corr[:, g, :], ALU.mult,
                       [curn, "corr"], [curn])
            stt("dve", pooled[:, g, :, 0:T].rearrange("p k (s l) -> p k s l", s=S), cur[:, :, :, 15:W], 1.0 / w,
                ug[:, :, :, 15:W], ALU.mult, ALU.subtract, [curn, "U"], [f"pooled{g}"])
        for sl in range(2):
            slot, sn = slab_get("win")
            wv = slot[:, :].rearrange("p (k n) -> p k n", k=8)
            for j in range(4):
                m = sl * 4 + j
                pt, pn = ps1()
                mmgroup(pt[:, 0:T], [(wv[:, kc, j * 128:(j + 1) * 128], hTt[:, kc, 0:T]) for kc in range(8)],
                        [sn] + hTn, pn)
                src = pt[:, 0:T].rearrange("p (s l) -> p s l", s=S)
                import os
                if not os.environ.get("DBG_NOEVAC"):
                    cp("act", U[:, m, :, 15:15 + L], src, pn, ["U"])
                if t["last"] and not os.environ.get("DBG_NOULAST"):
                    if os.environ.get("DBG_ULV") == "1":
                        cp("dve", ulast[:, m, 0, :], pt[:, T - 15:T], pn, ["ulast"])
                    elif os.environ.get("DBG_ULV") == "2":
                        cp("dve", ulast[:, m, 0:S, 0:8], src[:, :, L - 8:L], pn, ["ulast"])
                    elif os.environ.get("DBG_ULV") == "3":
                        cp("dve", ulast[:, m, 0:S, :], src[:, :, 0:15], pn, ["ulast"])
                    else:
                        cp(os.environ.get("DBG_ULENG", "dve"), ulast[:, m, 0:S, :], src[:, :, L - 15:L], pn, ["ulast"])
            slab_done()
            pool_elem(2 * sl)
            pool_elem(2 * sl + 1)
            if t["idx"] == 0 and sl == 1:
                cast_wout()

        if isp and not t["last"]:
            cp("pool", uhist[:, :, :, :], U[:, :, :, L:L + 15], ["U"], ["uhist"])
        if t['stop'] <= 1.2:
            return
        for sl in range(2):
            slot, sn = slab_get("win")
            wv = slot[:, :].rearrange("p (k n) -> p k n", k=8)
            for b in range(NB):
                pt, pn = ps1()
                mmgroup(pt[0:Q, :], [(hTt[:, kc, b * Q:(b + 1) * Q], wv[:, kc, :]) for kc in range(8)],
                        [sn] + hTn, pn)
                act(zs[0:Q, b, sl * 512:(sl + 1) * 512], pt[0:Q, :], AF.Silu, pn, [f"zs{b}"])
            slab_done()
        slot, sn = slab_get("wdt")
        wv = slot[:, 0:128].rearrange("p (k n) -> p k n", k=8)
        pt, pn = ps1()
        for b in range(NB):
            mmgroup(pt[0:Q, b * 16:(b + 1) * 16], [(hTt[:, kc, b * Q:(b + 1) * Q], wv[:, kc, :]) for kc in range(8)],
                    [sn] + hTn, pn)
        slab_done()
        ptv = pt[0:Q, 0:64].rearrange("p (b h) -> p b h", b=4)
        tt("dve", dtr[0:Q, :, :], ptv, dtbb[0:Q, :].unsqueeze(1).to_broadcast([Q, 4, 16]), ALU.add,
           pn + ["dtbb"], ["dtr"])
        act(dtt[0:Q, :, :], dtr[0:Q, :, :], AF.Exp, ["dtr"], ["dtt"])
        act(dtr[0:Q, :, :], dtt[0:Q, :, :], AF.Ln, ["dtt", "epsb"], ["dtr"], bias=epsb[0:Q, 1:2])
        tt("dve", a_t[0:Q, :, :], dtr[0:Q, :, :], Ab[0:Q, :].unsqueeze(1).to_broadcast([Q, 4, 16]), ALU.mult,
           ["dtr", "Ab"], ["a_t"])
        if t['stop'] <= 1.4:
            return
        for sl in range(3):
            slot, sn = slab_get("win")
            wv = slot[:, :].rearrange("p (k n) -> p k n", k=8)
            for j in range(4):
                m = sl * 4 + j
                pt, pn = ps1()
                mmgroup(pt[:, 0:T], [(wv[:, kc, j * 128:(j + 1) * 128], hTt[:, kc, 0:T]) for kc in range(8)],
                        [sn] + hTn, pn)
                src = pt[:, 0:T].rearrange("p (s l) -> p s l", s=S)
                cp("act", XBC[:, m, :, 3:3 + L], src, pn, ["XBC"])
                if t["last"]:
                    cp("dve", xlast[:, m, 0:S, :], src[:, :, L - 3:L], pn, ["xlast"])
            slab_done()
        if t['stop'] <= 1.6:
            return
        if t['stop'] <= 2:
            return
        slot, sn = slab_get("poolw")
        pwv = slot[:, 0:2048].rearrange("p (g k d) -> p g k d", g=4, k=2)
        for g in range(4):
            for m2 in range(2):
                pt, pn = ps1()
                mmgroup(pt[:, 0:T], [(pwv[:, g, k2, m2 * 128:(m2 + 1) * 128], pooled[:, g, k2, 0:T]) for k2 in range(2)],
                        [sn, f"pooled{g}"], pn)
                m = 2 * g + m2
                ts("dve", catT[:, m, 0:T], pt[:, 0:T], vecs[:, V_PB + m:V_PB + m + 1], vecs[:, V_PS + m:V_PS + m + 1],
                   ALU.add, ALU.mult, pn + ["vecs"], [f"cat{m}"])
        slab_done()
        if t['stop'] <= 3:
            return
        for m in range(12):
            pt, pn = ps1()
            mmgroup(pt[:, 0:T].rearrange("p (s l) -> p s l", s=S) if S > 1 else pt[:, 0:T],
                    [(diag[:, k * 12 + m, :], (XBC[:, m, :, k:k + L] if S > 1 else XBC[:, m, 0, k:k + L]))
                     for k in range(4)],
                    ["diag", "XBC"], pn)
            act(XC[:, m, 0:T], pt[:, 0:T], AF.Silu, pn + ["vecs"], ["XC"], bias=vecs[:, V_CB + m:V_CB + m + 1])
        if isp and not t["last"]:
            cp("pool", xhist[:, :, :, :], XBC[:, :, :, L:L + 3], ["XBC"], ["xhist"])
        if t['stop'] <= 4:
            return
        NEARLY = 3
        wo_early = []

        def wout_first_half():
            for m in range(NEARLY):
                slot, sn = slab_get("wout", ahead=m)
                wv = slot[:, 0:2048].rearrange("p (k n) -> p k n", k=16)
                pt, pn = ps1()
                pairs = [(wv[:, kc, :], catT[:, kc, 0:T]) for kc in range(8)]
                def fn1(e, pairs=pairs, pt=pt, T=T):
                    ins = None
                    for i, (l, r) in enumerate(pairs):
                        ins = e.matmul(pt[:, 0:T], lhsT=l, rhs=r, start=(i == 0), stop=False)
                    return ins
                P.add("pe", fn1, [sn] + [f"cat{k}" for k in range(8)], pn)
                wo_early.append((slot, sn, pt, pn))
                ps_held.add(int(pn[0][2:]))

        def ssd_s1(b):
            r2 = b % 2
            c0, c1 = b * Q, (b + 1) * Q
            pt, pn = ps1()
            mmgroup(pt[0:Q, 0:16], [(mask_le[0:Q, 0:Q], a_t[0:Q, b, :])], ["mask_le", "a_t"], pn)
            mmgroup(pt[0:Q, 16:32], [(mask_gt[0:Q, 0:Q], a_t[0:Q, b, :])], ["mask_gt", "a_t"], pn)
            mmgroup(pt[:, 32:48], [(ones_f[0:Q, :], a_t[0:Q, b, :])], ["ones_f", "a_t"], pn)
            act(eall[0:Q, r2, 0:32], pt[0:Q, 0:32], AF.Exp, pn, [f"eall{r2}"])
            act(eall[:, r2, 32:48], pt[:, 32:48], AF.Exp, pn, [f"eall{r2}"])
            edec = eall[0:Q, r2, 16:32]
            pt, pn = ps1()
            ptb = pt.bitcast(BF16)
            transposes([(ptb[0:Q, kc * 128:(kc + 1) * 128], XC[:, kc, c0:c1], ident_b[:, :]) for kc in range(8)],
                       ["XC", "ident_b"], pn)
            pxv = ptb[0:Q, 0:1024].rearrange("p (h d) -> p h d", h=16)
            tt("dve", xdt[0:Q, r2, :].rearrange("p (h d) -> p h d", h=16), pxv,
               dtr[0:Q, b, :].unsqueeze(2).to_broadcast([Q, 16, 64]), ALU.mult, pn + ["dtr"], [f"xdt{r2}"])
            cp("act", xD[0:Q, r2, :], ptb[0:Q, 0:1024], pn, [f"xD{r2}"])
            tt("pool", xdtd[0:Q, r2, :].rearrange("p (h d) -> p h d", h=16),
               xdt[0:Q, r2, :].rearrange("p (h d) -> p h d", h=16),
               edec.unsqueeze(2).to_broadcast([Q, 16, 64]), ALU.mult, [f"xdt{r2}", f"eall{r2}"], [f"xdtd{r2}"])
            pt, pn = ps1()
            ptb = pt.bitcast(BF16)
            transposes([(ptb[0:Q, g * 128:(g + 1) * 128], XC[:, 8 + g, c0:c1], ident_b[:, :]) for g in range(2)],
                       ["XC", "ident_b"], pn)
            cp("act", Btm[0:Q, r2, :], ptb[0:Q, 0:256], pn, [f"Btm{r2}"])
            pt, pn = ps1()
            for g in range(2):
                mmgroup(pt[0:Q, g * 128:g * 128 + Q], [(XC[:, 8 + g, c0:c1], XC[:, 10 + g, c0:c1])], ["XC"], pn)
            tt("dve", CBm[0:Q, 0, :, 0:Q], pt[0:Q, 0:256].rearrange("p (g i) -> p g i", g=2)[:, :, 0:Q],
               mask_le[0:Q, 0:Q].unsqueeze(1).to_broadcast([Q, 2, Q]), ALU.mult, pn + ["mask_le"], ["CBm"])


        def ssd_seg(b):
            for g in range(2):
                segv = segb[0:Q, g * 1024:g * 1024 + 8 * Q].rearrange("p (h i) -> p h i", h=8)
                tt("pool", segv, mask_le[0:Q, 0:Q].unsqueeze(1).to_broadcast([Q, 8, Q]),
                   a_t[0:Q, b, g * 8:(g + 1) * 8].unsqueeze(2).to_broadcast([Q, 8, Q]), ALU.mult,
                   ["mask_le", "a_t"], [f"segb{g}"])

        def ssd_s2(b):
            r2 = b % 2
            hpm = min(max(1, 512 // Q), 8)
            Mtv = Mt[0:Q, r2, 0:16 * Q].rearrange("p (h i) -> p h i", h=16)
            for g in range(2):
                for h0 in range(0, 8, hpm):
                    pt, pn = ps1()
                    mmgroup(pt[0:Q, 0:hpm * Q],
                            [(mask_gt_b[0:Q, 0:Q], segb[0:Q, g * 1024 + h0 * Q:g * 1024 + (h0 + hpm) * Q])],
                            ["mask_gt_b", f"segb{g}"], pn)
                    act(Mtv[:, g * 8 + h0:g * 8 + h0 + hpm, :],
                        pt[0:Q, 0:hpm * Q].rearrange("p (h i) -> p h i", h=hpm), AF.Exp, pn, [f"Mt{r2}g{g}"])
                tt("dve", Mtv[:, g * 8:(g + 1) * 8, :], Mtv[:, g * 8:(g + 1) * 8, :],
                   CBm[0:Q, 0, g, 0:Q].unsqueeze(1).to_broadcast([Q, 8, Q]), ALU.mult,
                   [f"Mt{r2}g{g}", "CBm"], [f"Mt{r2}g{g}"])

        def ssd_s3(b):
            r2 = b % 2
            c0, c1 = b * Q, (b + 1) * Q
            first_blk = isp and t["first"] and b == 0
            eacs = eall[0:Q, r2, 0:16]
            cd = eall[:, r2, 32:48]
            Mtv = Mt[0:Q, r2, 0:16 * Q].rearrange("p (h i) -> p h i", h=16)
            if not isp:
                dma(hT[:, :], sstate[b], (), ["hT"], stsem)
                cp("act", hTb[:, :], hT[:, :], ["hT"], ["hTb"])
            for g in range(2):
                py, pyn = ps1()
                def fn_yd(e, py=py, Mtv=Mtv, Q=Q, g=g, r2=r2):
                    ins = None
                    for hh in range(8):
                        h = g * 8 + hh
                        e.matmul(py[0:Q, hh * 64:(hh + 1) * 64], lhsT=Mtv[:, h, :], rhs=xdt[0:Q, r2, h * 64:(h + 1) * 64],
                                 start=True, stop=False)
                        ins = e.matmul(py[0:Q, hh * 64:(hh + 1) * 64], lhsT=Did[0:Q, h, 0:Q],
                                       rhs=xD[0:Q, r2, h * 64:(h + 1) * 64], start=False, stop=True)
                    return ins
                P.add("pe", fn_yd, [f"Mt{r2}g{g}", f"xdt{r2}", f"xD{r2}", "Did"], pyn)
                ysl_ = ytm[0:Q, 0, g * 512:(g + 1) * 512]
                zsl_ = zs[0:Q, b, g * 512:(g + 1) * 512]
                if not first_blk:
                    po, pon = ps1()
                    mmgroup(po[0:Q, :], [(XC[:, 10 + g, c0:c1], hTb[:, g * 512:(g + 1) * 512])], ["XC", "hTb"], pon)
                    tt("dve", ysl_.rearrange("p (h d) -> p h d", h=8), po[0:Q, :].rearrange("p (h d) -> p h d", h=8),
                       eacs[:, g * 8:(g + 1) * 8].unsqueeze(2).to_broadcast([Q, 8, 64]), ALU.mult,
                       pon + [f"eall{r2}"], [f"ytm{g}"])
                    tt("dve", ysl_, ysl_, py[0:Q, :], ALU.add, pyn + [f"ytm{g}"], [f"ytm{g}"])
                    tt("dve", ysl_, ysl_, zsl_, ALU.mult, [f"ytm{g}", f"zs{b}"], [f"ytm{g}"])
                else:
                    tt("dve", ysl_, py[0:Q, :], zsl_, ALU.mult, pyn + [f"zs{b}"], [f"ytm{g}"])
            pst, pstn = ps2()
            def fn_st(e, pst=pst, r2=r2, Q=Q):
                ins = None
                for g in range(2):
                    ins = e.matmul(pst[:, g * 512:(g + 1) * 512], lhsT=Btm[0:Q, r2, g * 128:(g + 1) * 128],
                                   rhs=xdtd[0:Q, r2, g * 512:(g + 1) * 512], start=True, stop=True)
                return ins
            P.add("pe", fn_st, [f"Btm{r2}", f"xdtd{r2}"], pstn)
            if first_blk:
                cp("act", hT[:, :], pst[:, :], pstn, ["hT"])
            else:
                tt("pool", hT[:, :].rearrange("p (h d) -> p h d", h=16), hT[:, :].rearrange("p (h d) -> p h d", h=16),
                   cd.unsqueeze(2).to_broadcast([128, 16, 64]), ALU.mult, ["hT", f"eall{r2}"], ["hT"])
                tt("dve", hT[:, :], hT[:, :], pst[:, :], ALU.add, pstn + ["hT"], ["hT"])
            last_blk_of_seq = (not isp) or (t["last"] and b == NB - 1)
            if not last_blk_of_seq:
                cp("act", hTb[:, :], hT[:, :], ["hT"], ["hTb"])
            else:
                sq = t["seq"] if isp else b
                dst = (nssm_p if isp else nssm_s)[sq].rearrange("(c p) n -> p c n", p=128)
                pt2, pn2 = ps2()
                transposes([(pt2[:, c * 128:(c + 1) * 128], hT[:, c * 128:(c + 1) * 128], ident_f[:, :]) for c in range(8)],
                           ["hT", "ident_f"], pn2)
                cp("act", ostage[:, :], pt2[:, :], pn2, ["ostage"])
                dma(dst, ostage[:, :].rearrange("p (c n) -> p c n", c=8), ["ostage"], [], osem, is_out=True)

        def ssd_s4(b):
            r2 = b % 2
            for g in range(2):
                col = r2 * 2 + g
                ysl_ = ytm[0:Q, 0, g * 512:(g + 1) * 512]
                nsl_ = ynb[0:Q, 0, g * 512:(g + 1) * 512]
                act(nsl_, ysl_, AF.Square, [f"ytm{g}"], [f"ynb{g}", f"ssq{col}"], accum=ssq[0:Q, col:col + 1])
                act(lnt[0:Q, 12 + col:13 + col], ssq[0:Q, col:col + 1], AF.Ln, [f"ssq{col}", "epsb"], [f"lntq{col}"],
                    bias=epsb[0:Q, 0:1], scale=1.0 / 512.0)
                act(ssq[0:Q, col:col + 1], lnt[0:Q, 12 + col:13 + col], AF.Exp, [f"lntq{col}"], [f"ssq{col}"], scale=-0.5)
                act(nsl_, ysl_, AF.Identity, [f"ytm{g}", f"ssq{col}"], [f"ynb{g}"], scale=ssq[0:Q, col:col + 1])

        def ssd_s4b(b):
            c0, c1 = b * Q, (b + 1) * Q
            for g in range(2):
                pt, pn = ps1()
                ptb = pt.bitcast(BF16)
                transposes([(ptb[:, k4 * Q:(k4 + 1) * Q], ynb[0:Q, 0, (g * 4 + k4) * 128:(g * 4 + k4 + 1) * 128],
                             ident_b[0:Q, 0:Q]) for k4 in range(4)], [f"ynb{g}", "ident_b"], pn)
                for k4 in range(4):
                    kc = g * 4 + k4
                    act(catT[:, 8 + kc, c0:c1], ptb[:, k4 * Q:(k4 + 1) * Q], AF.Identity, pn + ["vecs"], [f"cat{8 + kc}"],
                        scale=vecs[:, V_SNW + kc:V_SNW + kc + 1])

        ssd_seg(0)
        ssd_s1(0)
        ssd_s2(0)
        ssd_seg(1)
        ssd_s1(1)
        ssd_s2(1)
        for b in range(NB):
            if t["idx"] == 0:
                cast_ff1(2 * b)
                cast_ff1(2 * b + 1)
            ssd_s3(b)
            ssd_s4(b)
            if b + 2 < NB:
                ssd_seg(b + 2)
                ssd_s1(b + 2)
                ssd_s2(b + 2)
            if b == NB - 1:
                wout_first_half()
            ssd_s4b(b)
        if t['stop'] <= 5:
            return
        if t["last"]:
            for sg in range(S):
                sq = t["seq"] if isp else sg
                pt2, pn2 = ps2()
                transposes([(pt2[0:15, kc * 128:(kc + 1) * 128], ulast[:, kc, sg, :], ident_f[:, :]) for kc in range(8)],
                           ["ulast", "ident_f"], pn2)
                cp("act", ostage[0:15, :], pt2[0:15, :], pn2, ["ostage"])
                dma((npool_p if isp else npool_s)[sq], ostage[0:15, :], ["ostage"], [], osem, is_out=True)
                for half in range(2):
                    pt2, pn2 = ps2()
                    transposes([(pt2[0:3, kc * 128:(kc + 1) * 128], xlast[:, half * 6 + kc, sg, :], ident_f[:, :])
                                for kc in range(6)], ["xlast", "ident_f"], pn2)
                    cp("act", ostage[0:3, 0:768], pt2[0:3, 0:768], pn2, ["ostage"])
                    dma((nconv_p if isp else nconv_s)[sq][:, half * 768:(half + 1) * 768], ostage[0:3, 0:768],
                        ["ostage"], [], osem, is_out=True)
        if t['stop'] <= 55:
            return
        catn = [f"cat{m}" for m in range(16)]
        for m in range(8):
            if t["idx"] == 0:
                cast_ff2(m)
            if m < NEARLY:
                slot, sn, pt, pn = wo_early[m]
                wv = slot[:, 0:2048].rearrange("p (k n) -> p k n", k=16)
                pairs = [(wv[:, kc, :], catT[:, kc, 0:T]) for kc in range(8, 16)]
                def fn2(e, pairs=pairs, pt=pt, T=T):
                    ins = None
                    for i, (l, r) in enumerate(pairs):
                        ins = e.matmul(pt[:, 0:T], lhsT=l, rhs=r, start=False, stop=(i == len(pairs) - 1))
                    return ins
                P.add("pe", fn2, [sn] + catn[8:], pn)
                ps_held.discard(int(pn[0][2:]))
            else:
                slot, sn = slab_get("wout")
                wv = slot[:, 0:2048].rearrange("p (k n) -> p k n", k=16)
                pt, pn = ps1()
                mmgroup(pt[:, 0:T], [(wv[:, kc, :], catT[:, kc, 0:T]) for kc in range(16)], [sn] + catn, pn)
            slab_done()
            if isp:
                si = t["seq"]
                act(fT[:, m, 0:T], pt[:, 0:T], AF.Identity, pn + ["mod"], ["fT"], scale=modv(2, m, si))
            else:
                for sg in range(S):
                    act(fT[:, m, sg * L:(sg + 1) * L], pt[:, sg * L:(sg + 1) * L], AF.Identity, pn + ["mod"], ["fT"],
                        scale=modv(2, m, 2 + sg))
        for b in range(NB):
            pt2, pn2 = ps2()
            transposes([(pt2[0:Q, m * 128:(m + 1) * 128], fT[:, m, b * Q:(b + 1) * Q], ident_f[:, :]) for m in range(8)],
                       ["fT", "ident_f"], pn2)
            tt("dve", xres[0:Q, xs_[b], :], xres[0:Q, xs_[b], :], pt2[0:Q, :], ALU.add, pn2 + [f"x{xs_[b]}"], [f"x{xs_[b]}"])
        if t['stop'] <= 6:
            return
        norm_all(t, (wm2, "wm2"), 3)
        if t['stop'] <= 7:
            return
        for sl in range(8):
            slot, sn = slab_get("wff1")
            wv = slot[:, :].rearrange("p (k n) -> p k n", k=8)
            for j in range(4):
                m = sl * 4 + j
                pt, pn = ps1()
                mmgroup(pt[:, 0:T], [(wv[:, kc, j * 128:(j + 1) * 128], hTt[:, kc, 0:T]) for kc in range(8)],
                        [sn] + hTn, pn)
                act(aT[:, m, 0:T], pt[:, 0:T], AF.Relu, pn + ["vecs"], [f"aT{m}"], bias=vecs[:, V_B1 + m:V_B1 + m + 1])
                tt("pool" if m % 2 == 0 else "dve", aT[:, m, 0:T], aT[:, m, 0:T], aT[:, m, 0:T], ALU.mult,
                   [f"aT{m}"], [f"aT{m}"])
            slab_done()
        if t['stop'] <= 8:
            return
        nxt = t["next"]

        def st_n(b):
            if nxt is not None:
                norm_stats(nxt, b, (wm1, "wm1"), 0)

        def tr_n(b):
            if nxt is not None:
                norm_trev(nxt, b, (wm1, "wm1"), 0)

        stg_alt = {}
        if nxt is not None:
            emit_xload(nxt, 0)
            Qn = nxt["Q"]
            stg_bufs = {1: (segb[0:Qn, :].bitcast(F32), ["segb0", "segb1"]), 2: (ytm[0:Qn, 0, :], ["ytm0", "ytm1"]),
                        3: (ostg[0:Qn, :], ["ostage"])}
            for b_ in (1, 2, 3):
                if nxt["kind"] == "p":
                    srcb = xp[nxt["seq"], nxt["ti"] * 512 + b_ * 128:nxt["ti"] * 512 + (b_ + 1) * 128, :]
                else:
                    srcb = xs[b_ * 16:(b_ + 1) * 16, :]
                ap_, pcs_ = stg_bufs[b_]
                dma(ap_, srcb, (), pcs_, xstg_sem[b_])
                stg_alt[b_] = (ap_, pcs_)
            st_n(0)
            norm_stats(nxt, 1, (wm1, "wm1"), 0, alt=stg_alt[1])
        for m in range(8):
            if m == 2:
                tr_n(0)
                if nxt is not None:
                    norm_stats(nxt, 2, (wm1, "wm1"), 0, alt=stg_alt[2])
            if m == 4:
                tr_n(1)
                if nxt is not None:
                    norm_stats(nxt, 3, (wm1, "wm1"), 0, alt=stg_alt[3])
            if m == 6:
                tr_n(2)
            if m == 7:
                tr_n(3)
            slot, sn = slab_get("wff2")
            wv = slot[:, :].rearrange("p (k n) -> p k n", k=32)
            pt, pn = ps1()
            mmgroup(pt[:, 0:T], [(wv[:, kc, :], aT[:, kc, 0:T]) for kc in range(32)], [sn] + aTn, pn)
            slab_done()
            if isp:
                si = t["seq"]
                ts("dve", fT[:, m, 0:T], pt[:, 0:T], modv(5, m, si), g2b2[:, m, si:si + 1], ALU.mult, ALU.add,
                   pn + ["mod", "g2b2"], ["fT"])
            else:
                for sg in range(S):
                    si = 2 + sg
                    ts("dve", fT[:, m, sg * L:(sg + 1) * L], pt[:, sg * L:(sg + 1) * L], modv(5, m, si),
                       g2b2[:, m, si:si + 1], ALU.mult, ALU.add, pn + ["mod", "g2b2"], ["fT"])

        def tail_blk(b):
            pt2, pn2 = ps2()
            transposes([(pt2[0:Q, m * 128:(m + 1) * 128], fT[:, m, b * Q:(b + 1) * Q], ident_f[:, :]) for m in range(8)],
                       ["fT", "ident_f"], pn2)
            tt("dve", xres[0:Q, xs_[b], :], xres[0:Q, xs_[b], :], pt2[0:Q, :], ALU.add, pn2 + [f"x{xs_[b]}"], [f"x{xs_[b]}"])
            act(Mt[0:Q, 0, 0:1024], xres[0:Q, xs_[b], :], AF.Square, [f"x{xs_[b]}"],
                ["Mt0g0", "Mt0g1", f"ss{8 + b}"], accum=ss[0:Q, 8 + b:9 + b])
            rstd_from_ss(8 + b, Q, 1024.0)
            ysl = 0
            stt("dve", yst[0:Q, ysl, :], xres[0:Q, xs_[b], :], ss[0:Q, 8 + b:9 + b], fnwb[0:Q, :], ALU.mult, ALU.mult,
                [f"x{xs_[b]}", f"ss{8 + b}", "fnwb"], ["yst"])
            if isp:
                dst = y_p[t["seq"], t["ti"] * 512 + b * 128:t["ti"] * 512 + (b + 1) * 128, :]
            else:
                dst = y_s[b * 16:(b + 1) * 16, :]
            dma(dst, yst[0:Q, ysl, :], ["yst"], [], ysem[ysl], is_out=True)
            if nxt is not None and b + 1 < NB:
                emit_xload(nxt, b + 1)

        tail_blk(0)
        tail_blk(1)
        tail_blk(2)
        tail_blk(3)

    for ti_, t in enumerate(tiles):
        t["idx"] = ti_
        t["xs"] = [(4 * ti_ + b) % 5 for b in range(4)]
        t["next"] = tiles[ti_ + 1] if ti_ + 1 < len(tiles) else None
    if cfg.stop > 0:
        for ti_, t in enumerate(tiles):
            t["stop"] = cfg.stop if ti_ >= cfg.stop_tile else 99
            emit_tile(t)
            if t["stop"] < 99:
                break
    if cfg.stop >= 99:
        assert slab_state["cur"] == len(slabs)
    dumpable = dict(mod=(mod, ["mod"]), wm1=(wm1, ["wm1"]), hTt=(hTt, ["hTt"]), xn=(xn, ["xn0", "xn1", "xn2", "xn3"]),
                    ss=(ss, ["ss"]), G1=(G1, ["U", "XBC", "PSCR0", "PSCR1", "pooled0", "pooled1"]), G2=(G2, ["XC", "zs0", "zs1", "zs2", "zs3"]),
                    catT=(catT, [f"cat{m}" for m in range(16)]), dtt=(dtt, ["dtt"]), a_t=(a_t, ["a_t"]),
                    xres=(xres, ["x0", "x1", "x2", "x3"]), hT=(hT, ["hT"]), ytm=(ytm, ["ytm"]), Mt=(Mt, ["Mt0g0", "Mt0g1", "Mt1g0", "Mt1g1"]),
                    eall=(eall, ["eall0", "eall1"]), diag=(diag, ["diag"]), mask_le=(mask_le, ["mask_le"]),
                    mask_gt=(mask_gt, ["mask_gt"]), cT=(cT, ["cT"]), Ab=(Ab, ["Ab"]), xdt=(xdt, ["xdt0", "xdt1"]),
                    CBm=(CBm, ["CBm0", "CBm1"]), ynb=(ynb, ["ynb0", "ynb1"]), xD=(xD, ["xD"]), Btm=(Btm, ["Btm0", "Btm1"]))
    for nm in cfg.dump:
        tns, pcs = dumpable[nm]
        shp = list(tns.shape)
        dd = nc.dram_tensor("dbg_" + nm, shp, tns.dtype, kind="ExternalOutput").ap()
        full = tuple(slice(None) for _ in shp)
        dma(dd[full], tns[full], pcs, [], sem("dbgs_" + nm), is_out=True)

    P.finalize(nc, stack)
    with nc.Block() as block:
        @block.tensor
        def _(e):
            P.emit_stream("pe", e)

        @block.scalar
        def _(e):
            P.emit_stream("act", e)

        @block.vector
        def _(e):
            P.emit_stream("dve", e)

        @block.gpsimd
        def _(e):
            P.emit_stream("pool", e)

        @block.sync
        def _(e):
            P.emit_stream("sp", e)
    stack.close()
    return nc


def _fm(v, nk):
    return np.ascontiguousarray(np.asarray(v, np.float32).reshape(nk, 128).T)


def prep_shared(inp):
    f = lambda a: np.ascontiguousarray(np.asarray(a, np.float32))
    w_ada = f(inp["w_ada"][0])
    w_in = f(inp["w_in"][0])
    sh = {}
    sh["w_ada_r"] = np.ascontiguousarray(w_ada.reshape(8, 128, 12, 512).transpose(2, 1, 0, 3)).reshape(12, 128, 4096)
    sh["w_in_r"] = np.ascontiguousarray(w_in[:, :3584].reshape(8, 128, 7, 512).transpose(2, 1, 0, 3)).reshape(7, 128, 4096)
    sh["w_dt_r"] = np.ascontiguousarray(w_in[:, 3584:].reshape(8, 128, 16).transpose(1, 0, 2)).reshape(128, 128)
    pw = f(inp["pool_w"][0])
    sh["pool_w_r"] = np.ascontiguousarray(pw.reshape(4, 2, 128, 256).transpose(2, 0, 1, 3)).reshape(128, 2048)
    wo = f(inp["w_out"][0])
    sh["w_out_r"] = np.ascontiguousarray(wo.reshape(16, 128, 8, 128).transpose(2, 1, 0, 3)).reshape(8, 128, 2048)
    w1 = f(inp["w_ff1"][0])
    sh["w_ff1_r"] = np.ascontiguousarray(w1.reshape(8, 128, 8, 512).transpose(2, 1, 0, 3)).reshape(8, 128, 4096)
    w2 = f(inp["w_ff2"][0])
    sh["w_ff2_r"] = np.ascontiguousarray(w2.reshape(32, 128, 8, 128).transpose(2, 1, 0, 3)).reshape(8, 128, 4096)
    vecs = np.zeros((128, NV), np.float32)
    vecs[:, V_NMW:V_NMW + 8] = _fm(inp["norm_mix_w"][0], 8)
    vecs[:, V_NFW:V_NFW + 8] = _fm(inp["norm_ffn_w"][0], 8)
    vecs[:, V_PB:V_PB + 8] = _fm(np.asarray(inp["pool_b"][0]).reshape(-1), 8)
    vecs[:, V_PS:V_PS + 8] = _fm(inp["pool_scale"][0], 8)
    cw = np.asarray(inp["conv_w"][0], np.float32)
    for k in range(4):
        vecs[:, V_CW + k * 12:V_CW + (k + 1) * 12] = _fm(cw[k], 12)
    vecs[:, V_CB:V_CB + 12] = _fm(inp["conv_b"][0], 12)
    vecs[:, V_B1:V_B1 + 32] = _fm(inp["b_ff1"][0], 32)
    vecs[:, V_B2:V_B2 + 8] = _fm(inp["b_ff2"][0], 8)
    vecs[:, V_BADA:V_BADA + 48] = _fm(inp["b_ada"][0], 48)
    vecs[:, V_SNW:V_SNW + 8] = _fm(inp["ssd_norm_w"][0], 8)
    sh["vecs"] = vecs
    rows = np.zeros((1, NR), np.float32)
    rows[0, R_FNW:R_FNW + 1024] = np.asarray(inp["final_norm_w"], np.float32)
    rows[0, R_SNW:R_SNW + 1024] = np.asarray(inp["ssd_norm_w"][0], np.float32)
    rows[0, R_DSK:R_DSK + 16] = np.asarray(inp["d_skip"][0], np.float32)
    rows[0, R_DTB:R_DTB + 16] = np.asarray(inp["dt_bias"][0], np.float32)
    rows[0, R_ALOG:R_ALOG + 16] = np.asarray(inp["a_log"][0], np.float32)
    sh["rows"] = rows
    return sh


def prep_core(inp, c):
    f = lambda a: np.ascontiguousarray(np.asarray(a, np.float32))
    m = {}
    m["xp"] = f(inp["x_prompt"][2 * c:2 * c + 2])
    m["xs"] = f(inp["x_sample"][4 * c:4 * c + 4]).reshape(64, 1024)
    cp_ = f(inp["cache_pool"][0, 4 * c:4 * c + 4])
    m["cpool"] = np.ascontiguousarray(cp_.reshape(4, 15, 8, 128).transpose(3, 2, 0, 1))
    cc = f(inp["cache_conv"][0, 4 * c:4 * c + 4])
    m["cconv"] = np.ascontiguousarray(cc.reshape(4, 3, 12, 128).transpose(3, 2, 0, 1))
    st = f(inp["state_ssm"][0, 4 * c:4 * c + 4])
    m["sstate"] = np.ascontiguousarray(st.reshape(4, 1024, 128).transpose(0, 2, 1))
    cvec = np.concatenate([f(inp["c_prompt"][2 * c:2 * c + 2]), f(inp["c_sample"][4 * c:4 * c + 4])], 0)
    m["cT"] = np.ascontiguousarray(cvec.reshape(6, 8, 128).transpose(2, 1, 0))
    return m


_NC_CACHE = {}


def kernel(**inputs):
    cfg = Cfg()
    if "nc" not in _NC_CACHE:
        _NC_CACHE["nc"] = build_program(cfg)
    nc = _NC_CACHE["nc"]
    sh = prep_shared(inputs)
    in_maps = []
    for c in range(NCORES):
        m = dict(sh)
        m.update(prep_core(inputs, c))
        in_maps.append(m)
    res = run_bass_kernel_spmd(nc, in_maps, core_ids=list(range(NCORES)))
    rs = res.results
    cat = lambda k: np.concatenate([np.asarray(r[k], np.float32) for r in rs], 0)
    y_prompt = cat("y_p")
    y_sample = cat("y_s").reshape(32, 16, 1024)
    npool_p = cat("npool_p")[None]
    nconv_p = cat("nconv_p")[None]
    nssm_p = cat("nssm_p").reshape(16, 16, 64, 128)[None]
    npool_s = cat("npool_s")[None]
    nconv_s = cat("nconv_s")[None]
    nssm_s = cat("nssm_s").reshape(32, 16, 64, 128)[None]
    return (y_prompt, y_sample, npool_p, nconv_p, nssm_p, npool_s, nconv_s, nssm_s)
```

```python
import numpy as np
from contextlib import ExitStack
import concourse.bass as bass
import concourse.mybir as mybir
from concourse.bass_utils import run_bass_kernel_spmd

F32 = mybir.dt.float32
BF16 = mybir.dt.bfloat16
AF = mybir.ActivationFunctionType
ALU = mybir.AluOpType

NCORES = 8
D = 1024
SEQ = 4096
EPS = 1e-6
SEM_LIM = 12000


class Op:
    __slots__ = ("eng", "fn", "deps", "needed", "cnt", "dma", "dsem", "dval", "name")

    def __init__(self, eng, fn, name=""):
        self.eng = eng
        self.fn = fn
        self.deps = []
        self.needed = False
        self.cnt = 0
        self.dma = False
        self.dsem = None
        self.dval = 0
        self.name = name


class Prog:
    ENGS = ("pe", "act", "dve", "pool", "sp")

    def __init__(self):
        self.streams = {e: [] for e in self.ENGS}
        self.lastw = {}
        self.readers = {}
        self.over = {}
        self.dma_cnt = {}
        self.out_dmas = []
        self.all_dma_sems = {}

    def alias(self, a_list, b_list):
        for a in a_list:
            for b in b_list:
                self.over.setdefault(a, set()).add(b)
                self.over.setdefault(b, set()).add(a)

    def _expand(self, p):
        o = self.over.get(p)
        if o:
            return [p] + list(o)
        return [p]

    def add(self, eng, fn, reads=(), writes=(), dma_sem=None, name="", is_out=False):
        o = Op(eng, fn, name)
        deps = {}

        def dep(d, kind):
            if d is None or d is o:
                return
            k = id(d)
            if k not in deps:
                deps[k] = (d, kind)
            elif kind == "raw":
                deps[k] = (d, kind)

        for p0 in reads:
            for p in self._expand(p0):
                dep(self.lastw.get(p), "raw")
            if p0.startswith("ps"):
                r = self.readers.get(p0)
                if r:
                    for d in r[0].values():
                        dep(d, "war")
        for p0 in writes:
            for p in self._expand(p0):
                dep(self.lastw.get(p), "waw")
                r = self.readers.get(p)
                if r:
                    for d in r[0].values():
                        dep(d, "war")
                    for d in r[1]:
                        dep(d, "war")
        o.deps = list(deps.values())
        for p in reads:
            r = self.readers.setdefault(p, [{}, []])
            if dma_sem is not None:
                r[1].append(o)
            else:
                r[0][eng] = o
        for p in writes:
            self.lastw[p] = o
            self.readers[p] = [{}, []]
        if dma_sem is not None:
            o.dma = True
            o.dsem = dma_sem
            c = self.dma_cnt.get(id(dma_sem), 0) + 1
            self.dma_cnt[id(dma_sem)] = c
            o.dval = 16 * c
            self.all_dma_sems[id(dma_sem)] = dma_sem
            if is_out:
                self.out_dmas.append(o)
        self.streams[eng].append(o)
        return o

    def finalize(self, nc, stack):
        for e in self.ENGS:
            for o in self.streams[e]:
                for d, kind in o.deps:
                    if d.dma:
                        continue
                    if d.eng == o.eng and o.eng == "pe":
                        continue
                    d.needed = True
        sems = {}
        for e in self.ENGS:
            c = 0
            for o in self.streams[e]:
                if o.needed and not o.dma:
                    c += 1
                    o.cnt = c
            nep = (c + SEM_LIM - 1) // SEM_LIM + 1
            sems[e] = [stack.enter_context(nc.semaphore(f"s_{e}_{i}")) for i in range(nep)]
        self.sems = sems
        return sems

    def emit_stream(self, e, eh):
        sems = self.sems
        done = {k: 0 for k in self.ENGS}
        dwait = {}
        for o in self.streams[e]:
            need = {}
            for d, kind in o.deps:
                if d.dma:
                    k = id(d.dsem)
                    if dwait.get(k, 0) < d.dval:
                        dwait[k] = d.dval
                        eh.wait_ge(d.dsem, d.dval)
                    continue
                if d.eng == o.eng and o.eng == "pe":
                    continue
                if d.cnt > need.get(d.eng, 0):
                    need[d.eng] = d.cnt
            for de, c in need.items():
                if c > done[de]:
                    done[de] = c
                    ep = (c - 1) // SEM_LIM
                    loc = (c - 1) % SEM_LIM + 1
                    eh.wait_ge(sems[de][ep], loc)
            ins = o.fn(eh)
            if o.dma:
                ins.then_inc(o.dsem, 16)
            elif o.needed:
                ep = (o.cnt - 1) // SEM_LIM
                ins.then_inc(sems[e][ep], 1)
        if e == "sp":
            for k, sm in self.all_dma_sems.items():
                eh.wait_ge(sm, self.dma_cnt[k] * 16)


class Cfg:
    n_pseq = 2
    n_ptiles = 8
    n_sseq = 4
    do_sample = True
    R = 4
    stop = 99
    stop_tile = 0
    dump = ()


V_NMW, V_NFW, V_PB, V_PS, V_CW, V_CB, V_B1, V_B2, V_BADA, V_SNW = 0, 8, 16, 24, 32, 80, 92, 124, 132, 180
NV = 188
R_FNW, R_SNW, R_DSK, R_DTB, R_ALOG = 0, 1024, 2048, 2064, 2080
NR = 2096


def build_program(cfg):
    nc = bass.Bass("TRN2", target_bir_lowering=False)
    P = Prog()
    stack = ExitStack()

    def din(name, shape, dt=F32):
        return nc.dram_tensor(name, list(shape), dt, kind="ExternalInput").ap()

    def dout(name, shape, dt=F32):
        return nc.dram_tensor(name, list(shape), dt, kind="ExternalOutput").ap()

    def dscr(name, shape, dt=BF16):
        return nc.dram_tensor(name, list(shape), dt, kind="Internal").ap()

    NPS, NSS = cfg.n_pseq, cfg.n_sseq
    NSEQ = NPS + NSS
    xp = din("xp", [2, SEQ, D])
    xs = din("xs", [64, D])
    cpool = din("cpool", [128, 8, 4, 15])
    cconv = din("cconv", [128, 12, 4, 3])
    sstate = din("sstate", [4, 128, 1024])
    cT_d = din("cT", [128, 8, 6])
    wada_d = din("w_ada_r", [12, 128, 8 * 512])
    win_d = din("w_in_r", [7, 128, 8 * 512])
    wdt_d = din("w_dt_r", [128, 8 * 16])
    poolw_d = din("pool_w_r", [128, 4 * 2 * 256])
    wout_d = din("w_out_r", [8, 128, 16 * 128])
    wff1_d = din("w_ff1_r", [8, 128, 8 * 512])
    wff2_d = din("w_ff2_r", [8, 128, 32 * 128])
    vecs_d = din("vecs", [128, NV])
    rows_d = din("rows", [1, NR])

    y_p = dout("y_p", [2, SEQ, D])
    y_s = dout("y_s", [64, D])
    npool_p = dout("npool_p", [2, 15, 1024])
    nconv_p = dout("nconv_p", [2, 3, 1536])
    nssm_p = dout("nssm_p", [2, 1024, 128])
    npool_s = dout("npool_s", [4, 15, 1024])
    nconv_s = dout("nconv_s", [4, 3, 1536])
    nssm_s = dout("nssm_s", [4, 1024, 128])

    win_b = dscr("win_b", [7, 128, 4096])
    wdt_b = dscr("wdt_b", [128, 128])
    poolw_b = dscr("poolw_b", [128, 2048])
    wout_b = dscr("wout_b", [8, 128, 2048])
    wff1_b = dscr("wff1_b", [8, 128, 4096])
    wff2_b = dscr("wff2_b", [8, 128, 4096])

    def sem(name):
        return stack.enter_context(nc.semaphore(name))

    def sb(name, shape, dt=F32):
        return nc.alloc_sbuf_tensor("sb_" + name, list(shape), dt)

    ident_f = sb("ident_f", [128, 128])
    ident_b = sb("ident_b", [128, 128], BF16)
    mask_le = sb("mask_le", [128, 128])
    mask_gt = sb("mask_gt", [128, 128])
    ones_f = sb("ones_f", [128, 128])
    epsb = sb("epsb", [128, 2])
    vecs = sb("vecs", [128, NV])
    fnwb = sb("fnwb", [128, 1024])
    dskb = sb("dskb", [128, 16])
    dtbb = sb("dtbb", [128, 16])
    Ab = sb("Ab", [128, 16])
    diag = sb("diag", [128, 48, 128], BF16)
    Did = sb("Did", [128, 16, 128], BF16)
    corr = sb("corr", [128, 4, 16])
    cT = sb("cT", [128, 8, 6])
    mod = sb("mod", [128, 48, 6])
    wm1 = sb("wm1", [128, 8, 6])
    wm2 = sb("wm2", [128, 8, 6])
    g2b2 = sb("g2b2", [128, 8, 6])
    hT = sb("hT", [128, 1024])
    hTb = sb("hTb", [128, 1024], BF16)
    xres = sb("xres", [128, 7, 1024])
    xn = sb("xn", [128, 2, 1024], BF16)
    ystage = xn[:, :, :].bitcast(F32) if False else None
    hTt = sb("hTt", [128, 8, 512], BF16)
    ss = sb("ss", [128, 16])
    catT = sb("catT", [128, 16, 512], BF16)
    G1 = sb("G1", [128, 18720], BF16)
    aT = G1[:, 0:16384].rearrange("p (k t) -> p k t", k=32)
    U_p = G1[:, 0:8 * 527].rearrange("p (k s t) -> p k s t", k=8, s=1)
    U_s = G1[:, 0:8 * 4 * 31].rearrange("p (k s t) -> p k s t", k=8, s=4)
    o1 = 8 * 527
    PSCR = G1[:, o1:o1 + 4 * 2 * 527].bitcast(F32)
    o2 = o1 + 4 * 2 * 527
    pooled = G1[:, o2:o2 + 4 * 2 * 512].rearrange("p (r k t) -> p r k t", r=4, k=2)
    o3 = o2 + 4096
    XBC_p = G1[:, o3:o3 + 12 * 515].rearrange("p (k s t) -> p k s t", k=12, s=1)
    XBC_s = G1[:, o3:o3 + 12 * 4 * 19].rearrange("p (k s t) -> p k s t", k=12, s=4)
    assert o3 + 12 * 515 <= 18720
    G2 = sb("G2", [128, 10240], BF16)
    XC = G2[:, 0:6144].rearrange("p (k t) -> p k t", k=12)
    zs = G2[:, 6144:10240].rearrange("p (b d) -> p b d", b=4)
    fT = G2[:, 0:8192].bitcast(F32).rearrange("p (k t) -> p k t", k=8)
    dtr = sb("dtr", [128, 4, 16])
    dtt = sb("dtt", [128, 4, 16])
    dte = sb("dte", [128, 4, 16])
    lnt = sb("lnt", [128, 16])
    uhist = sb("uhist", [128, 8, 1, 15], BF16)
    xhist = sb("xhist", [128, 12, 1, 3], BF16)
    a_t = sb("a_t", [128, 4, 16])
    eall = sb("eall", [128, 2, 48])
    xdt = sb("xdt", [128, 1, 1024], BF16)
    xdtd = sb("xdtd", [128, 1, 1024], BF16)
    xD = sb("xD", [128, 1, 1024], BF16)
    Btm = sb("Btm", [128, 2, 256], BF16)
    CBm = sb("CBm", [128, 2, 2, 128], BF16)
    ostg = sb("ostg", [128, 1024])
    segb = sb("segb", [128, 2048], BF16)
    mask_gt_b = sb("mask_gt_b", [128, 128], BF16)
    Mt = sb("Mt", [128, 1, 2048], BF16)
    ytm = sb("ytm", [128, 1, 1024])
    ynb = sb("ynb", [128, 1, 1024], BF16)
    ssq = sb("ssq", [128, 8])
    ulast = sb("ulast", [128, 8, 4, 15])
    xlast = sb("xlast", [128, 12, 4, 3])
    ostage = ostg[:, :]
    cstage = Mt[:, 0, 0:1440].bitcast(F32).rearrange("p (k s t) -> p k s t", k=12, s=4)
    R = cfg.R
    ring = [sb(f"ring{i}", [128, 4096], BF16) for i in range(R)]
    ring_sem = [sem(f"rs{i}") for i in range(R)]
    wada_st = [G1[:, 0:8192].bitcast(F32), G1[:, 8192:16384].bitcast(F32)]
    wada_sem = [sem("was0"), sem("was1")]

    aTn = [f"aT{m}" for m in range(32)]
    G1P = ["U", "PSCR0", "PSCR1", "pooled0", "pooled1", "pooled2", "pooled3", "XBC"]
    P.alias(aTn, G1P)
    P.alias(["fT"], ["XC", "zs0", "zs1", "zs2", "zs3"])
    P.alias(["ynbw"], ["ynb0", "ynb1"])
    P.alias(["Mtg0", "Mtg1"], ["cstage"])
    P.alias(["wst0", "wst1"], aTn + G1P)

    PS = [nc.alloc_psum_tensor(f"ps{i}", [128, 1024], F32) for i in range(4)]
    ps_state = {"i": 0}

    ps_held = set()

    def ps1():
        i = ps_state["i"]
        while i in ps_held:
            i = (i + 1) % 8
        ps_state["i"] = (i + 1) % 8
        t = PS[i // 2]
        h = i % 2
        return t[:, h * 512:(h + 1) * 512], [f"ps{i}"]

    def ps2():
        i = ps_state["i"]
        if i % 2:
            i = (i + 1) % 8
        while i in ps_held or (i + 1) in ps_held:
            i = (i + 2) % 8
        ps_state["i"] = (i + 2) % 8
        return PS[i // 2][:, :], [f"ps{i}", f"ps{i + 1}"]

    def act(out, in_, func, reads, writes, bias=None, scale=None, accum=None, name=""):
        kw = {}
        if bias is not None:
            kw["bias"] = bias
        if scale is not None:
            kw["scale"] = scale
        if accum is not None:
            kw["accum_out"] = accum
        return P.add("act", lambda e: e.activation(out=out, in_=in_, func=func, **kw), reads, writes, name=name)

    def tt(eng, out, in0, in1, op, reads, writes, name=""):
        return P.add(eng, lambda e: e.tensor_tensor(out=out, in0=in0, in1=in1, op=op), reads, writes, name=name)

    def ts(eng, out, in0, s1, s2, op0, op1, reads, writes, name=""):
        if op1 is None:
            return P.add(eng, lambda e: e.tensor_scalar(out=out, in0=in0, scalar1=s1, scalar2=None, op0=op0),
                         reads, writes, name=name)
        return P.add(eng, lambda e: e.tensor_scalar(out=out, in0=in0, scalar1=s1, scalar2=s2, op0=op0, op1=op1),
                     reads, writes, name=name)

    def stt(eng, out, in0, scalar, in1, op0, op1, reads, writes, name=""):
        return P.add(eng, lambda e: e.scalar_tensor_tensor(out=out, in0=in0, scalar=scalar, in1=in1, op0=op0, op1=op1),
                     reads, writes, name=name)

    def cp(eng, out, in_, reads, writes, name=""):
        if eng == "act":
            return act(out, in_, AF.Identity, reads, writes, name=name)
        return P.add(eng, lambda e: e.tensor_copy(out=out, in_=in_), reads, writes, name=name)

    def memset(eng, ap, val, writes):
        return P.add(eng, lambda e: e.memset(ap, val), (), writes)

    def dma(out, in_, reads, writes, s, eng="sp", is_out=False, name=""):
        return P.add(eng, lambda e: e.dma_start(out=out, in_=in_), reads, writes, dma_sem=s, is_out=is_out, name=name)

    def mmgroup(out, pairs, reads, writes, name=""):
        n = len(pairs)

        def fn(e):
            ins = None
            for i, (l, r) in enumerate(pairs):
                ins = e.matmul(out, lhsT=l, rhs=r, start=(i == 0), stop=(i == n - 1))
            return ins
        return P.add("pe", fn, reads, writes, name=name)

    def transposes(items, reads, writes, name=""):
        def fn(e):
            ins = None
            for (o, i, idn) in items:
                ins = e.transpose(out=o, in_=i, identity=idn)
            return ins
        return P.add("pe", fn, reads, writes, name=name)

    cs = {k: sem("cs_" + k) for k in ("win", "wdt", "poolw", "wout", "wff1", "wff2")}
    for s_ in range(7):
        dma(win_b[s_], win_d[s_], (), ["d_win"], cs["win"], eng="pool")
    dma(wdt_b[:, :], wdt_d[:, :], (), ["d_wdt"], cs["wdt"], eng="pool")
    dma(poolw_b[:, :], poolw_d[:, :], (), ["d_poolw"], cs["poolw"], eng="pool")
    def cast_wout():
        for s_ in range(8):
            dma(wout_b[s_], wout_d[s_], (), ["d_wout"], cs["wout"], eng="pool")

    def cast_ff1(s_):
        dma(wff1_b[s_], wff1_d[s_], (), ["d_wff1"], cs["wff1"], eng="pool")

    def cast_ff2(s_):
        dma(wff2_b[s_], wff2_d[s_], (), ["d_wff2"], cs["wff2"], eng="pool")

    misc = [sem(f"misc{i}") for i in range(6)]
    dma(vecs[:, :], vecs_d[:, :], (), ["vecs"], misc[0])
    dma(cT[:, :, :], cT_d[:, :, :], (), ["cT"], misc[1])
    dma(fnwb[:, :], rows_d[0:1, R_FNW:R_FNW + 1024].partition_broadcast(128), (), ["fnwb"], misc[2])
    dma(dskb[:, :], rows_d[0:1, R_DSK:R_DSK + 16].partition_broadcast(128), (), ["dskb"], misc[4])
    dma(dtbb[:, :], rows_d[0:1, R_DTB:R_DTB + 16].partition_broadcast(128), (), ["dtbb"], misc[5])
    alog_sem = sem("alog")
    dma(Ab[:, :], rows_d[0:1, R_ALOG:R_ALOG + 16].partition_broadcast(128), (), ["Ab"], alog_sem)

    memset("pool", ident_f[:, :], 0.0, ["ident_f"])
    P.add("pool", lambda e: e.affine_select(out=ident_f[:, :], in_=ident_f[:, :], compare_op=ALU.not_equal, fill=1.0,
                                            base=0, pattern=[[-1, 128]], channel_multiplier=1),
          ["ident_f"], ["ident_f"])
    cp("pool", ident_b[:, :], ident_f[:, :], ["ident_f"], ["ident_b"])
    memset("pool", ones_f[:, :], 1.0, ["ones_f"])
    memset("pool", epsb[:, :], EPS, ["epsb"])
    memset("pool", epsb[:, 1:2], 1.0, ["epsb"])
    P.add("pool", lambda e: e.affine_select(out=mask_le[:, :], in_=ones_f[:, :], compare_op=ALU.is_ge, fill=0.0,
                                            base=0, pattern=[[1, 128]], channel_multiplier=-1),
          ["ones_f"], ["mask_le"])
    P.add("pool", lambda e: e.affine_select(out=mask_gt[:, :], in_=ones_f[:, :], compare_op=ALU.is_gt, fill=0.0,
                                            base=0, pattern=[[-1, 128]], channel_multiplier=1),
          ["ones_f"], ["mask_gt"])
    cp("pool", mask_gt_b[:, :], mask_gt[:, :], ["mask_gt"], ["mask_gt_b"])
    for j in range(48):
        ts("pool", diag[:, j, :], ident_f[:, :], vecs[:, V_CW + j:V_CW + j + 1], None, ALU.mult, None,
           ["ident_f", "vecs"], ["diag"])
    for h_ in range(16):
        ts("pool", Did[:, h_, :], ident_f[:, :], dskb[:, h_:h_ + 1], None, ALU.mult, None,
           ["ident_f", "dskb"], ["Did"])
    memset("pool", corr[:, :, :], 1.0, ["corr"])
    for g, w in enumerate((2, 4, 8, 16)):
        for t in range(w - 1):
            memset("pool", corr[:, g, t:t + 1], float(w) / float(t + 1), ["corr"])
    act(Ab[:, :], Ab[:, :], AF.Exp, ["Ab"], ["Ab"])
    ts("dve", Ab[:, :], Ab[:, :], -1.0, None, ALU.mult, None, ["Ab"], ["Ab"])
    act(cT[:, :, :], cT[:, :, :], AF.Silu, ["cT"], ["cT"])

    xsem = [sem(f"xs{b}") for b in range(7)]
    for b_ in range(4):
        if cfg.n_pseq > 0 and cfg.n_ptiles > 0:
            dma(xres[0:128, b_, :], xp[0, b_ * 128:(b_ + 1) * 128, :], (), [f"x{b_}"], xsem[b_])
        else:
            dma(xres[0:16, b_, :], xs[b_ * 16:(b_ + 1) * 16, :], (), [f"x{b_}"], xsem[b_])
    modrow = ostg
    for sl in range(12):
        st = wada_st[sl % 2]
        stn = f"wst{sl % 2}"
        dma(st, wada_d[sl], (), [stn], wada_sem[sl % 2])
        stv = st.rearrange("p (k n) -> p k n", k=8)
        pt, pn = ps1()
        mmgroup(pt[0:6, :], [(cT[:, kc, :], stv[:, kc, :]) for kc in range(8)], [stn, "cT"], pn)
        cp("act", modrow[0:6, 0:512], pt[0:6, :], pn, ["ostage"])
        ptT, pnT = ps1()
        transposes([(ptT[:, j * 8:j * 8 + 6], modrow[0:6, j * 128:(j + 1) * 128], ident_f[0:6, 0:6]) for j in range(4)],
                   ["ostage", "ident_f"], pnT)
        tt("dve", mod[:, sl * 4:(sl + 1) * 4, :],
           ptT[:, 0:32].rearrange("p (j s) -> p j s", j=4)[:, :, 0:6],
           vecs[:, V_BADA + sl * 4:V_BADA + sl * 4 + 4].unsqueeze(2).to_broadcast([128, 4, 6]),
           ALU.add, pnT + ["vecs"], ["mod"])
    ts("dve", wm1[:, :, :], mod[:, 8:16, :], 1.0, None, ALU.add, None, ["mod"], ["wm1"])
    tt("dve", wm1[:, :, :], wm1[:, :, :], vecs[:, V_NMW:V_NMW + 8].unsqueeze(2).to_broadcast([128, 8, 6]), ALU.mult,
       ["wm1", "vecs"], ["wm1"])
    ts("dve", wm2[:, :, :], mod[:, 32:40, :], 1.0, None, ALU.add, None, ["mod"], ["wm2"])
    tt("dve", wm2[:, :, :], wm2[:, :, :], vecs[:, V_NFW:V_NFW + 8].unsqueeze(2).to_broadcast([128, 8, 6]), ALU.mult,
       ["wm2", "vecs"], ["wm2"])
    tt("dve", g2b2[:, :, :], mod[:, 40:48, :], vecs[:, V_B2:V_B2 + 8].unsqueeze(2).to_broadcast([128, 8, 6]), ALU.mult,
       ["mod", "vecs"], ["g2b2"])

    def modv(part, kc, si):
        return mod[:, part * 8 + kc, si:si + 1]

    tiles = []
    for sq in range(NPS):
        for ti in range(cfg.n_ptiles):
            tiles.append(dict(kind="p", seq=sq, ti=ti, S=1, L=512, Q=128, NB=4,
                              first=(ti == 0), last=(ti == cfg.n_ptiles - 1)))
    if cfg.do_sample:
        tiles.append(dict(kind="s", S=4, L=16, Q=16, NB=4, first=True, last=True))

    slabs = []
    for _t in tiles:
        for s_ in range(4):
            slabs.append(("win", win_b[s_], 4096, "d_win", cs["win"], 7))
        slabs.append(("wdt", wdt_b[:, :], 128, "d_wdt", cs["wdt"], 1))
        for s_ in range(4, 7):
            slabs.append(("win", win_b[s_], 4096, "d_win", cs["win"], 7))
        slabs.append(("poolw", poolw_b[:, :], 2048, "d_poolw", cs["poolw"], 1))
        for s_ in range(8):
            slabs.append(("wout", wout_b[s_], 2048, "d_wout", cs["wout"], 8))
        for s_ in range(8):
            slabs.append(("wff1", wff1_b[s_], 4096, "d_wff1", cs["wff1"], 8))
        for s_ in range(8):
            slabs.append(("wff2", wff2_b[s_], 4096, "d_wff2", cs["wff2"], 8))
    slab_state = {"issued": 0, "cur": 0}

    def slab_issue():
        j = slab_state["issued"]
        if j >= len(slabs):
            return
        kind, src, n, piece, _csem, _ncast = slabs[j]
        slot = j % R
        dma(ring[slot][:, 0:n], src, [piece], [f"ring{slot}"], ring_sem[slot], name=f"slab{j}")
        slab_state["issued"] = j + 1

    def slab_get(kind, ahead=0):
        j = slab_state["cur"] + ahead
        assert slabs[j][0] == kind, (slabs[j][0], kind)
        slot = j % R
        return ring[slot], f"ring{slot}"

    def slab_done():
        slab_state["cur"] += 1
        slab_issue()

    for _ in range(R):
        slab_issue()

    ysem = [sem("ys0"), sem("ys1")]
    osem = sem("osem")
    csem_ = sem("csem")
    stsem = sem("stsem")
    x3sem = sem("x3sem")
    y_slot = {"i": 0}
    yst = xn[:, :, :].rearrange("p b d -> p (b d)").bitcast(F32).rearrange("p (r d) -> p r d", r=1)

    def rstd_from_ss(col, Q, n):
        act(lnt[0:Q, col:col + 1], ss[0:Q, col:col + 1], AF.Ln, [f"ss{col}", "epsb"], [f"lnt{col}"], bias=epsb[0:Q, 0:1], scale=1.0 / n)
        act(ss[0:Q, col:col + 1], lnt[0:Q, col:col + 1], AF.Exp, [f"lnt{col}"], [f"ss{col}"], scale=-0.5)

    def emit_xload(t, b):
        Q = t["Q"]
        if t["kind"] == "p":
            src = xp[t["seq"], t["ti"] * 512 + b * 128:t["ti"] * 512 + (b + 1) * 128, :]
        else:
            src = xs[b * 16:(b + 1) * 16, :]
        sl = t["xs"][b]
        dma(xres[0:Q, sl, :], src, (), [f"x{sl}"], xsem[sl])

    hTn = ["hTt0", "hTt1", "hTt2", "hTt3"]
    nbufs = [(xdt, "xdt"), (xD, "xD"), (xdtd, "xdtd"), (ynb, "ynbw")]

    def norm_stats(t, b, wm, shpart, alt=None):
        S, L, Q, NB = t["S"], t["L"], t["Q"], t["NB"]
        xs_ = t["xs"]
        nb_, nbn = nbufs[b]
        if alt is None:
            xsrc, xpc = xres[0:Q, xs_[b], :], f"x{xs_[b]}"
        else:
            xsrc, xpc = alt
        act(nb_[0:Q, 0, :], xsrc, AF.Square, [xpc], [nbn, f"ss{b}"], accum=ss[0:Q, b:b + 1])
        rstd_from_ss(b, Q, 1024.0)
        ts("dve", nb_[0:Q, 0, :], xsrc, ss[0:Q, b:b + 1], None, ALU.mult, None, [xpc, f"ss{b}"], [nbn])

    def norm_trev(t, b, wm, shpart):
        S, L, Q, NB = t["S"], t["L"], t["Q"], t["NB"]
        nb_, nbn = nbufs[b]
        pt2, pn2 = ps2()
        ptb = pt2.bitcast(BF16)
        for half in range(2):
            transposes([(ptb[:, half * 1024 + k4 * 128:half * 1024 + k4 * 128 + Q],
                         nb_[0:Q, 0, (half * 4 + k4) * 128:(half * 4 + k4 + 1) * 128], ident_b[0:Q, 0:Q])
                        for k4 in range(4)], [nbn, "ident_b"], [pn2[half]])
        si = t["seq"] if t["kind"] == "p" else 2 + b
        for kc in range(8):
            half, k4 = kc // 4, kc % 4
            src = ptb[:, half * 1024 + k4 * 128:half * 1024 + k4 * 128 + Q]
            dstp = hTt[:, kc, b * Q:(b + 1) * Q]
            if half == 0:
                act(dstp, src, AF.Identity, [pn2[half], wm[1], "mod"], [f"hTt{b}"],
                    bias=modv(shpart, kc, si), scale=wm[0][:, kc, si:si + 1])
            else:
                ts("dve", dstp, src, wm[0][:, kc, si:si + 1], modv(shpart, kc, si), ALU.mult, ALU.add,
                   [pn2[half], wm[1], "mod"], [f"hTt{b}"])

    def norm_all(t, wm, shpart):
        NB = t["NB"]
        norm_stats(t, 0, wm, shpart)
        for b in range(NB):
            if b + 1 < NB:
                norm_stats(t, b + 1, wm, shpart)
            norm_trev(t, b, wm, shpart)

    def emit_tile(t):
        S, L, Q, NB = t["S"], t["L"], t["Q"], t["NB"]
        T = S * L
        isp = t["kind"] == "p"
        xs_ = t["xs"]
        U = U_p if isp else U_s
        XBC = XBC_p if isp else XBC_s
        if isp and t["first"]:
            memset("pool", U[:, :, :, 0:15], 0.0, ["U"])
            memset("pool", XBC[:, :, :, 0:3], 0.0, ["XBC"])
        elif isp:
            cp("pool", U[:, :, :, 0:15], uhist[:, :, :, :], ["uhist"], ["U"])
            cp("pool", XBC[:, :, :, 0:3], xhist[:, :, :, :], ["xhist"], ["XBC"])
        if not isp:
            dma(cstage[:, 0:8, :, :], cpool[:, :, :, :], (), ["cstage"], csem_)
            cp("pool", U[:, :, :, 0:15], cstage[:, 0:8, :, :], ["cstage"], ["U"])
            dma(cstage[:, :, :, 0:3], cconv[:, :, :, :], (), ["cstage"], csem_)
            cp("pool", XBC[:, :, :, 0:3], cstage[:, :, :, 0:3], ["cstage"], ["XBC"])
        if t["idx"] == 0:
            norm_all(t, (wm1, "wm1"), 0)
        if t['stop'] <= 1:
            return

        def pool_elem(g):
            w = (2, 4, 8, 16)[g]
            W = 15 + L
            ug = U[:, 2 * g:2 * g + 2, :, :]
            sc = [PSCR[:, i * (2 * S * W):(i + 1) * (2 * S * W)].rearrange("p (k s t) -> p k s t", k=2, s=S)
                  for i in range(2)]
            cur, curn = ug, "U"
            sh = 1
            nb = 0
            while sh < w:
                dst, dn = sc[nb % 2], f"PSCR{nb % 2}"
                tt("dve", dst[:, :, :, sh:W], cur[:, :, :, sh:W], cur[:, :, :, 0:W - sh], ALU.add, [curn], [dn])
                cur, curn = dst, dn
                sh *= 2
                nb += 1
            if isp and t["first"]:
                for k2 in range(2):
                    tt("dve", cur[:, k2, 0, 15:31], cur[:, k2, 0, 15:31], corr[:, g, :], ALU.mult,
                       [curn, "corr"], [curn])
            stt("dve", pooled[:, g, :, 0:T].rearrange("p k (s l) -> p k s l", s=S), cur[:, :, :, 15:W], 1.0 / w,
                ug[:, :, :, 15:W], ALU.mult, ALU.subtract, [curn, "U"], [f"pooled{g}"])
        for sl in range(2):
            slot, sn = slab_get("win")
            wv = slot[:, :].rearrange("p (k n) -> p k n", k=8)
            for j in range(4):
                m = sl * 4 + j
                pt, pn = ps1()
                mmgroup(pt[:, 0:T], [(wv[:, kc, j * 128:(j + 1) * 128], hTt[:, kc, 0:T]) for kc in range(8)],
                        [sn] + hTn, pn)
                src = pt[:, 0:T].rearrange("p (s l) -> p s l", s=S)
                import os
                if not os.environ.get("DBG_NOEVAC"):
                    cp("act", U[:, m, :, 15:15 + L], src, pn, ["U"])
                if t["last"] and not os.environ.get("DBG_NOULAST"):
                    if os.environ.get("DBG_ULV") == "1":
                        cp("dve", ulast[:, m, 0, :], pt[:, T - 15:T], pn, ["ulast"])
                    elif os.environ.get("DBG_ULV") == "2":
                        cp("dve", ulast[:, m, 0:S, 0:8], src[:, :, L - 8:L], pn, ["ulast"])
                    elif os.environ.get("DBG_ULV") == "3":
                        cp("dve", ulast[:, m, 0:S, :], src[:, :, 0:15], pn, ["ulast"])
                    else:
                        cp(os.environ.get("DBG_ULENG", "dve"), ulast[:, m, 0:S, :], src[:, :, L - 15:L], pn, ["ulast"])
            slab_done()
            pool_elem(2 * sl)
            pool_elem(2 * sl + 1)
            if t["idx"] == 0 and sl == 1:
                cast_wout()

        if isp and not t["last"]:
            cp("pool", uhist[:, :, :, :], U[:, :, :, L:L + 15], ["U"], ["uhist"])
        if t['stop'] <= 1.2:
            return
        for sl in range(2):
            slot, sn = slab_get("win")
            wv = slot[:, :].rearrange("p (k n) -> p k n", k=8)
            for b in range(NB):
                pt, pn = ps1()
                mmgroup(pt[0:Q, :], [(hTt[:, kc, b * Q:(b + 1) * Q], wv[:, kc, :]) for kc in range(8)],
                        [sn] + hTn, pn)
                act(zs[0:Q, b, sl * 512:(sl + 1) * 512], pt[0:Q, :], AF.Silu, pn, [f"zs{b}"])
            slab_done()
        slot, sn = slab_get("wdt")
        wv = slot[:, 0:128].rearrange("p (k n) -> p k n", k=8)
        pt, pn = ps1()
        for b in range(NB):
            mmgroup(pt[0:Q, b * 16:(b + 1) * 16], [(hTt[:, kc, b * Q:(b + 1) * Q], wv[:, kc, :]) for kc in range(8)],
                    [sn] + hTn, pn)
        slab_done()
        ptv = pt[0:Q, 0:64].rearrange("p (b h) -> p b h", b=4)
        tt("dve", dtr[0:Q, :, :], ptv, dtbb[0:Q, :].unsqueeze(1).to_broadcast([Q, 4, 16]), ALU.add,
           pn + ["dtbb"], ["dtr"])
        act(dte[0:Q, :, :], dtr[0:Q, :, :], AF.Exp, ["dtr"], ["dte"])
        act(dtt[0:Q, :, :], dte[0:Q, :, :], AF.Ln, ["dte", "epsb"], ["dtt"], bias=epsb[0:Q, 1:2])
        tt("dve", a_t[0:Q, :, :], dtt[0:Q, :, :], Ab[0:Q, :].unsqueeze(1).to_broadcast([Q, 4, 16]), ALU.mult,
           ["dtt", "Ab"], ["a_t"])
        if t['stop'] <= 1.4:
            return
        for sl in range(3):
            slot, sn = slab_get("win")
            wv = slot[:, :].rearrange("p (k n) -> p k n", k=8)
            for j in range(4):
                m = sl * 4 + j
                pt, pn = ps1()
                mmgroup(pt[:, 0:T], [(wv[:, kc, j * 128:(j + 1) * 128], hTt[:, kc, 0:T]) for kc in range(8)],
                        [sn] + hTn, pn)
                src = pt[:, 0:T].rearrange("p (s l) -> p s l", s=S)
                cp("act", XBC[:, m, :, 3:3 + L], src, pn, ["XBC"])
                if t["last"]:
                    cp("dve", xlast[:, m, 0:S, :], src[:, :, L - 3:L], pn, ["xlast"])
            slab_done()
        if t['stop'] <= 1.6:
            return
        if t['stop'] <= 2:
            return
        slot, sn = slab_get("poolw")
        pwv = slot[:, 0:2048].rearrange("p (g k d) -> p g k d", g=4, k=2)
        for g in range(4):
            for m2 in range(2):
                pt, pn = ps1()
                mmgroup(pt[:, 0:T], [(pwv[:, g, k2, m2 * 128:(m2 + 1) * 128], pooled[:, g, k2, 0:T]) for k2 in range(2)],
                        [sn, f"pooled{g}"], pn)
                m = 2 * g + m2
                ts("dve", catT[:, m, 0:T], pt[:, 0:T], vecs[:, V_PB + m:V_PB + m + 1], vecs[:, V_PS + m:V_PS + m + 1],
                   ALU.add, ALU.mult, pn + ["vecs"], [f"cat{m}"])
        slab_done()
        if t['stop'] <= 3:
            return
        for m in range(12):
            pt, pn = ps1()
            mmgroup(pt[:, 0:T].rearrange("p (s l) -> p s l", s=S) if S > 1 else pt[:, 0:T],
                    [(diag[:, k * 12 + m, :], (XBC[:, m, :, k:k + L] if S > 1 else XBC[:, m, 0, k:k + L]))
                     for k in range(4)],
                    ["diag", "XBC"], pn)
            act(XC[:, m, 0:T], pt[:, 0:T], AF.Silu, pn + ["vecs"], ["XC"], bias=vecs[:, V_CB + m:V_CB + m + 1])
        if isp and not t["last"]:
            cp("pool", xhist[:, :, :, :], XBC[:, :, :, L:L + 3], ["XBC"], ["xhist"])
        if t['stop'] <= 4:
            return
        NEARLY = 3
        wo_early = []

        def wout_first_half():
            for m in range(NEARLY):
                slot, sn = slab_get("wout", ahead=m)
                wv = slot[:, 0:2048].rearrange("p (k n) -> p k n", k=16)
                pt, pn = ps1()
                pairs = [(wv[:, kc, :], catT[:, kc, 0:T]) for kc in range(8)]
                def fn1(e, pairs=pairs, pt=pt, T=T):
                    ins = None
                    for i, (l, r) in enumerate(pairs):
                        ins = e.matmul(pt[:, 0:T], lhsT=l, rhs=r, start=(i == 0), stop=False)
                    return ins
                P.add("pe", fn1, [sn] + [f"cat{k}" for k in range(8)], pn)
                wo_early.append((slot, sn, pt, pn))
                ps_held.add(int(pn[0][2:]))

        def ssd_s1(b):
            r2 = b % 2
            c0, c1 = b * Q, (b + 1) * Q
            pt, pn = ps1()
            mmgroup(pt[0:Q, 0:16], [(mask_le[0:Q, 0:Q], a_t[0:Q, b, :])], ["mask_le", "a_t"], pn)
            mmgroup(pt[0:Q, 16:32], [(mask_gt[0:Q, 0:Q], a_t[0:Q, b, :])], ["mask_gt", "a_t"], pn)
            mmgroup(pt[:, 32:48], [(ones_f[0:Q, :], a_t[0:Q, b, :])], ["ones_f", "a_t"], pn)
            act(eall[0:Q, r2, 0:32], pt[0:Q, 0:32], AF.Exp, pn, [f"eall{r2}"])
            act(eall[:, r2, 32:48], pt[:, 32:48], AF.Exp, pn, [f"eall{r2}"])
            edec = eall[0:Q, r2, 16:32]
            pt, pn = ps1()
            ptb = pt.bitcast(BF16)
            transposes([(ptb[0:Q, kc * 128:(kc + 1) * 128], XC[:, kc, c0:c1], ident_b[:, :]) for kc in range(8)],
                       ["XC", "ident_b"], pn)
            pxv = ptb[0:Q, 0:1024].rearrange("p (h d) -> p h d", h=16)
            tt("dve", xdt[0:Q, 0, :].rearrange("p (h d) -> p h d", h=16), pxv,
               dtt[0:Q, b, :].unsqueeze(2).to_broadcast([Q, 16, 64]), ALU.mult, pn + ["dtt"], ["xdt"])
            cp("act", xD[0:Q, 0, :], ptb[0:Q, 0:1024], pn, ["xD"])
            tt("pool", xdtd[0:Q, 0, :].rearrange("p (h d) -> p h d", h=16),
               xdt[0:Q, 0, :].rearrange("p (h d) -> p h d", h=16),
               edec.unsqueeze(2).to_broadcast([Q, 16, 64]), ALU.mult, ["xdt", f"eall{r2}"], ["xdtd"])
            pt, pn = ps1()
            ptb = pt.bitcast(BF16)
            transposes([(ptb[0:Q, g * 128:(g + 1) * 128], XC[:, 8 + g, c0:c1], ident_b[:, :]) for g in range(2)],
                       ["XC", "ident_b"], pn)
            cp("act", Btm[0:Q, r2, :], ptb[0:Q, 0:256], pn, [f"Btm{r2}"])
            pt, pn = ps1()
            for g in range(2):
                mmgroup(pt[0:Q, g * 128:g * 128 + Q], [(XC[:, 8 + g, c0:c1], XC[:, 10 + g, c0:c1])], ["XC"], pn)
            tt("dve", CBm[0:Q, r2, :, 0:Q], pt[0:Q, 0:256].rearrange("p (g i) -> p g i", g=2)[:, :, 0:Q],
               mask_le[0:Q, 0:Q].unsqueeze(1).to_broadcast([Q, 2, Q]), ALU.mult, pn + ["mask_le"], [f"CBm{r2}"])


        def ssd_seg(b):
            for g in range(2):
                segv = segb[0:Q, g * 1024:g * 1024 + 8 * Q].rearrange("p (h i) -> p h i", h=8)
                tt("pool", segv, mask_le[0:Q, 0:Q].unsqueeze(1).to_broadcast([Q, 8, Q]),
                   a_t[0:Q, b, g * 8:(g + 1) * 8].unsqueeze(2).to_broadcast([Q, 8, Q]), ALU.mult,
                   ["mask_le", "a_t"], [f"segb{g}"])

        def ssd_s2(b):
            r2 = b % 2
            hpm = min(max(1, 512 // Q), 8)
            Mtv = Mt[0:Q, 0, 0:16 * Q].rearrange("p (h i) -> p h i", h=16)
            for g in range(2):
                for h0 in range(0, 8, hpm):
                    pt, pn = ps1()
                    mmgroup(pt[0:Q, 0:hpm * Q],
                            [(mask_gt_b[0:Q, 0:Q], segb[0:Q, g * 1024 + h0 * Q:g * 1024 + (h0 + hpm) * Q])],
                            ["mask_gt_b", f"segb{g}"], pn)
                    act(Mtv[:, g * 8 + h0:g * 8 + h0 + hpm, :],
                        pt[0:Q, 0:hpm * Q].rearrange("p (h i) -> p h i", h=hpm), AF.Exp, pn, [f"Mtg{g}"])
                tt("dve", Mtv[:, g * 8:(g + 1) * 8, :], Mtv[:, g * 8:(g + 1) * 8, :],
                   CBm[0:Q, r2, g, 0:Q].unsqueeze(1).to_broadcast([Q, 8, Q]), ALU.mult,
                   [f"Mtg{g}", f"CBm{r2}"], [f"Mtg{g}"])

        def ssd_s3(b):
            r2 = b % 2
            c0, c1 = b * Q, (b + 1) * Q
            first_blk = isp and t["first"] and b == 0
            eacs = eall[0:Q, r2, 0:16]
            cd = eall[:, r2, 32:48]
            Mtv = Mt[0:Q, 0, 0:16 * Q].rearrange("p (h i) -> p h i", h=16)
            if not isp:
                dma(hT[:, :], sstate[b], (), ["hT"], stsem)
                cp("act", hTb[:, :], hT[:, :], ["hT"], ["hTb"])
            for g in range(2):
                py, pyn = ps1()
                def fn_yd(e, py=py, Mtv=Mtv, Q=Q, g=g):
                    ins = None
                    for hh in range(8):
                        h = g * 8 + hh
                        e.matmul(py[0:Q, hh * 64:(hh + 1) * 64], lhsT=Mtv[:, h, :], rhs=xdt[0:Q, 0, h * 64:(h + 1) * 64],
                                 start=True, stop=False)
                        ins = e.matmul(py[0:Q, hh * 64:(hh + 1) * 64], lhsT=Did[0:Q, h, 0:Q],
                                       rhs=xD[0:Q, 0, h * 64:(h + 1) * 64], start=False, stop=True)
                    return ins
                P.add("pe", fn_yd, [f"Mtg{g}", "xdt", "xD", "Did"], pyn)
                ysl_ = ytm[0:Q, 0, g * 512:(g + 1) * 512]
                zsl_ = zs[0:Q, b, g * 512:(g + 1) * 512]
                if not first_blk:
                    po, pon = ps1()
                    mmgroup(po[0:Q, :], [(XC[:, 10 + g, c0:c1], hTb[:, g * 512:(g + 1) * 512])], ["XC", "hTb"], pon)
                    tt("dve", ysl_.rearrange("p (h d) -> p h d", h=8), po[0:Q, :].rearrange("p (h d) -> p h d", h=8),
                       eacs[:, g * 8:(g + 1) * 8].unsqueeze(2).to_broadcast([Q, 8, 64]), ALU.mult,
                       pon + [f"eall{r2}"], [f"ytm{g}"])
                    tt("dve", ysl_, ysl_, py[0:Q, :], ALU.add, pyn + [f"ytm{g}"], [f"ytm{g}"])
                    tt("dve", ysl_, ysl_, zsl_, ALU.mult, [f"ytm{g}", f"zs{b}"], [f"ytm{g}"])
                else:
                    tt("dve", ysl_, py[0:Q, :], zsl_, ALU.mult, pyn + [f"zs{b}"], [f"ytm{g}"])
            pst, pstn = ps2()
            def fn_st(e, pst=pst, r2=r2, Q=Q):
                ins = None
                for g in range(2):
                    ins = e.matmul(pst[:, g * 512:(g + 1) * 512], lhsT=Btm[0:Q, r2, g * 128:(g + 1) * 128],
                                   rhs=xdtd[0:Q, 0, g * 512:(g + 1) * 512], start=True, stop=True)
                return ins
            P.add("pe", fn_st, [f"Btm{r2}", "xdtd"], pstn)
            if first_blk:
                cp("act", hT[:, :], pst[:, :], pstn, ["hT"])
            else:
                tt("pool", hT[:, :].rearrange("p (h d) -> p h d", h=16), hT[:, :].rearrange("p (h d) -> p h d", h=16),
                   cd.unsqueeze(2).to_broadcast([128, 16, 64]), ALU.mult, ["hT", f"eall{r2}"], ["hT"])
                tt("dve", hT[:, :], hT[:, :], pst[:, :], ALU.add, pstn + ["hT"], ["hT"])
            last_blk_of_seq = (not isp) or (t["last"] and b == NB - 1)
            if not last_blk_of_seq:
                cp("act", hTb[:, :], hT[:, :], ["hT"], ["hTb"])
            else:
                sq = t["seq"] if isp else b
                dst = (nssm_p if isp else nssm_s)[sq].rearrange("(c p) n -> p c n", p=128)
                pt2, pn2 = ps2()
                transposes([(pt2[:, c * 128:(c + 1) * 128], hT[:, c * 128:(c + 1) * 128], ident_f[:, :]) for c in range(8)],
                           ["hT", "ident_f"], pn2)
                cp("act", ostage[:, :], pt2[:, :], pn2, ["ostage"])
                dma(dst, ostage[:, :].rearrange("p (c n) -> p c n", c=8), ["ostage"], [], osem, is_out=True)

        def ssd_s4(b):
            r2 = b % 2
            for g in range(2):
                col = r2 * 2 + g
                ysl_ = ytm[0:Q, 0, g * 512:(g + 1) * 512]
                nsl_ = ynb[0:Q, 0, g * 512:(g + 1) * 512]
                act(nsl_, ysl_, AF.Square, [f"ytm{g}"], [f"ynb{g}", f"ssq{col}"], accum=ssq[0:Q, col:col + 1])
                act(lnt[0:Q, 12 + col:13 + col], ssq[0:Q, col:col + 1], AF.Ln, [f"ssq{col}", "epsb"], [f"lntq{col}"],
                    bias=epsb[0:Q, 0:1], scale=1.0 / 512.0)
                act(ssq[0:Q, col:col + 1], lnt[0:Q, 12 + col:13 + col], AF.Exp, [f"lntq{col}"], [f"ssq{col}"], scale=-0.5)
                act(nsl_, ysl_, AF.Identity, [f"ytm{g}", f"ssq{col}"], [f"ynb{g}"], scale=ssq[0:Q, col:col + 1])

        def ssd_s4b(b):
            c0, c1 = b * Q, (b + 1) * Q
            for g in range(2):
                pt, pn = ps1()
                ptb = pt.bitcast(BF16)
                transposes([(ptb[:, k4 * Q:(k4 + 1) * Q], ynb[0:Q, 0, (g * 4 + k4) * 128:(g * 4 + k4 + 1) * 128],
                             ident_b[0:Q, 0:Q]) for k4 in range(4)], [f"ynb{g}", "ident_b"], pn)
                tt("dve", catT[:, 8 + g * 4:12 + g * 4, c0:c1], ptb[:, 0:4 * Q].rearrange("p (k q) -> p k q", k=4),
                   vecs[:, V_SNW + g * 4:V_SNW + g * 4 + 4].unsqueeze(2).to_broadcast([128, 4, Q]), ALU.mult,
                   pn + ["vecs"], [f"cat{8 + g * 4 + k4}" for k4 in range(4)])

        hoist = True
        ssd_seg(0)
        ssd_s1(0)
        ssd_s2(0)
        if NB > 1 and hoist:
            ssd_seg(1)
        for b in range(NB):
            if t["idx"] == 0:
                cast_ff1(2 * b)
                cast_ff1(2 * b + 1)
            ssd_s3(b)
            if b + 1 < NB:
                ssd_s1(b + 1)
            ssd_s4(b)
            if b + 1 < NB:
                if not hoist:
                    ssd_seg(b + 1)
                ssd_s2(b + 1)
                if b + 2 < NB and hoist:
                    ssd_seg(b + 2)
            if b == NB - 1:
                wout_first_half()
            ssd_s4b(b)
        if t['stop'] <= 5:
            return
        if t["last"]:
            for sg in range(S):
                sq = t["seq"] if isp else sg
                pt2, pn2 = ps2()
                transposes([(pt2[0:15, kc * 128:(kc + 1) * 128], ulast[:, kc, sg, :], ident_f[:, :]) for kc in range(8)],
                           ["ulast", "ident_f"], pn2)
                cp("act", ostage[0:15, :], pt2[0:15, :], pn2, ["ostage"])
                dma((npool_p if isp else npool_s)[sq], ostage[0:15, :], ["ostage"], [], osem, is_out=True)
                for half in range(2):
                    pt2, pn2 = ps2()
                    transposes([(pt2[0:3, kc * 128:(kc + 1) * 128], xlast[:, half * 6 + kc, sg, :], ident_f[:, :])
                                for kc in range(6)], ["xlast", "ident_f"], pn2)
                    cp("act", ostage[0:3, 0:768], pt2[0:3, 0:768], pn2, ["ostage"])
                    dma((nconv_p if isp else nconv_s)[sq][:, half * 768:(half + 1) * 768], ostage[0:3, 0:768],
                        ["ostage"], [], osem, is_out=True)
        if t['stop'] <= 55:
            return
        catn = [f"cat{m}" for m in range(16)]
        for m in range(8):
            if t["idx"] == 0:
                cast_ff2(m)
            if m < NEARLY:
                slot, sn, pt, pn = wo_early[m]
                wv = slot[:, 0:2048].rearrange("p (k n) -> p k n", k=16)
                pairs = [(wv[:, kc, :], catT[:, kc, 0:T]) for kc in range(8, 16)]
                def fn2(e, pairs=pairs, pt=pt, T=T):
                    ins = None
                    for i, (l, r) in enumerate(pairs):
                        ins = e.matmul(pt[:, 0:T], lhsT=l, rhs=r, start=False, stop=(i == len(pairs) - 1))
                    return ins
                P.add("pe", fn2, [sn] + catn[8:], pn)
                ps_held.discard(int(pn[0][2:]))
            else:
                slot, sn = slab_get("wout")
                wv = slot[:, 0:2048].rearrange("p (k n) -> p k n", k=16)
                pt, pn = ps1()
                mmgroup(pt[:, 0:T], [(wv[:, kc, :], catT[:, kc, 0:T]) for kc in range(16)], [sn] + catn, pn)
            slab_done()
            if isp:
                si = t["seq"]
                act(fT[:, m, 0:T], pt[:, 0:T], AF.Identity, pn + ["mod"], ["fT"], scale=modv(2, m, si))
            else:
                for sg in range(S):
                    act(fT[:, m, sg * L:(sg + 1) * L], pt[:, sg * L:(sg + 1) * L], AF.Identity, pn + ["mod"], ["fT"],
                        scale=modv(2, m, 2 + sg))
        for b in range(NB):
            pt2, pn2 = ps2()
            transposes([(pt2[0:Q, m * 128:(m + 1) * 128], fT[:, m, b * Q:(b + 1) * Q], ident_f[:, :]) for m in range(8)],
                       ["fT", "ident_f"], pn2)
            tt("dve", xres[0:Q, xs_[b], :], xres[0:Q, xs_[b], :], pt2[0:Q, :], ALU.add, pn2 + [f"x{xs_[b]}"], [f"x{xs_[b]}"])
        if t['stop'] <= 6:
            return
        norm_all(t, (wm2, "wm2"), 3)
        if t['stop'] <= 7:
            return
        for sl in range(8):
            slot, sn = slab_get("wff1")
            wv = slot[:, :].rearrange("p (k n) -> p k n", k=8)
            for j in range(4):
                m = sl * 4 + j
                pt, pn = ps1()
                mmgroup(pt[:, 0:T], [(wv[:, kc, j * 128:(j + 1) * 128], hTt[:, kc, 0:T]) for kc in range(8)],
                        [sn] + hTn, pn)
                act(aT[:, m, 0:T], pt[:, 0:T], AF.Relu, pn + ["vecs"], [f"aT{m}"], bias=vecs[:, V_B1 + m:V_B1 + m + 1])
                tt("pool" if m % 2 == 0 else "dve", aT[:, m, 0:T], aT[:, m, 0:T], aT[:, m, 0:T], ALU.mult,
                   [f"aT{m}"], [f"aT{m}"])
            slab_done()
        if t['stop'] <= 8:
            return
        nxt = t["next"]

        def st_n(b):
            if nxt is not None:
                norm_stats(nxt, b, (wm1, "wm1"), 0)

        def tr_n(b):
            if nxt is not None:
                norm_trev(nxt, b, (wm1, "wm1"), 0)

        if nxt is not None:
            for b_ in range(3):
                emit_xload(nxt, b_)
            Qn = nxt["Q"]
            if nxt["kind"] == "p":
                src3 = xp[nxt["seq"], nxt["ti"] * 512 + 384:nxt["ti"] * 512 + 512, :]
            else:
                src3 = xs[48:64, :]
            dma(ostg[0:Qn, :], src3, (), ["ostage"], x3sem)
            st_n(0)
            st_n(1)
        for m in range(8):
            if m == 2:
                tr_n(0)
                st_n(2)
            if m == 4:
                tr_n(1)
                if nxt is not None:
                    norm_stats(nxt, 3, (wm1, "wm1"), 0, alt=(ostg[0:nxt["Q"], :], "ostage"))
            if m == 6:
                tr_n(2)
            if m == 7:
                tr_n(3)
            slot, sn = slab_get("wff2")
            wv = slot[:, :].rearrange("p (k n) -> p k n", k=32)
            pt, pn = ps1()
            mmgroup(pt[:, 0:T], [(wv[:, kc, :], aT[:, kc, 0:T]) for kc in range(32)], [sn] + aTn, pn)
            slab_done()
            if isp:
                si = t["seq"]
                ts("dve", fT[:, m, 0:T], pt[:, 0:T], modv(5, m, si), g2b2[:, m, si:si + 1], ALU.mult, ALU.add,
                   pn + ["mod", "g2b2"], ["fT"])
            else:
                for sg in range(S):
                    si = 2 + sg
                    ts("dve", fT[:, m, sg * L:(sg + 1) * L], pt[:, sg * L:(sg + 1) * L], modv(5, m, si),
                       g2b2[:, m, si:si + 1], ALU.mult, ALU.add, pn + ["mod", "g2b2"], ["fT"])

        def tail_blk(b):
            pt2, pn2 = ps2()
            transposes([(pt2[0:Q, m * 128:(m + 1) * 128], fT[:, m, b * Q:(b + 1) * Q], ident_f[:, :]) for m in range(8)],
                       ["fT", "ident_f"], pn2)
            tt("dve", xres[0:Q, xs_[b], :], xres[0:Q, xs_[b], :], pt2[0:Q, :], ALU.add, pn2 + [f"x{xs_[b]}"], [f"x{xs_[b]}"])
            act(Mt[0:Q, 0, 0:1024], xres[0:Q, xs_[b], :], AF.Square, [f"x{xs_[b]}"],
                ["Mtg0", "Mtg1", f"ss{8 + b}"], accum=ss[0:Q, 8 + b:9 + b])
            rstd_from_ss(8 + b, Q, 1024.0)
            ysl = 0
            stt("dve", yst[0:Q, ysl, :], xres[0:Q, xs_[b], :], ss[0:Q, 8 + b:9 + b], fnwb[0:Q, :], ALU.mult, ALU.mult,
                [f"x{xs_[b]}", f"ss{8 + b}", "fnwb"], ["yst"])
            if isp:
                dst = y_p[t["seq"], t["ti"] * 512 + b * 128:t["ti"] * 512 + (b + 1) * 128, :]
            else:
                dst = y_s[b * 16:(b + 1) * 16, :]
            dma(dst, yst[0:Q, ysl, :], ["yst"], [], ysem[ysl], is_out=True)
            if nxt is not None and b == 0:
                emit_xload(nxt, 3)

        tail_blk(0)
        tail_blk(1)
        tail_blk(2)
        tail_blk(3)

    for ti_, t in enumerate(tiles):
        t["idx"] = ti_
        t["xs"] = [(4 * ti_ + b) % 7 for b in range(4)]
        t["next"] = tiles[ti_ + 1] if ti_ + 1 < len(tiles) else None
    if cfg.stop > 0:
        for ti_, t in enumerate(tiles):
            t["stop"] = cfg.stop if ti_ >= cfg.stop_tile else 99
            emit_tile(t)
            if t["stop"] < 99:
                break
    if cfg.stop >= 99:
        assert slab_state["cur"] == len(slabs)
    dumpable = dict(mod=(mod, ["mod"]), wm1=(wm1, ["wm1"]), hTt=(hTt, ["hTt"]), xn=(xn, ["xn0", "xn1", "xn2", "xn3"]),
                    ss=(ss, ["ss"]), G1=(G1, ["U", "XBC", "PSCR0", "PSCR1", "pooled0", "pooled1"]), G2=(G2, ["XC", "zs0", "zs1", "zs2", "zs3"]),
                    catT=(catT, [f"cat{m}" for m in range(16)]), dtt=(dtt, ["dtt"]), a_t=(a_t, ["a_t"]),
                    xres=(xres, ["x0", "x1", "x2", "x3"]), hT=(hT, ["hT"]), ytm=(ytm, ["ytm"]), Mt=(Mt, ["Mt0g0", "Mt0g1", "Mt1g0", "Mt1g1"]),
                    eall=(eall, ["eall0", "eall1"]), diag=(diag, ["diag"]), mask_le=(mask_le, ["mask_le"]),
                    mask_gt=(mask_gt, ["mask_gt"]), cT=(cT, ["cT"]), Ab=(Ab, ["Ab"]), xdt=(xdt, ["xdt0", "xdt1"]),
                    CBm=(CBm, ["CBm0", "CBm1"]), ynb=(ynb, ["ynb0", "ynb1"]), xD=(xD, ["xD"]), Btm=(Btm, ["Btm0", "Btm1"]))
    for nm in cfg.dump:
        tns, pcs = dumpable[nm]
        shp = list(tns.shape)
        dd = nc.dram_tensor("dbg_" + nm, shp, tns.dtype, kind="ExternalOutput").ap()
        full = tuple(slice(None) for _ in shp)
        dma(dd[full], tns[full], pcs, [], sem("dbgs_" + nm), is_out=True)

    P.finalize(nc, stack)
    with nc.Block() as block:
        @block.tensor
        def _(e):
            P.emit_stream("pe", e)

        @block.scalar
        def _(e):
            P.emit_stream("act", e)

        @block.vector
        def _(e):
            P.emit_stream("dve", e)

        @block.gpsimd
        def _(e):
            P.emit_stream("pool", e)

        @block.sync
        def _(e):
            P.emit_stream("sp", e)
    stack.close()
    return nc


def _fm(v, nk):
    return np.ascontiguousarray(np.asarray(v, np.float32).reshape(nk, 128).T)


def prep_shared(inp):
    f = lambda a: np.ascontiguousarray(np.asarray(a, np.float32))
    w_ada = f(inp["w_ada"][0])
    w_in = f(inp["w_in"][0])
    sh = {}
    sh["w_ada_r"] = np.ascontiguousarray(w_ada.reshape(8, 128, 12, 512).transpose(2, 1, 0, 3)).reshape(12, 128, 4096)
    sh["w_in_r"] = np.ascontiguousarray(w_in[:, :3584].reshape(8, 128, 7, 512).transpose(2, 1, 0, 3)).reshape(7, 128, 4096)
    sh["w_dt_r"] = np.ascontiguousarray(w_in[:, 3584:].reshape(8, 128, 16).transpose(1, 0, 2)).reshape(128, 128)
    pw = f(inp["pool_w"][0])
    sh["pool_w_r"] = np.ascontiguousarray(pw.reshape(4, 2, 128, 256).transpose(2, 0, 1, 3)).reshape(128, 2048)
    wo = f(inp["w_out"][0])
    sh["w_out_r"] = np.ascontiguousarray(wo.reshape(16, 128, 8, 128).transpose(2, 1, 0, 3)).reshape(8, 128, 2048)
    w1 = f(inp["w_ff1"][0])
    sh["w_ff1_r"] = np.ascontiguousarray(w1.reshape(8, 128, 8, 512).transpose(2, 1, 0, 3)).reshape(8, 128, 4096)
    w2 = f(inp["w_ff2"][0])
    sh["w_ff2_r"] = np.ascontiguousarray(w2.reshape(32, 128, 8, 128).transpose(2, 1, 0, 3)).reshape(8, 128, 4096)
    vecs = np.zeros((128, NV), np.float32)
    vecs[:, V_NMW:V_NMW + 8] = _fm(inp["norm_mix_w"][0], 8)
    vecs[:, V_NFW:V_NFW + 8] = _fm(inp["norm_ffn_w"][0], 8)
    vecs[:, V_PB:V_PB + 8] = _fm(np.asarray(inp["pool_b"][0]).reshape(-1), 8)
    vecs[:, V_PS:V_PS + 8] = _fm(inp["pool_scale"][0], 8)
    cw = np.asarray(inp["conv_w"][0], np.float32)
    for k in range(4):
        vecs[:, V_CW + k * 12:V_CW + (k + 1) * 12] = _fm(cw[k], 12)
    vecs[:, V_CB:V_CB + 12] = _fm(inp["conv_b"][0], 12)
    vecs[:, V_B1:V_B1 + 32] = _fm(inp["b_ff1"][0], 32)
    vecs[:, V_B2:V_B2 + 8] = _fm(inp["b_ff2"][0], 8)
    vecs[:, V_BADA:V_BADA + 48] = _fm(inp["b_ada"][0], 48)
    vecs[:, V_SNW:V_SNW + 8] = _fm(inp["ssd_norm_w"][0], 8)
    sh["vecs"] = vecs
    rows = np.zeros((1, NR), np.float32)
    rows[0, R_FNW:R_FNW + 1024] = np.asarray(inp["final_norm_w"], np.float32)
    rows[0, R_SNW:R_SNW + 1024] = np.asarray(inp["ssd_norm_w"][0], np.float32)
    rows[0, R_DSK:R_DSK + 16] = np.asarray(inp["d_skip"][0], np.float32)
    rows[0, R_DTB:R_DTB + 16] = np.asarray(inp["dt_bias"][0], np.float32)
    rows[0, R_ALOG:R_ALOG + 16] = np.asarray(inp["a_log"][0], np.float32)
    sh["rows"] = rows
    return sh


def prep_core(inp, c):
    f = lambda a: np.ascontiguousarray(np.asarray(a, np.float32))
    m = {}
    m["xp"] = f(inp["x_prompt"][2 * c:2 * c + 2])
    m["xs"] = f(inp["x_sample"][4 * c:4 * c + 4]).reshape(64, 1024)
    cp_ = f(inp["cache_pool"][0, 4 * c:4 * c + 4])
    m["cpool"] = np.ascontiguousarray(cp_.reshape(4, 15, 8, 128).transpose(3, 2, 0, 1))
    cc = f(inp["cache_conv"][0, 4 * c:4 * c + 4])
    m["cconv"] = np.ascontiguousarray(cc.reshape(4, 3, 12, 128).transpose(3, 2, 0, 1))
    st = f(inp["state_ssm"][0, 4 * c:4 * c + 4])
    m["sstate"] = np.ascontiguousarray(st.reshape(4, 1024, 128).transpose(0, 2, 1))
    cvec = np.concatenate([f(inp["c_prompt"][2 * c:2 * c + 2]), f(inp["c_sample"][4 * c:4 * c + 4])], 0)
    m["cT"] = np.ascontiguousarray(cvec.reshape(6, 8, 128).transpose(2, 1, 0))
    return m


_NC_CACHE = {}


def kernel(**inputs):
    cfg = Cfg()
    if "nc" not in _NC_CACHE:
        _NC_CACHE["nc"] = build_program(cfg)
    nc = _NC_CACHE["nc"]
    sh = prep_shared(inputs)
    in_maps = []
    for c in range(NCORES):
        m = dict(sh)
        m.update(prep_core(inputs, c))
        in_maps.append(m)
    res = run_bass_kernel_spmd(nc, in_maps, core_ids=list(range(NCORES)))
    rs = res.results
    cat = lambda k: np.concatenate([np.asarray(r[k], np.float32) for r in rs], 0)
    y_prompt = cat("y_p")
    y_sample = cat("y_s").reshape(32, 16, 1024)
    npool_p = cat("npool_p")[None]
    nconv_p = cat("nconv_p")[None]
    nssm_p = cat("nssm_p").reshape(16, 16, 64, 128)[None]
    npool_s = cat("npool_s")[None]
    nconv_s = cat("nconv_s")[None]
    nssm_s = cat("nssm_s").reshape(32, 16, 64, 128)[None]
    return (y_prompt, y_sample, npool_p, nconv_p, nssm_p, npool_s, nconv_s, nssm_s)
```
